# Optimizing a Trainium2 kernel written in Bass

```python
import math
import jax, jax.numpy as jnp
from jax import lax
import numpy as np

D_MODEL = 1024
BATCH = 16
SEQ = 2048
DEPTH = 1

GLA_HEADS = 4
GLA_DK = D_MODEL // 2
GLA_DV = D_MODEL
GLA_HK = GLA_DK // GLA_HEADS
GLA_HV = GLA_DV // GLA_HEADS
GLA_RANK = 16
GLA_TAU = 16.0
GLA_CHUNK = 64
LOG_DECAY_FLOOR = -1.25

GMLP_WIDTH = D_MODEL
GMLP_GROUPS = 8
GMLP_GC = GMLP_WIDTH // GMLP_GROUPS
GMLP_CHUNK = 128

EPS = 1e-6

IN_SPLITS = (GLA_DK, GLA_DK, GLA_DV, GLA_DV, GLA_RANK, GLA_RANK,
             GMLP_WIDTH, GMLP_WIDTH, GMLP_WIDTH, D_MODEL, D_MODEL)
IN_WIDTH = sum(IN_SPLITS)

kernel_name = 'hybrid_gla_gmlp_adaln_block'


def _split_cols(p, sizes):
    outs = []
    off = 0
    for s in sizes:
        outs.append(p[..., off:off + s])
        off += s
    return outs


def _rmsnorm(x, g):
    xf = x.astype(jnp.float32)
    r = lax.rsqrt(jnp.mean(xf * xf, axis=-1, keepdims=True) + EPS)
    return (xf * r).astype(x.dtype) * g


def _layernorm(x, g, b):
    xf = x.astype(jnp.float32)
    mu = jnp.mean(xf, axis=-1, keepdims=True)
    var = jnp.mean(jnp.square(xf - mu), axis=-1, keepdims=True)
    return ((xf - mu) * lax.rsqrt(var + EPS)).astype(x.dtype) * g + b


def _gla_chunked(q, k, v, log_a, strict):
    B, H, S, dk = q.shape
    dv = v.shape[-1]
    C = GLA_CHUNK
    N = S // C
    q = q.reshape(B, H, N, C, dk)
    k = k.reshape(B, H, N, C, dk)
    v = v.reshape(B, H, N, C, dv)
    b = jnp.cumsum(log_a.reshape(B, H, N, C, dk), axis=3)
    b_last = b[:, :, :, -1:, :]
    q_t = q * jnp.exp(b)
    k_t = k * jnp.exp(-b)
    k_end = k * jnp.exp(b_last - b)
    mask = jnp.tril(jnp.ones((C, C), dtype=bool), k=-1 if strict else 0)
    attn = jnp.where(mask, jnp.einsum('bhntd,bhnsd->bhnts', q_t, k_t), 0.0)
    o_intra = jnp.einsum('bhnts,bhnsv->bhntv', attn, v)
    chunk_kv = jnp.einsum('bhnsd,bhnsv->bhndv', k_end, v)
    chunk_decay = jnp.exp(b_last[:, :, :, 0, :])

    def step(state, inp):
        d, kv = inp
        return d[..., None] * state + kv, state

    init = jnp.zeros((B, H, dk, dv), dtype=q.dtype)
    _, states = lax.scan(step, init, (jnp.moveaxis(chunk_decay, 2, 0), jnp.moveaxis(chunk_kv, 2, 0)))
    states = jnp.moveaxis(states, 0, 2)
    o_inter = jnp.einsum('bhntd,bhndv->bhntv', q_t, states)
    return (o_intra + o_inter).reshape(B, H, S, dv)


def _layer(x, c, norm_g, w_ada, b_ada, w_in, alpha_fw_w, alpha_fw_b, alpha_bw_w, alpha_bw_b,
           gla_norm_g, gmlp_ln_g, gmlp_ln_b, gmlp_ws, gmlp_bs, w_br_gla, w_br_gmlp, w_out):
    B, S, _ = x.shape
    mod = jax.nn.silu(c) @ w_ada + b_ada
    shift, scale, gate = jnp.split(mod, 3, axis=-1)
    h = _rmsnorm(x, norm_g) * (1.0 + scale[:, None, :]) + shift[:, None, :]

    p = h @ w_in
    (q, k, v, z_gla, ra_f, ra_b, u, vs, z_gmlp, m_gla, m_gmlp) = _split_cols(p, IN_SPLITS)

    def heads(t, hd):
        return t.reshape(B, S, GLA_HEADS, hd).transpose(0, 2, 1, 3).astype(jnp.float32)

    log_a_f = jnp.maximum(jax.nn.log_sigmoid(ra_f @ alpha_fw_w + alpha_fw_b) / GLA_TAU, LOG_DECAY_FLOOR)
    log_a_b = jnp.maximum(jax.nn.log_sigmoid(ra_b @ alpha_bw_w + alpha_bw_b) / GLA_TAU, LOG_DECAY_FLOOR)
    qh = heads(q, GLA_HK) * (GLA_HK ** -0.5)
    kh = heads(k, GLA_HK)
    vh = heads(v, GLA_HV)
    laf = heads(log_a_f, GLA_HK)
    lab = heads(log_a_b, GLA_HK)
    flip = lambda t: jnp.flip(t, axis=2)
    o_fw = _gla_chunked(qh, kh, vh, laf, strict=False)
    o_bw = flip(_gla_chunked(flip(qh), flip(kh), flip(vh), flip(lab), strict=True))
    o = o_fw + o_bw
    o = o * lax.rsqrt(jnp.mean(o * o, axis=-1, keepdims=True) + EPS) * gla_norm_g[None, :, None, :]
    o = o.transpose(0, 2, 1, 3).reshape(B, S, GLA_DV).astype(x.dtype)
    y_gla = (o * jax.nn.silu(z_gla)) @ w_br_gla

    u = jax.nn.gelu(u, approximate=False)
    vs = _layernorm(jax.nn.gelu(vs, approximate=False), gmlp_ln_g, gmlp_ln_b)
    vs = vs.reshape(B, S // GMLP_CHUNK, GMLP_CHUNK, GMLP_GROUPS, GMLP_GC)
    sg = jnp.einsum('gts,bnsgc->bntgc', gmlp_ws, vs) + gmlp_bs.T[None, None, :, :, None]
    sg = sg.reshape(B, S, GMLP_WIDTH)
    y_gmlp = (u * sg * jax.nn.silu(z_gmlp)) @ w_br_gmlp

    merged = jax.nn.sigmoid(m_gla) * y_gla + jax.nn.sigmoid(m_gmlp) * y_gmlp
    return x + gate[:, None, :] * (merged @ w_out)


def setup_inputs(seed: int = 0) -> dict:
    key = jax.random.key(seed)
    ks = jax.random.split(key, 20)
    D = D_MODEL
    nrm = lambda k, shape, s: jax.random.normal(k, shape, dtype=jnp.float32) * s
    return {
        'x': nrm(ks[0], (BATCH, SEQ, D), 1.0),
        'c': nrm(ks[1], (BATCH, D), 1.0),
        'norm_g': 1.0 + nrm(ks[2], (DEPTH, D), 0.02),
        'w_ada': nrm(ks[3], (DEPTH, D, 3 * D), 0.5 * D ** -0.5),
        'b_ada': nrm(ks[4], (DEPTH, 3 * D), 0.02),
        'w_in': nrm(ks[5], (DEPTH, D, IN_WIDTH), D ** -0.5),
        'alpha_fw_w': nrm(ks[6], (DEPTH, GLA_RANK, GLA_DK), GLA_RANK ** -0.5),
        'alpha_fw_b': nrm(ks[7], (DEPTH, GLA_DK), 0.1),
        'alpha_bw_w': nrm(ks[8], (DEPTH, GLA_RANK, GLA_DK), GLA_RANK ** -0.5),
        'alpha_bw_b': nrm(ks[9], (DEPTH, GLA_DK), 0.1),
        'gla_norm_g': 1.0 + nrm(ks[10], (DEPTH, GLA_HEADS, GLA_HV), 0.02),
        'gmlp_ln_g': 1.0 + nrm(ks[11], (DEPTH, GMLP_WIDTH), 0.02),
        'gmlp_ln_b': nrm(ks[12], (DEPTH, GMLP_WIDTH), 0.02),
        'gmlp_ws': nrm(ks[13], (DEPTH, GMLP_GROUPS, GMLP_CHUNK, GMLP_CHUNK), GMLP_CHUNK ** -0.5),
        'gmlp_bs': 1.0 + nrm(ks[14], (DEPTH, GMLP_GROUPS, GMLP_CHUNK), 0.02),
        'w_br_gla': nrm(ks[15], (DEPTH, GLA_DV, D), GLA_DV ** -0.5),
        'w_br_gmlp': nrm(ks[16], (DEPTH, GMLP_WIDTH, D), GMLP_WIDTH ** -0.5),
        'w_out': nrm(ks[17], (DEPTH, D, D), D ** -0.5),
        'final_g': 1.0 + nrm(ks[18], (D,), 0.02),
    }


def reference(x, c, norm_g, w_ada, b_ada, w_in, alpha_fw_w, alpha_fw_b, alpha_bw_w, alpha_bw_b,
              gla_norm_g, gmlp_ln_g, gmlp_ln_b, gmlp_ws, gmlp_bs, w_br_gla, w_br_gmlp, w_out, final_g):
    h = x
    for l in range(DEPTH):
        h = _layer(h, c, norm_g[l], w_ada[l], b_ada[l], w_in[l],
                   alpha_fw_w[l], alpha_fw_b[l], alpha_bw_w[l], alpha_bw_b[l],
                   gla_norm_g[l], gmlp_ln_g[l], gmlp_ln_b[l], gmlp_ws[l], gmlp_bs[l],
                   w_br_gla[l], w_br_gmlp[l], w_out[l])
    return _rmsnorm(h, final_g)
```

```python
from contextlib import ExitStack
import math
import numpy as np
import concourse.bass as bass
import concourse.mybir as mybir
from concourse.bass_utils import run_bass_kernel_spmd

F32 = mybir.dt.float32
BF16 = mybir.dt.bfloat16
AF = mybir.ActivationFunctionType
ALU = mybir.AluOpType

NCORES = 8
D = 1024
SEQ = 2048
NSEQ = 2
NT = SEQ // 128
NCH = SEQ // 64
EPS = 1e-6
IN_W = 8224
C_Q, C_K, C_V, C_ZG, C_RAF, C_RAB, C_U, C_VS, C_ZM, C_MGLA, C_MGMLP = (
    0, 512, 1024, 2048, 3072, 3088, 3104, 4128, 5152, 6176, 7200)


class _Dummy:
    def then_inc(self, *a, **k):
        return self


class _Rec:
    def __init__(self, eng):
        self.eng = eng
        self.ns = 0.0
        self.tset = None

    @staticmethod
    def _n(ap):
        n = 1
        for d in list(ap.shape)[1:]:
            n *= int(d)
        return n

    def __getattr__(self, name):
        def f(*a, **kw):
            eng = self.eng
            if name == "matmul":
                n = self._n(kw["rhs"])
                k = 4.0 if kw["lhsT"].dtype == F32 else 1.0
                self.ns += k * max(64, n) / 2.05 + 6
            elif name == "transpose":
                self.ns += 64 / 2.05 + 20
            elif name == "activation":
                fn_ = kw.get("func")
                if fn_ in (AF.Exp, AF.Ln):
                    self.tset = "explog"
                elif fn_ in (AF.Silu, AF.Sigmoid, AF.Gelu):
                    self.tset = str(fn_)
                self.ns += (self._n(kw["in_"]) + 190) / 1.2 + (90 if not isinstance(kw.get("scale", 1.0), float) else 0) \
                    + (90 if kw.get("accum_out") is not None else 0)
            elif name == "dma_start":
                self.ns += 60
            elif eng == "pool":
                ap = kw.get("out", a[0] if a else None)
                self.ns += self._n(ap) * (3.8 if name == "tensor_copy" else 2.3) + 100
            else:
                ap = kw.get("out", a[0] if a else None)
                self.ns += (self._n(ap) + 70) / 0.96
            return _Dummy()
        return f


class _Op:
    __slots__ = ("i", "eng", "fn", "deps", "dur", "busy", "sem", "inc", "ticket", "tset")


class Sched:
    ENGS = ("pe", "act", "dve", "pool", "sp")
    EDGE_NS = 350.0
    SLACK_NS = 120.0

    def __init__(self, nc, stack):
        self.nc = nc
        self.stack = stack
        self.sem = {}
        self.total = {}
        self.isdma = {}
        self.waited = {e: {} for e in self.ENGS}
        for e in self.ENGS:
            self._newsem("E_" + e, False)
        self.ops = []
        self.lastw = {}
        self.readers = {}
        self.lastdma = {}
        self.nblocks = 0

    def _newsem(self, name, isdma):
        self.sem[name] = self.stack.enter_context(self.nc.semaphore(name))
        self.total[name] = 0
        self.isdma[name] = isdma

    def _add(self, eng, fn, reads, writes, sem, inc, dur, busy, tset=None):
        o = _Op()
        o.i = len(self.ops)
        o.eng, o.fn, o.sem, o.inc, o.dur, o.busy = eng, fn, sem, inc, dur, busy
        o.tset = tset
        deps = set()
        for k in reads:
            if k is None:
                continue
            w = self.lastw.get(k)
            if w is not None:
                deps.add(w.i)
        for k in writes:
            if k is None:
                continue
            w = self.lastw.get(k)
            if w is not None:
                deps.add(w.i)
            for r in self.readers.get(k, ()):
                deps.add(r.i)
        if self.isdma[sem]:
            p = self.lastdma.get(sem)
            if p is not None:
                deps.add(p.i)
            self.lastdma[sem] = o
        deps.discard(o.i)
        o.deps = deps
        self.ops.append(o)
        for k in reads:
            if k is not None:
                self.readers.setdefault(k, []).append(o)
        for k in writes:
            if k is not None:
                self.lastw[k] = o
                self.readers[k] = []
        return o

    def op(self, eng, fn, reads=(), writes=()):
        rec = _Rec(eng)
        fn(rec)
        self._add(eng, fn, reads, writes, "E_" + eng, 1, rec.ns, rec.ns, rec.tset)

    def dma(self, queue, out, in_, semkey, reads=(), writes=()):
        s = "D_" + str(semkey)
        if s not in self.sem:
            self._newsem(s, True)
        nbytes = 1
        for d in out.shape:
            nbytes *= int(d)
        nbytes *= 2 if out.dtype == BF16 else 4
        dur = 4000.0 + nbytes / 250.0
        busy = 1000.0 if queue == "pool" else 80.0 + nbytes / 480.0
        self._add(queue, lambda e, o=out, i=in_: e.dma_start(out=o, in_=i), reads, writes, s, 16, dur, busy)

    def _schedule(self):
        ops = self.ops
        n = len(ops)
        succ = [[] for _ in range(n)]
        npred = [0] * n
        for o in ops:
            for d in o.deps:
                succ[d].append(o.i)
                npred[o.i] += 1
        bl = [0.0] * n
        for o in reversed(ops):
            m = 0.0
            for s_ in succ[o.i]:
                if bl[s_] > m:
                    m = bl[s_]
            bl[o.i] = o.dur + m
        ready = {e: [] for e in self.ENGS}
        rt = [0.0] * n
        free = {e: 0.0 for e in self.ENGS}
        order = {e: [] for e in self.ENGS}
        for o in ops:
            if npred[o.i] == 0:
                ready[o.eng].append(o.i)
        done = 0
        end = 0.0
        cur_set = None
        TL = 1350.0
        while done < n:
            best = None
            for e in self.ENGS:
                rl = ready[e]
                if not rl:
                    continue
                t = free[e]
                if e == "act":
                    pen = [TL if (ops[i].tset is not None and ops[i].tset != cur_set) else 0.0 for i in rl]
                else:
                    pen = [0.0] * len(rl)
                ests = [max(t, rt[i]) + p_ for i, p_ in zip(rl, pen)]
                lim = min(ests) + self.SLACK_NS
                c = None
                cst = 0.0
                for i, st_ in zip(rl, ests):
                    if st_ <= lim and (c is None or bl[i] > bl[c] or (bl[i] == bl[c] and i < c)):
                        c = i
                        cst = st_
                if best is None or cst < best[0]:
                    best = (cst, e, c)
            st, e, c = best
            ready[e].remove(c)
            order[e].append(c)
            o = ops[c]
            if e == "act" and o.tset is not None:
                cur_set = o.tset
            free[e] = st + o.busy
            fin = st + o.dur
            end = max(end, fin)
            done += 1
            for s_ in succ[c]:
                lat = 0.0 if (e == "pe" and ops[s_].eng == "pe") else self.EDGE_NS
                if fin + lat > rt[s_]:
                    rt[s_] = fin + lat
                npred[s_] -= 1
                if npred[s_] == 0:
                    ready[ops[s_].eng].append(s_)
        return order, end

    def flush(self):
        nc = self.nc
        ops = self.ops
        fin = _Op()
        fin.i = len(ops)
        fin.eng, fin.fn, fin.sem, fin.inc, fin.dur, fin.busy, fin.tset = "sp", None, None, 0, 0.0, 0.0, None
        fin.deps = set(o.i for o in ops if self.isdma[o.sem])
        ops.append(fin)
        order, est = self._schedule()
        self.last_estimate_ns = est
        for e in self.ENGS:
            for i in order[e]:
                o = ops[i]
                if o.sem is None:
                    continue
                self.total[o.sem] += o.inc
                o.ticket = (o.sem, self.total[o.sem])
        progs = {}
        for e in self.ENGS:
            prog = []
            for i in order[e]:
                o = ops[i]
                need = {}
                for d in o.deps:
                    s_, v = ops[d].ticket
                    if s_ == "E_pe" and e == "pe":
                        continue
                    if need.get(s_, 0) < v:
                        need[s_] = v
                for s_, v in need.items():
                    if self.waited[e].get(s_, 0) >= v:
                        continue
                    self.waited[e][s_] = v
                    prog.append(("wait", s_, v))
                if o.fn is not None:
                    prog.append(("op", o.fn, o.sem, o.inc))
            progs[e] = prog
        self.nblocks += 1
        with nc.Block("blk%d" % self.nblocks) as block:
            for eng, deco in (("pe", block.tensor), ("act", block.scalar), ("dve", block.vector),
                              ("pool", block.gpsimd), ("sp", block.sync)):
                prog = progs[eng]

                def body(e, prog=prog):
                    for it in prog:
                        if it[0] == "wait":
                            e.wait_ge(self.sem[it[1]], it[2])
                        else:
                            it[1](e).then_inc(self.sem[it[2]], it[3])
                deco(body)
        self.ops = []
        self.lastw = {}
        self.readers = {}
        self.lastdma = {}


class Buf:
    def __init__(self, nc, stack, name, shape, dtype, n=1):
        self.name = name
        self.n = n
        self.t = [stack.enter_context(nc.sbuf_tensor("%s_%d" % (name, i), list(shape), dtype))
                  for i in range(n)]

    def __call__(self, s=0):
        return self.t[s % self.n]

    def k(self, s=0):
        return (self.name, s % self.n)


class PsRing:
    def __init__(self, nc, stack):
        self.pairs = [stack.enter_context(nc.psum_tensor("psp%d" % i, [128, 1024], F32))
                      for i in range(4)]
        self.freeb = list(range(8))

    def alloc(self, nb=1):
        fb = self.freeb
        if nb == 1:
            pick = None
            for b in fb:
                if (b ^ 1) not in fb:
                    pick = b
                    break
            if pick is None:
                if not fb:
                    raise RuntimeError("PSUM exhausted (1 bank)")
                pick = fb[0]
            fb.remove(pick)
            p, h = divmod(pick, 2)
            return self.pairs[p][:, h * 512:(h + 1) * 512], [("ps", pick)]
        for b in fb:
            if b % 2 == 0 and (b + 1) in fb:
                fb.remove(b)
                fb.remove(b + 1)
                return self.pairs[b // 2][:, :], [("ps", b), ("ps", b + 1)]
        raise RuntimeError("PSUM exhausted (2 banks), free=%s" % fb)

    def free(self, keys):
        for k in keys:
            assert k[1] not in self.freeb
            self.freeb.append(k[1])


def interleave(gens):
    gens = list(gens)
    while gens:
        for g in list(gens):
            try:
                next(g)
            except StopIteration:
                gens.remove(g)


def build_program(debug=False):
    nc = bass.Bass("TRN2", target_bir_lowering=False)

    def din(name, shape, dt=F32):
        return nc.dram_tensor(name, list(shape), dt, kind="ExternalInput").ap()

    x = din("x", [NSEQ * SEQ, D])
    cT = din("cT", [128, 8, NSEQ])
    norm_g = din("norm_g", [1, D])
    w_ada = din("w_ada", [D, 3 * D])
    b_ada = din("b_ada", [1, 3 * D])
    w_in = din("w_in", [D, IN_W])
    a_fw_w = din("alpha_fw_w", [16, 512])
    a_fw_b = din("alpha_fw_b", [1, 512])
    a_bw_w = din("alpha_bw_w", [16, 512])
    a_bw_b = din("alpha_bw_b", [1, 512])
    gla_g = din("gla_norm_g", [1, D])
    ln_g = din("gmlp_ln_g", [1, D])
    ln_b = din("gmlp_ln_b", [1, D])
    wsT_d = din("wsT", [128, 8, 128])
    bsT_d = din("bsT", [128, 8])
    w_brg = din("w_br_gla", [D, D])
    w_brm = din("w_br_gmlp", [D, D])
    w_out = din("w_out", [D, D])
    fin_g = din("final_g", [1, D])
    ident_d = din("c_ident", [128, 128])
    cumf_d = din("c_cumf", [128, 128])
    cumb_d = din("c_cumb", [128, 128])
    maskb_d = din("c_maskb", [128, 128])
    ind_d = din("c_ind", [128, 2])
    onerows_d = din("c_onerows", [128, 128])
    sel_d = din("c_sel", [2, 256])

    out = nc.dram_tensor("out", [NSEQ * SEQ, D], F32, kind="ExternalOutput").ap()
    skind = "ExternalOutput" if debug else "Internal"
    hts = nc.dram_tensor("s_hts", [NSEQ * NT, 128, D], BF16, kind=skind).ap()
    sbw = nc.dram_tensor("s_sbw", [NSEQ * NCH, 128, D], BF16, kind=skind).ap()
    ygla = nc.dram_tensor("s_ygla", [NSEQ * SEQ, D], F32, kind=skind).ap()
    vsc = nc.dram_tensor("s_v", [NSEQ * NT, 128, D], BF16, kind=skind).ap()
    wst = nc.dram_tensor("s_wst", [7, 128, 8 * D], BF16, kind="Internal").ap()
    labs = nc.dram_tensor("s_lab", [NSEQ * NT, 128, D], BF16, kind=skind).ap()

    w_in_v = w_in.rearrange("(kc p) n -> p kc n", p=128)

    with ExitStack() as top:
        S = Sched(nc, top)
        R = PsRing(nc, top)

        def B(stack, name, shape, dtype, n=1):
            return Buf(nc, stack, name, shape, dtype, n)

        dbg_n = [0]

        def dump(name, buf, slot, shape, dtype):
            if not debug:
                return
            dt_ = nc.dram_tensor("dbg_" + name, list(shape), dtype, kind="ExternalOutput").ap()
            dbg_n[0] += 1
            S.dma("sp", dt_, buf(slot)[:], "dbg%d" % dbg_n[0], reads=[buf.k(slot)])

        ident = B(top, "ident", [128, 128], BF16)
        GATE = B(top, "GATE", [128, D], F32, NSEQ)
        junk = B(top, "junk", [128, D], BF16)
        S.dma("pool", ident()[:], ident_d[:, :], "ident", writes=[ident.k()])

        def load_w(stack, name, src_cols, queue="pool"):
            w = B(stack, name, [128, 8, src_cols.shape[2]], BF16)
            S.dma(queue, w()[:], src_cols, name, writes=[w.k()])
            return w

        def bcast_row(stack, name, row_ap):
            t = B(stack, name, [128, D], F32)
            S.dma("sp", t()[:], row_ap[0:1, :].broadcast_to([128, D]), name, writes=[t.k()])
            return t

        def act(out_ap, in_ap, func, reads, writes, **kw):
            S.op("act", lambda e: e.activation(out=out_ap, in_=in_ap, func=func, **kw), reads, writes)

        def rstd_from_ss(ss_ap, n, key):
            act(ss_ap, ss_ap, AF.Ln, [key], [key], scale=1.0 / n, bias=EPS)
            act(ss_ap, ss_ap, AF.Exp, [key], [key], scale=-0.5)

        def transpose8(src, src_key, dst, dst_key, evac="act"):
            ps, pk = R.alloc(1)
            psb = ps.bitcast(BF16)

            def f(e):
                for j in range(8):
                    i = e.transpose(out=psb[:, j * 128:(j + 1) * 128], in_=src[:, j * 128:(j + 1) * 128],
                                    identity=ident()[:])
                return i
            S.op("pe", f, [src_key, ident.k()], pk)
            dflat = dst.rearrange("p a b -> p (a b)")
            if evac == "act":
                act(dflat, psb, AF.Copy, pk, [dst_key])
            else:
                S.op("dve", lambda e: e.tensor_copy(out=dflat, in_=psb), pk, [dst_key])
            R.free(pk)

        def proj_tok(hT, hT_key, W, col0, ncols):
            nb = ncols // 512
            ps, pk = R.alloc(nb)

            def f(e):
                for n in range(nb):
                    for kc in range(8):
                        i = e.matmul(ps[:, n * 512:(n + 1) * 512], lhsT=hT[:, kc, :],
                                     rhs=W()[:, kc, col0 + n * 512: col0 + (n + 1) * 512],
                                     start=(kc == 0), stop=(kc == 7))
                return i
            S.op("pe", f, [hT_key, W.k()], pk)
            return ps, pk

        def proj_feat(hT, hT_key, W, col0):
            ps, pk = R.alloc(1)

            def f(e):
                for h in range(4):
                    for kc in range(8):
                        i = e.matmul(ps[:, h * 128:(h + 1) * 128],
                                     lhsT=W()[:, kc, col0 + h * 128: col0 + (h + 1) * 128],
                                     rhs=hT[:, kc, :], start=(kc == 0), stop=(kc == 7))
                return i
            S.op("pe", f, [hT_key, W.k()], pk)
            return ps, pk

        ph12 = ExitStack()
        cumb = B(ph12, "cumb", [128, 128], BF16)
        S.dma("pool", cumb()[:], cumb_d[:, :], "cumb", writes=[cumb.k()])
        onerows = B(ph12, "onerows", [128, 128], F32)
        S.dma("sp", onerows()[:], onerows_d[:, :], "onerows", writes=[onerows.k()])
        alpha3 = {}
        Wra = {}
        for dname, aw, ab, c_ra in (("f", a_fw_w, a_fw_b, C_RAF), ("b", a_bw_w, a_bw_b, C_RAB)):
            A3 = B(ph12, "A3" + dname, [128, 512], F32)
            S.op("pool", lambda e, A3=A3: e.memset(A3()[:], 0.0), [], [A3.k()])
            for blk in range(4):
                S.dma("sp", A3()[32 * blk:32 * blk + 16, :], aw[:, :], "A3" + dname, writes=[A3.k()])
                S.dma("sp", A3()[32 * blk + 16:32 * blk + 17, :], ab[:, :], "A3" + dname, writes=[A3.k()])
            a3 = B(ph12, "alpha3" + dname, [128, 512], BF16)
            alo = B(ph12, "alo" + dname, [128, 512], BF16)
            S.op("dve", lambda e, a3=a3, A3=A3: e.tensor_copy(out=a3()[:], in_=A3()[:]), [A3.k()], [a3.k()])
            S.op("dve", lambda e, a3=a3, A3=A3, alo=alo: e.tensor_tensor(out=alo()[:], in0=A3()[:], in1=a3()[:],
                                                                        op=ALU.subtract),
                 [A3.k(), a3.k()], [alo.k()])
            for r0 in (32, 96):
                S.dma("sp", a3()[r0:r0 + 32, :], alo()[r0:r0 + 32, :], "alo" + dname, reads=[alo.k()],
                      writes=[a3.k()])
            alpha3[dname] = a3
            w = B(ph12, "Wra" + dname, [128, 8, 128], BF16)
            S.op("pool", lambda e, w=w: e.memset(w()[:], 0.0), [], [w.k()])
            for blk in range(4):
                S.dma("pool", w()[:, :, 32 * blk:32 * blk + 16], w_in_v[:, :, c_ra:c_ra + 16], "Wra" + dname,
                      writes=[w.k()])
            Wra[dname] = w

        def decay_logs(hTt, hTk, dname, RAt, RAtk, RA3, RA3k, la_, lak, lahi, lahik, lalo, lalok):
            ps_r, pk_r = R.alloc(1)

            def fr(e):
                for kc in range(8):
                    i_ = e.matmul(ps_r[:, 0:128], lhsT=Wra[dname]()[:, kc, :], rhs=hTt[:, kc, :],
                                  start=(kc == 0), stop=(kc == 7))
                return i_
            S.op("pe", fr, [hTk, Wra[dname].k()], pk_r)
            S.op("dve", lambda e: e.tensor_tensor(out=RAt[:], in0=ps_r[:, 0:128], in1=onerows()[:], op=ALU.add),
                 pk_r + [onerows.k()], [RAtk])
            R.free(pk_r)
            S.op("dve", lambda e: e.tensor_copy(out=RA3[:], in_=RAt[:]), [RAtk], [RA3k])
            S.op("dve", lambda e: e.tensor_tensor(out=RA3[64:128, :], in0=RAt[64:128, :], in1=RA3[64:128, :],
                                                  op=ALU.subtract), [RAtk, RA3k], [RA3k])
            ps_z, pk_z = R.alloc(1)
            S.op("pe", lambda e: e.matmul(ps_z, lhsT=RA3[:], rhs=alpha3[dname]()[:], start=True, stop=True),
                 [RA3k, alpha3[dname].k()], pk_z)
            act(la_[:], ps_z, AF.Exp, pk_z, [lak], scale=-1.0)
            R.free(pk_z)
            act(la_[:], la_[:], AF.Ln, [lak], [lak], bias=1.0)
            S.op("dve", lambda e: e.tensor_scalar(out=la_[:], in0=la_[:], scalar1=-1.0 / 16.0, scalar2=-1.25,
                                                  op0=ALU.mult, op1=ALU.max), [lak], [lak])
            S.op("dve", lambda e: e.tensor_copy(out=lahi[:], in_=la_[:]), [lak], [lahik])
            S.op("dve", lambda e: e.tensor_tensor(out=lalo[:], in0=la_[:], in1=lahi[:], op=ALU.subtract),
                 [lak, lahik], [lalok])

        Wqk = load_w(ph12, "Wqk", w_in_v[:, :, C_Q:C_Q + 1024])
        cumf = B(ph12, "cumf", [128, 128], BF16)
        maskf = B(ph12, "maskf", [128, 4, 128], F32)
        maskb = B(ph12, "maskb", [128, 4, 128], F32)
        gng = B(ph12, "gng", [128, D], F32)
        Wz = B(ph12, "Wz2", [128, 8, 1024], BF16)
        Wbr = B(ph12, "Wbr2", [128, 8, 1024], BF16)

        def issue_sweep2_loads(gate, part):
            if part == 1:
                S.dma("pool", Wz()[:], w_in_v[:, :, C_ZG:C_ZG + 1024], "Wz2", reads=[gate], writes=[Wz.k()])
                return
            if part == 2:
                S.dma("pool", Wbr()[:], w_brg.rearrange("(kc p) n -> p kc n", p=128), "Wbr2", reads=[gate],
                      writes=[Wbr.k()])
                return
            S.dma("pool", cumf()[:], cumf_d[:, :], "cumf", reads=[gate], writes=[cumf.k()])
            for h in range(4):
                S.dma("sp", maskf()[:, h, :], cumf_d[:, :], "maskf", writes=[maskf.k()])
                S.dma("sp", maskb()[:, h, :], maskb_d[:, :], "maskb", writes=[maskb.k()])
            S.dma("sp", gng()[:], gla_g[0:1, :].broadcast_to([128, D]), "gng", writes=[gng.k()])

        with ExitStack() as ph:
            ind = B(ph, "ind", [128, 2], BF16)
            S.dma("pool", ind()[:], ind_d[:, :], "ind", writes=[ind.k()])
            sel = B(ph, "sel", [2, 256], F32)
            S.dma("sp", sel()[:], sel_d[:, :], "sel", writes=[sel.k()])
            Wv = load_w(ph, "Wv1", w_in_v[:, :, C_V:C_V + 1024])

            G1 = B(ph, "G1", [128, D], F32, NSEQ)
            SH = B(ph, "SH", [128, D], F32, NSEQ)
            with ExitStack() as ad:
                scT = B(ad, "scT", [128, 8, NSEQ], F32)
                S.dma("sp", scT()[:], cT[:, :, :], "scT", writes=[scT.k()])
                act(scT()[:], scT()[:], AF.Silu, [scT.k()], [scT.k()])
                bada2 = B(ad, "bada2", [2, 3 * D], F32)
                S.dma("sp", bada2()[:], b_ada[0:1, :].broadcast_to([2, 3 * D]), "bada2", writes=[bada2.k()])
                ng2 = B(ad, "ng2", [2, D], F32)
                S.dma("sp", ng2()[:], norm_g[0:1, :].broadcast_to([2, D]), "ng2", writes=[ng2.k()])
                mod = B(ad, "mod", [2, 3 * D], F32)
                g1rows = B(ad, "g1rows", [2, D], F32)
                wa = B(ad, "wa", [128, 3 * D], F32, 4)
                psm = [R.alloc(1) for _ in range(6)]
                for kc in range(8):
                    S.dma("sp", wa(kc)[:], w_ada[kc * 128:(kc + 1) * 128, :], "wa%d" % (kc % 4), writes=[wa.k(kc)])

                    def f(e, kc=kc):
                        for n in range(6):
                            i = e.matmul(psm[n][0][0:2, :], lhsT=scT()[:, kc, :],
                                         rhs=wa(kc)[:, n * 512:(n + 1) * 512],
                                         start=(kc == 0), stop=(kc == 7), skip_group_check=True)
                        return i
                    S.op("pe", f, [scT.k(), wa.k(kc)], [k_ for (_, pk) in psm for k_ in pk])
                for n in range(6):
                    ps, pk = psm[n]
                    S.op("dve", lambda e, n=n, ps=ps: e.tensor_tensor(
                        out=mod()[:, n * 512:(n + 1) * 512], in0=ps[0:2, :],
                        in1=bada2()[:, n * 512:(n + 1) * 512], op=ALU.add), pk + [bada2.k()], [mod.k()])
                    R.free(pk)
                S.op("dve", lambda e: e.scalar_tensor_tensor(
                    out=g1rows()[:], in0=mod()[:, D:2 * D], scalar=1.0, in1=ng2()[:],
                    op0=ALU.add, op1=ALU.mult), [mod.k(), ng2.k()], [g1rows.k()])
                for b in range(NSEQ):
                    for (rows, rk, c0, dst) in ((g1rows, g1rows.k(), 0, G1), (mod, mod.k(), 0, SH),
                                                (mod, mod.k(), 2 * D, GATE)):
                        for hf in range(2):
                            ps, pk = R.alloc(1)
                            S.op("pe", lambda e, ps=ps, rows=rows, c0=c0, hf=hf, b=b: e.matmul(
                                ps, lhsT=sel()[0:2, b * 128:(b + 1) * 128],
                                rhs=rows()[0:2, c0 + hf * 512: c0 + (hf + 1) * 512], start=True, stop=True),
                                [sel.k(), rk], pk)
                            act(dst(b)[:, hf * 512:(hf + 1) * 512], ps, AF.Copy, pk, [dst.k(b)])
                            R.free(pk)
                S.flush()
            xt = B(ph, "xt1", [128, D], F32, 6)
            ss = B(ph, "ss1", [128, 1], F32, 3)
            xn = B(ph, "xn1", [128, D], F32, 3)
            hb = B(ph, "hb1", [128, D], BF16, 3)
            hT = B(ph, "hT1", [128, 8, 128], BF16, 3)
            RAt = B(ph, "RAt1", [128, 128], F32, 3)
            RA3 = B(ph, "RA31", [128, 128], BF16, 3)
            lahl = B(ph, "lahl1", [128, D], BF16, 3)
            la = B(ph, "la1", [128, 512], F32, 3)
            En = B(ph, "En1", [128, 512], F32, 3)
            ktb = B(ph, "ktb1", [128, 512], BF16, 3)
            vb = B(ph, "vb1", [128, D], BF16, 3)
            dec = [B(ph, "dec1_%d" % b, [128, 8], F32, 2) for b in range(NSEQ)]
            T = B(ph, "T1", [128, D], F32, NSEQ)
            Sb = B(ph, "Sb1", [128, D], BF16, 3)
            for b in range(NSEQ):
                S.op("pool", lambda e, b=b: e.memset(T(b)[:], 0.0), [], [("T1h", b, 0), ("T1h", b, 1)])
                for s in range(2):
                    S.op("pool", lambda e, b=b, s=s: e.memset(dec[b](s)[:], 0.0), [], [dec[b].k(s)])

            cnt = {"x": 0, "t": 0, "s": 0}

            def load_x(b, i, xbuf):
                s = cnt["x"]
                cnt["x"] += 1
                r0 = b * SEQ + i * 128
                S.dma("sp", xbuf(s)[:], x[r0:r0 + 128, :], "%s%d" % (xbuf.name, s % xbuf.n), writes=[xbuf.k(s)])
                return s

            def sweep1_tile(b, i, xs):
                s = cnt["t"]
                cnt["t"] += 1
                act(junk()[:], xt(xs)[:], AF.Square, [xt.k(xs)], [ss.k(s), junk.k()], accum_out=ss(s)[:, 0:1])
                rstd_from_ss(ss(s)[:, 0:1], D, ss.k(s))
                S.op("dve", lambda e: e.scalar_tensor_tensor(
                    out=xn(s)[:], in0=xt(xs)[:], scalar=ss(s)[:, 0:1], in1=G1(b)[:],
                    op0=ALU.mult, op1=ALU.mult), [xt.k(xs), ss.k(s), G1.k(b)], [xn.k(s)])
                S.op("dve", lambda e: e.tensor_tensor(out=hb(s)[:], in0=xn(s)[:], in1=SH(b)[:], op=ALU.add),
                     [xn.k(s), SH.k(b)], [hb.k(s)])
                yield
                transpose8(hb(s), hb.k(s), hT(s), hT.k(s))
                S.dma("sp", hts[b * NT + i], hT(s).rearrange("p a b -> p (a b)"), "hT1_%d" % (s % 3),
                      reads=[hT.k(s)])
                yield
                lahi_ap, lalo_ap = lahl(s)[:, 0:512], lahl(s)[:, 512:1024]
                decay_logs(hT(s), hT.k(s), "b", RAt(s), RAt.k(s), RA3(s), RA3.k(s), la(s), la.k(s),
                           lahi_ap, lahl.k(s), lalo_ap, lahl.k(s))
                S.dma("sp", labs[b * NT + i], lahl(s)[:], "lahl1_%d" % (s % 3), reads=[lahl.k(s)])
                yield
                ps_b, pk_b = R.alloc(1)

                def fbt(e):
                    e.matmul(ps_b, lhsT=cumb()[:], rhs=lahi_ap, start=True, stop=False)
                    return e.matmul(ps_b, lhsT=cumb()[:], rhs=lalo_ap, start=False, stop=True)
                S.op("pe", fbt, [cumb.k(), lahl.k(s)], pk_b)
                ps_d, pk_d = R.alloc(1)

                def fd(e):
                    for h in range(4):
                        e.matmul(ps_d[:, 2 * h:2 * h + 2], lhsT=lahi_ap[:, h * 128:(h + 1) * 128],
                                 rhs=ind()[:], start=True, stop=False)
                        i_ = e.matmul(ps_d[:, 2 * h:2 * h + 2], lhsT=lalo_ap[:, h * 128:(h + 1) * 128],
                                      rhs=ind()[:], start=False, stop=True)
                    return i_
                S.op("pe", fd, [lahl.k(s), ind.k()], pk_d)
                act(En(s)[:], ps_b, AF.Exp, pk_b, [En.k(s)], scale=-1.0)
                act(dec[b](i)[:], ps_d[:, 0:8], AF.Exp, pk_d, [dec[b].k(i)])
                R.free(pk_b)
                R.free(pk_d)
                yield
                ps_k, pk_k = proj_tok(hT(s), hT.k(s), Wqk, 512, 512)
                S.op("dve", lambda e: e.tensor_tensor(out=ktb(s)[:], in0=ps_k, in1=En(s)[:], op=ALU.mult),
                     pk_k + [En.k(s)], [ktb.k(s)])
                R.free(pk_k)
                for n_ in range(2):
                    ps_v, pk_v = R.alloc(1)

                    def fv(e, n_=n_, ps_v=ps_v):
                        for kc in range(8):
                            i_ = e.matmul(ps_v, lhsT=hT(s)[:, kc, :], rhs=Wv()[:, kc, n_ * 512:(n_ + 1) * 512],
                                          start=(kc == 0), stop=(kc == 7))
                        return i_
                    S.op("pe", fv, [hT.k(s), Wv.k()], pk_v)
                    act(vb(s)[:, n_ * 512:(n_ + 1) * 512], ps_v, AF.Copy, pk_v, [("vb1h", s % vb.n, n_)])
                    R.free(pk_v)
                vbk = [("vb1h", s % vb.n, 0), ("vb1h", s % vb.n, 1)]
                S.dma("sp", vsc[b * NT + i], vb(s)[:], "vb1_%d" % (s % 3), reads=vbk)
                if i == NT - 1:
                    dump("la_%d" % b, la, s, [128, 512], F32)
                    dump("En_%d" % b, En, s, [128, 512], F32)
                    dump("ktb_%d" % b, ktb, s, [128, 512], BF16)
                    dump("dec_%d" % b, dec[b], i, [128, 8], F32)
                yield
                for c in (1, 0):
                    n = 2 * i + c
                    dprev, dpk = (dec[b](i + 1), dec[b].k(i + 1)) if c == 1 else (dec[b](i), dec[b].k(i))
                    pcol = 0 if c == 1 else 1
                    for hp in range(2):
                        ps_kv, pk_kv = R.alloc(1)

                        def fkv(e, c=c, hp=hp, ps_kv=ps_kv):
                            for hh in range(2):
                                h = 2 * hp + hh
                                i_ = e.matmul(ps_kv[:, hh * 256:(hh + 1) * 256],
                                              lhsT=ktb(s)[64 * c:64 * c + 64, h * 128:(h + 1) * 128],
                                              rhs=vb(s)[64 * c:64 * c + 64, h * 256:(h + 1) * 256],
                                              start=True, stop=True)
                            return i_
                        S.op("pe", fkv, [ktb.k(s), vbk[hp]], pk_kv)

                        def fT(e, hp=hp, ps_kv=ps_kv, dprev=dprev, pcol=pcol):
                            for hh in range(2):
                                h = 2 * hp + hh
                                i_ = e.scalar_tensor_tensor(
                                    out=T(b)[:, h * 256:(h + 1) * 256], in0=T(b)[:, h * 256:(h + 1) * 256],
                                    scalar=dprev[:, 2 * h + pcol:2 * h + pcol + 1],
                                    in1=ps_kv[:, hh * 256:(hh + 1) * 256], op0=ALU.mult, op1=ALU.add)
                            return i_
                        S.op("dve", fT, [("T1h", b, hp), dpk] + pk_kv, [("T1h", b, hp)])
                        R.free(pk_kv)
                    if n >= 1:
                        q = cnt["s"]
                        cnt["s"] += 1

                        def fS(e, c=c, q=q):
                            for h in range(4):
                                i_ = e.activation(out=Sb(q)[:, h * 256:(h + 1) * 256],
                                                  in_=T(b)[:, h * 256:(h + 1) * 256], func=AF.Identity,
                                                  scale=dec[b](i)[:, 2 * h + c:2 * h + c + 1])
                            return i_
                        S.op("act", fS, [("T1h", b, 0), ("T1h", b, 1), dec[b].k(i)], [Sb.k(q)])
                        S.dma("sp", sbw[b * NCH + n - 1], Sb(q)[:], "Sb1_%d" % (q % 3), reads=[Sb.k(q)])
                    yield

            xq = {NT - 1: [load_x(b, NT - 1, xt) for b in range(NSEQ)],
                  NT - 2: [load_x(b, NT - 2, xt) for b in range(NSEQ)]}
            for i in range(NT - 1, -1, -1):
                xs_cur = xq.pop(i)
                if i - 2 >= 0:
                    xq[i - 2] = [load_x(b, i - 2, xt) for b in range(NSEQ)]
                interleave([sweep1_tile(b, i, xs_cur[b]) for b in range(NSEQ)])
                if i in (NT - 4, NT - 8, NT - 12):
                    issue_sweep2_loads(xt.k(xs_cur[0]), {NT - 4: 0, NT - 8: 1, NT - 12: 2}[i])
            S.flush()

        with ExitStack() as ph:
            hT = B(ph, "hT2", [128, 8, 128], BF16, 4)
            SbL = B(ph, "SbL", [128, 2, D], BF16, 4)
            RAt = B(ph, "RAt2", [128, 128], F32, 2)
            RA3 = B(ph, "RA32", [128, 128], BF16, 2)
            lafh = B(ph, "lafh", [128, 512], BF16, 2)
            lafl = B(ph, "lafl", [128, 512], BF16, 2)
            labhl = B(ph, "labhl", [128, D], BF16, 4)
            laf = B(ph, "laf2", [128, 512], F32, 2)
            Ef = B(ph, "Ef", [128, 512], F32, 2)
            Enf = B(ph, "Enf", [128, 512], F32, 2)
            Eb = B(ph, "Eb", [128, 512], F32, 2)
            Enb = B(ph, "Enb", [128, 512], F32, 2)
            qtf = B(ph, "qtf", [128, 512], BF16, 2)
            ktf = B(ph, "ktf", [128, 512], BF16, 2)
            qtb = B(ph, "qtb", [128, 512], BF16, 2)
            ktbb = B(ph, "ktbb", [128, 512], BF16, 2)
            Af = B(ph, "Af", [128, 512], BF16, 2)
            Ab = B(ph, "Ab", [128, 512], BF16, 2)
            ktok = B(ph, "ktok", [128, 512], BF16, 2)
            vb = B(ph, "vb2", [128, D], BF16, 4)
            decf = [B(ph, "decf_%d" % b, [128, 8], F32, 2) for b in range(NSEQ)]
            T = B(ph, "T2", [128, D], F32, NSEQ)
            Sf = [B(ph, "Sf_%d" % b, [128, D], BF16, 3) for b in range(NSEQ)]
            sso = B(ph, "sso", [128, 4], F32, 2)
            sz = B(ph, "sz2", [128, D], F32, 2)
            og = B(ph, "og", [128, D], BF16, 2)
            ogT = B(ph, "ogT", [128, 8, 128], BF16, 2)
            ysb = B(ph, "ysb", [128, D], F32, 2)
            for b in range(NSEQ):
                S.op("pool", lambda e, b=b: e.memset(T(b)[:], 0.0), [], [("T2h", b, 0), ("T2h", b, 1)])
                for s in range(2):
                    S.op("pool", lambda e, b=b, s=s: e.memset(decf[b](s)[:], 0.0), [], [decf[b].k(s)])
            LNQ = math.log(128.0 ** -0.5)
            cnt = {"l": 0, "t": 0}
            sfc = [0, 0]

            def load_tile2(b, i):
                s = cnt["l"]
                cnt["l"] += 1
                S.dma("sp", hT(s).rearrange("p a b -> p (a b)"), hts[b * NT + i], "hT2_%d" % (s % 4),
                      writes=[hT.k(s)])
                if 2 * i + 1 < NCH - 1:
                    S.dma("sp", SbL(s)[:], sbw[b * NCH + 2 * i: b * NCH + 2 * i + 2].rearrange("c p n -> p c n"),
                          "SbL_%d" % (s % 4), writes=[SbL.k(s)])
                else:
                    S.dma("sp", SbL(s)[:, 0, :], sbw[b * NCH + 2 * i], "SbL_%d" % (s % 4), writes=[SbL.k(s)])
                S.dma("sp", vb(s)[:], vsc[b * NT + i], "vb2_%d" % (s % 4), writes=[vb.k(s)])
                S.dma("sp", labhl(s)[:], labs[b * NT + i], "labhl_%d" % (s % 4), writes=[labhl.k(s)])
                return s

            def sweep2_tile(b, i, ls):
                s = cnt["t"]
                cnt["t"] += 1
                hTt, hTk = hT(ls), hT.k(ls)
                decay_logs(hTt, hTk, "f", RAt(s), RAt.k(s), RA3(s), RA3.k(s), laf(s), laf.k(s),
                           lafh(s), lafh.k(s), lafl(s), lafl.k(s))
                yield
                for (lh, ll, lks, cm, Epos, Eneg, dcy) in (
                        (lafh(s)[:], lafl(s)[:], [lafh.k(s), lafl.k(s)], cumf, Ef, Enf, True),
                        (labhl(ls)[:, 0:512], labhl(ls)[:, 512:1024], [labhl.k(ls)], cumb, Eb, Enb, False)):
                    ps_b, pk_b = R.alloc(1)

                    def fb(e, ps_b=ps_b, lh=lh, ll=ll, cm=cm):
                        for h in range(4):
                            e.matmul(ps_b[:, h * 128:(h + 1) * 128], lhsT=lh[:, h * 128:(h + 1) * 128],
                                     rhs=cm()[:], start=True, stop=False)
                            i_ = e.matmul(ps_b[:, h * 128:(h + 1) * 128], lhsT=ll[:, h * 128:(h + 1) * 128],
                                          rhs=cm()[:], start=False, stop=True)
                        return i_
                    S.op("pe", fb, lks + [cm.k()], pk_b)
                    act(Epos(s)[:], ps_b, AF.Exp, pk_b, [Epos.k(s)], bias=LNQ)
                    act(Eneg(s)[:], ps_b, AF.Exp, pk_b, [Eneg.k(s)], scale=-1.0)
                    if dcy:
                        act(decf[b](i)[:], ps_b[:, 63::64], AF.Exp, pk_b, [decf[b].k(i)])
                    R.free(pk_b)
                yield
                ps_q, pk_q = proj_feat(hTt, hTk, Wqk, 0)
                for (dst, E_) in ((qtf, Ef), (qtb, Eb)):
                    S.op("dve", lambda e, dst=dst, E_=E_: e.tensor_tensor(
                        out=dst(s)[:], in0=ps_q, in1=E_(s)[:], op=ALU.mult), pk_q + [E_.k(s)], [dst.k(s)])
                R.free(pk_q)
                ps_k, pk_k = proj_feat(hTt, hTk, Wqk, 512)
                for (dst, E_) in ((ktf, Enf), (ktbb, Enb)):
                    S.op("dve", lambda e, dst=dst, E_=E_: e.tensor_tensor(
                        out=dst(s)[:], in0=ps_k, in1=E_(s)[:], op=ALU.mult), pk_k + [E_.k(s)], [dst.k(s)])
                R.free(pk_k)
                yield
                for (kt_, qt_, msk, A_) in ((ktf, qtf, maskf, Af), (ktbb, qtb, maskb, Ab)):
                    ps_a, pk_a = R.alloc(1)

                    def fa(e, ps_a=ps_a, kt_=kt_, qt_=qt_):
                        for h in range(4):
                            i_ = e.matmul(ps_a[:, h * 128:(h + 1) * 128], lhsT=kt_(s)[:, h * 128:(h + 1) * 128],
                                          rhs=qt_(s)[:, h * 128:(h + 1) * 128], start=True, stop=True)
                        return i_
                    S.op("pe", fa, [kt_.k(s), qt_.k(s)], pk_a)
                    S.op("dve", lambda e, ps_a=ps_a, msk=msk, A_=A_: e.tensor_tensor(
                        out=A_(s)[:], in0=ps_a, in1=msk().rearrange("p a b -> p (a b)"), op=ALU.mult),
                        pk_a + [msk.k()], [A_.k(s)])
                    R.free(pk_a)
                ps_t, pk_t = R.alloc(1)
                pst = ps_t.bitcast(BF16)

                def ft(e):
                    for h in range(4):
                        i_ = e.transpose(out=pst[:, h * 128:(h + 1) * 128], in_=ktf(s)[:, h * 128:(h + 1) * 128],
                                         identity=ident()[:])
                    return i_
                S.op("pe", ft, [ktf.k(s), ident.k()], pk_t)
                act(ktok(s)[:], pst[:, 0:512], AF.Copy, pk_t, [ktok.k(s)])
                R.free(pk_t)
                yield
                n0 = 2 * i
                q0 = sfc[b]
                pso = []
                for hp in range(2):
                    ps_o, pk_o = R.alloc(1)
                    pso.append((ps_o, pk_o))

                    def fo1(e, hp=hp, ps_o=ps_o):
                        for hh in range(2):
                            h = 2 * hp + hh
                            oc = slice(hh * 256, (hh + 1) * 256)
                            vc = slice(h * 256, (h + 1) * 256)
                            hc = slice(h * 128, (h + 1) * 128)
                            e.matmul(ps_o[:, oc], lhsT=Af(s)[:, hc], rhs=vb(ls)[:, vc], start=(hh == 0), stop=False,
                                     skip_group_check=True)
                            i_ = e.matmul(ps_o[:, oc], lhsT=Ab(s)[:, hc], rhs=vb(ls)[:, vc], start=False, stop=False,
                                          skip_group_check=True)
                            for c in (0, 1):
                                if n0 + c < NCH - 1:
                                    i_ = e.matmul(ps_o[64 * c:64 * c + 64, oc],
                                                  lhsT=qtb(s)[:, h * 128 + 64 * c: h * 128 + 64 * c + 64],
                                                  rhs=SbL(ls)[:, c, vc], start=False, stop=False,
                                                  skip_group_check=True)
                            if n0 >= 1:
                                i_ = e.matmul(ps_o[0:64, oc], lhsT=qtf(s)[:, h * 128: h * 128 + 64],
                                              rhs=Sf[b](q0)[:, vc], start=False, stop=False, skip_group_check=True)
                        return i_
                    rd = [Af.k(s), Ab.k(s), vb.k(ls), qtb.k(s), qtf.k(s), SbL.k(ls)]
                    if n0 >= 1:
                        rd.append(Sf[b].k(q0))
                    S.op("pe", fo1, rd, pk_o)
                for c in (0, 1):
                    dprev, dpk = (decf[b](i - 1), decf[b].k(i - 1)) if c == 0 else (decf[b](i), decf[b].k(i))
                    pcol = 1 if c == 0 else 0
                    for hp in range(2):
                        ps_kv, pk_kv = R.alloc(1)

                        def fkv(e, c=c, hp=hp, ps_kv=ps_kv):
                            for hh in range(2):
                                h = 2 * hp + hh
                                i_ = e.matmul(ps_kv[:, hh * 256:(hh + 1) * 256],
                                              lhsT=ktok(s)[64 * c:64 * c + 64, h * 128:(h + 1) * 128],
                                              rhs=vb(ls)[64 * c:64 * c + 64, h * 256:(h + 1) * 256],
                                              start=True, stop=True)
                            return i_
                        S.op("pe", fkv, [ktok.k(s), vb.k(ls)], pk_kv)

                        def fT(e, hp=hp, ps_kv=ps_kv, dprev=dprev, pcol=pcol):
                            for hh in range(2):
                                h = 2 * hp + hh
                                i_ = e.scalar_tensor_tensor(
                                    out=T(b)[:, h * 256:(h + 1) * 256], in0=T(b)[:, h * 256:(h + 1) * 256],
                                    scalar=dprev[:, 2 * h + pcol:2 * h + pcol + 1],
                                    in1=ps_kv[:, hh * 256:(hh + 1) * 256], op0=ALU.mult, op1=ALU.add)
                            return i_
                        S.op("dve", fT, [("T2h", b, hp), dpk] + pk_kv, [("T2h", b, hp)])
                        R.free(pk_kv)
                    if n0 + c < NCH - 1:
                        sfc[b] += 1
                        q = sfc[b]

                        def fS(e, c=c, q=q):
                            for h in range(4):
                                i_ = e.activation(out=Sf[b](q)[:, h * 256:(h + 1) * 256],
                                                  in_=T(b)[:, h * 256:(h + 1) * 256], func=AF.Identity,
                                                  scale=decf[b](i)[:, 2 * h + c:2 * h + c + 1])
                            return i_
                        S.op("act", fS, [("T2h", b, 0), ("T2h", b, 1), decf[b].k(i)], [Sf[b].k(q)])
                    if c == 0:
                        q1 = sfc[b]
                        for hp in range(2):
                            ps_o, pk_o = pso[hp]

                            def fo2(e, q1=q1, hp=hp, ps_o=ps_o):
                                for hh in range(2):
                                    h = 2 * hp + hh
                                    i_ = e.matmul(ps_o[64:128, hh * 256:(hh + 1) * 256],
                                                  lhsT=qtf(s)[:, h * 128 + 64: h * 128 + 128],
                                                  rhs=Sf[b](q1)[:, h * 256:(h + 1) * 256], start=False, stop=True,
                                                  skip_group_check=True)
                                return i_
                            S.op("pe", fo2, [qtf.k(s), Sf[b].k(q1)] + pk_o, pk_o)
                yield
                for n_ in range(2):
                    ps_z, pk_z = R.alloc(1)

                    def fz(e, n_=n_, ps_z=ps_z):
                        for kc in range(8):
                            i_ = e.matmul(ps_z, lhsT=hTt[:, kc, :], rhs=Wz()[:, kc, n_ * 512:(n_ + 1) * 512],
                                          start=(kc == 0), stop=(kc == 7))
                        return i_
                    S.op("pe", fz, [hTk, Wz.k()], pk_z)
                    act(sz(s)[:, n_ * 512:(n_ + 1) * 512], ps_z, AF.Silu, pk_z, [sz.k(s)])
                    R.free(pk_z)
                S.op("dve", lambda e: e.tensor_tensor(out=sz(s)[:], in0=sz(s)[:], in1=gng()[:], op=ALU.mult),
                     [sz.k(s), gng.k()], [sz.k(s)])
                for hp in range(2):
                    ps_o, pk_o = pso[hp]

                    def fsq(e, hp=hp, ps_o=ps_o):
                        for hh in range(2):
                            h = 2 * hp + hh
                            i_ = e.activation(out=junk()[:, h * 256:(h + 1) * 256], in_=ps_o[:, hh * 256:(hh + 1) * 256],
                                              func=AF.Square, accum_out=sso(s)[:, h:h + 1])
                        return i_
                    S.op("act", fsq, pk_o, [sso.k(s), ("junkh", hp)])
                rstd_from_ss(sso(s)[:], 256, sso.k(s))
                for hp in range(2):
                    ps_o, pk_o = pso[hp]

                    def fog(e, hp=hp, ps_o=ps_o):
                        for hh in range(2):
                            h = 2 * hp + hh
                            i_ = e.scalar_tensor_tensor(out=og(s)[:, h * 256:(h + 1) * 256],
                                                        in0=ps_o[:, hh * 256:(hh + 1) * 256],
                                                        scalar=sso(s)[:, h:h + 1],
                                                        in1=sz(s)[:, h * 256:(h + 1) * 256],
                                                        op0=ALU.mult, op1=ALU.mult)
                        return i_
                    S.op("dve", fog, pk_o + [sso.k(s), sz.k(s)], [og.k(s)])
                    R.free(pk_o)
                yield
                transpose8(og(s), og.k(s), ogT(s), ogT.k(s))
                for n_ in range(2):
                    ps_y, pk_y = R.alloc(1)

                    def fy(e, n_=n_, ps_y=ps_y):
                        for kc in range(8):
                            i_ = e.matmul(ps_y, lhsT=ogT(s)[:, kc, :], rhs=Wbr()[:, kc, n_ * 512:(n_ + 1) * 512],
                                          start=(kc == 0), stop=(kc == 7))
                        return i_
                    S.op("pe", fy, [ogT.k(s), Wbr.k()], pk_y)
                    S.op("dve", lambda e, n_=n_, ps_y=ps_y: e.tensor_copy(out=ysb(s)[:, n_ * 512:(n_ + 1) * 512],
                                                                        in_=ps_y), pk_y, [ysb.k(s)])
                    R.free(pk_y)
                r0 = b * SEQ + i * 128
                S.dma("sp", ygla[r0:r0 + 128, :], ysb(s)[:], "ysb_%d" % (s % 2), reads=[ysb.k(s)])
                yield

            w3src = (w_in_v[:, :, C_U:C_U + 1024], w_in_v[:, :, C_VS:C_VS + 1024], w_in_v[:, :, C_ZM:C_ZM + 1024],
                     w_brm.rearrange("(kc p) n -> p kc n", p=128), w_in_v[:, :, C_MGMLP:C_MGMLP + 1024],
                     w_in_v[:, :, C_MGLA:C_MGLA + 1024], w_out.rearrange("(kc p) n -> p kc n", p=128))
            ls_next = [load_tile2(b, 0) for b in range(NSEQ)]
            for i in range(NT):
                ls_cur = ls_next
                if i + 1 < NT:
                    ls_next = [load_tile2(b, i + 1) for b in range(NSEQ)]
                interleave([sweep2_tile(b, i, ls_cur[b]) for b in range(NSEQ)])
                if 1 <= i <= 7:
                    S.dma("pool", wst[i - 1].rearrange("p (a b) -> p a b", a=8), w3src[i - 1], "wst%d" % (i - 1),
                          reads=[hT.k(ls_cur[0])])
            S.flush()

        ph12.close()

        with ExitStack() as ph:
            lng = bcast_row(ph, "lng", ln_g)
            lnb = bcast_row(ph, "lnb", ln_b)
            fg = bcast_row(ph, "fg", fin_g)
            wsT = B(ph, "wsT", [128, 8, 128], BF16)
            S.dma("pool", wsT()[:], wsT_d[:, :, :], "wsT", writes=[wsT.k()])
            bsT = B(ph, "bsT", [128, 8], F32)
            S.dma("sp", bsT()[:], bsT_d[:, :], "bsT", writes=[bsT.k()])
            W3 = {}
            prev = None
            for j, nm in ((0, "Wu"), (1, "Wvs"), (2, "Wzm"), (4, "Wmm"), (5, "Wml"), (3, "Wbm"), (6, "Wo")):
                w = B(ph, nm, [128, 8, 1024], BF16)
                W3[nm] = w
            W3 = [W3[nm] for nm in ("Wu", "Wvs", "Wzm", "Wbm", "Wmm", "Wml", "Wo")]
            Wu, Wvs, Wzm, Wbm, Wmm, Wml, Wo = W3

            hT = B(ph, "hT3", [128, 8, 128], BF16, 2)
            xt = B(ph, "xt3", [128, D], F32, 2)
            ygl = B(ph, "ygl", [128, D], F32, 2)
            gu = B(ph, "gu", [128, D], F32)
            gv = B(ph, "gv", [128, D], F32)
            st = B(ph, "st3", [128, 16], F32, 2)
            vsn = B(ph, "vsn", [128, D], BF16)
            sz = B(ph, "sz3", [128, D], F32)
            prod = B(ph, "prod", [128, D], BF16)
            prodT = B(ph, "prodT", [128, 8, 128], BF16)
            sgm = B(ph, "sgm", [128, D], F32)
            sgl = B(ph, "sgl", [128, D], F32)
            mrgb = B(ph, "mrgb", [128, D], BF16)
            mrgT = B(ph, "mrgT", [128, 8, 128], BF16)
            ot = B(ph, "ot", [128, D], F32, 2)
            cnt = {"l": 0, "t": 0}

            def load_tile3(b, i, rest=True):
                s = cnt["l"]
                cnt["l"] += 1
                r0 = b * SEQ + i * 128
                S.dma("sp", hT(s).rearrange("p a b -> p (a b)"), hts[b * NT + i], "hT3_%d" % (s % 2),
                      writes=[hT.k(s)])
                if rest:
                    load_rest3(b, i, s, [])
                return s

            def load_rest3(b, i, s, gate):
                r0 = b * SEQ + i * 128
                S.dma("sp", xt(s)[:], x[r0:r0 + 128, :], "xt3_%d" % (s % 2), reads=gate, writes=[xt.k(s)])
                S.dma("sp", ygl(s)[:], ygla[r0:r0 + 128, :], "ygl_%d" % (s % 2), reads=gate, writes=[ygl.k(s)])

            def proj_half(hTt, hTk, W, n):
                ps, pk = R.alloc(1)

                def f(e):
                    for kc in range(8):
                        i_ = e.matmul(ps, lhsT=hTt[:, kc, :], rhs=W()[:, kc, n * 512:(n + 1) * 512],
                                      start=(kc == 0), stop=(kc == 7))
                    return i_
                S.op("pe", f, [hTk, W.k()], pk)
                return ps, pk

            def mm_half(xT, xTk, W, n):
                ps, pk = R.alloc(1)

                def f(e):
                    for kc in range(8):
                        i_ = e.matmul(ps, lhsT=xT()[:, kc, :], rhs=W()[:, kc, n * 512:(n + 1) * 512],
                                      start=(kc == 0), stop=(kc == 7))
                    return i_
                S.op("pe", f, [xTk, W.k()], pk)
                return ps, pk

            def sweep3_tile(b, i, ls):
                s = cnt["t"]
                cnt["t"] += 1
                hTt, hTk = hT(ls), hT.k(ls)
                H = (slice(0, 512), slice(512, 1024))
                for n in range(2):
                    ps, pk = proj_half(hTt, hTk, Wu, n)
                    act(gu()[:, H[n]], ps, AF.Gelu, pk, [gu.k()])
                    R.free(pk)
                for n in range(2):
                    ps, pk = proj_half(hTt, hTk, Wvs, n)
                    act(gv()[:, H[n]], ps, AF.Gelu, pk, [gv.k()])
                    R.free(pk)
                S.op("dve", lambda e: e.bn_stats(out=st(s)[:, 0:6], in_=gv()[:, 0:512]), [gv.k()], [st.k(s)])
                S.op("dve", lambda e: e.bn_stats(out=st(s)[:, 6:12], in_=gv()[:, 512:1024]), [gv.k()], [st.k(s)])
                S.op("dve", lambda e: e.bn_aggr(out=st(s)[:, 12:14], in_=st(s)[:, 0:12]), [st.k(s)], [st.k(s)])
                act(st(s)[:, 13:14], st(s)[:, 13:14], AF.Ln, [st.k(s)], [st.k(s)], bias=EPS)
                act(st(s)[:, 13:14], st(s)[:, 13:14], AF.Exp, [st.k(s)], [st.k(s)], scale=-0.5)
                S.op("dve", lambda e: e.scalar_tensor_tensor(out=gv()[:], in0=gv()[:], scalar=st(s)[:, 12:13],
                                                             in1=lng()[:], op0=ALU.subtract, op1=ALU.mult),
                     [gv.k(), st.k(s), lng.k()], [gv.k()])
                S.op("dve", lambda e: e.scalar_tensor_tensor(out=vsn()[:], in0=gv()[:], scalar=st(s)[:, 13:14],
                                                             in1=lnb()[:], op0=ALU.mult, op1=ALU.add),
                     [gv.k(), st.k(s), lnb.k()], [vsn.k()])
                for n in range(2):
                    ps, pk = proj_half(hTt, hTk, Wzm, n)
                    act(sz()[:, H[n]], ps, AF.Silu, pk, [sz.k()])
                    R.free(pk)
                S.op("dve", lambda e: e.tensor_tensor(out=gu()[:], in0=gu()[:], in1=sz()[:], op=ALU.mult),
                     [gu.k(), sz.k()], [gu.k()])
                for n in range(2):
                    ps, pk = proj_half(hTt, hTk, Wmm, n)
                    act(sgm()[:, H[n]], ps, AF.Sigmoid, pk, [sgm.k()])
                    R.free(pk)
                for n in range(2):
                    ps, pk = proj_half(hTt, hTk, Wml, n)
                    act(sgl()[:, H[n]], ps, AF.Sigmoid, pk, [sgl.k()])
                    R.free(pk)
                S.op("dve", lambda e: e.tensor_tensor(out=sgl()[:], in0=sgl()[:], in1=ygl(ls)[:], op=ALU.mult),
                     [sgl.k(), ygl.k(ls)], [sgl.k()])
                for n in range(2):
                    ps_g, pk_g = R.alloc(1)

                    def fg_(e, n=n, ps_g=ps_g):
                        for g4 in range(4):
                            g = 4 * n + g4
                            i_ = e.matmul(ps_g[:, g4 * 128:(g4 + 1) * 128], lhsT=wsT()[:, g, :],
                                          rhs=vsn()[:, g * 128:(g + 1) * 128], start=True, stop=True)
                        return i_
                    S.op("pe", fg_, [wsT.k(), vsn.k()], pk_g)

                    def fp(e, n=n, ps_g=ps_g):
                        for g4 in range(4):
                            g = 4 * n + g4
                            i_ = e.scalar_tensor_tensor(out=prod()[:, g * 128:(g + 1) * 128],
                                                        in0=ps_g[:, g4 * 128:(g4 + 1) * 128], scalar=bsT()[:, g:g + 1],
                                                        in1=gu()[:, g * 128:(g + 1) * 128], op0=ALU.add, op1=ALU.mult)
                        return i_
                    S.op("dve", fp, pk_g + [bsT.k(), gu.k()], [prod.k()])
                    R.free(pk_g)
                transpose8(prod(), prod.k(), prodT(), prodT.k())
                for n in range(2):
                    ps_y, pk_y = mm_half(prodT, prodT.k(), Wbm, n)
                    S.op("dve", lambda e, n=n, ps_y=ps_y: e.tensor_tensor(
                        out=sgm()[:, H[n]], in0=ps_y, in1=sgm()[:, H[n]], op=ALU.mult), pk_y + [sgm.k()], [sgm.k()])
                    R.free(pk_y)
                S.op("dve", lambda e: e.tensor_tensor(out=mrgb()[:], in0=sgm()[:], in1=sgl()[:], op=ALU.add),
                     [sgm.k(), sgl.k()], [mrgb.k()])
                transpose8(mrgb(), mrgb.k(), mrgT(), mrgT.k())
                for n in range(2):
                    ps_o, pk_o = mm_half(mrgT, mrgT.k(), Wo, n)
                    S.op("dve", lambda e, n=n, ps_o=ps_o: e.tensor_tensor(
                        out=sgm()[:, H[n]], in0=ps_o, in1=GATE(b)[:, H[n]], op=ALU.mult),
                        pk_o + [GATE.k(b)], [sgm.k()])
                    R.free(pk_o)
                S.op("dve", lambda e: e.tensor_tensor(out=sgm()[:], in0=sgm()[:], in1=xt(ls)[:], op=ALU.add),
                     [sgm.k(), xt.k(ls)], [sgm.k()])
                act(junk()[:], sgm()[:], AF.Square, [sgm.k()], [st.k(s), junk.k()], accum_out=st(s)[:, 14:15])
                rstd_from_ss(st(s)[:, 14:15], D, st.k(s))
                S.op("dve", lambda e: e.scalar_tensor_tensor(out=ot(s)[:], in0=sgm()[:], scalar=st(s)[:, 14:15],
                                                             in1=fg()[:], op0=ALU.mult, op1=ALU.mult),
                     [sgm.k(), st.k(s), fg.k()], [ot.k(s)])
                r0 = b * SEQ + i * 128
                S.dma("sp", out[r0:r0 + 128, :], ot(s)[:], "ot_%d" % (s % 2), reads=[ot.k(s)])

            order = [(b, i) for b in range(NSEQ) for i in range(NT)]
            ls_next = load_tile3(*order[0], rest=False)
            for q_, (j, nm) in enumerate(((0, "Wu"), (1, "Wvs"), (2, "Wzm"), (4, "Wmm"), (5, "Wml"), (3, "Wbm"), (6, "Wo"))):
                w = dict(Wu=Wu, Wvs=Wvs, Wzm=Wzm, Wbm=Wbm, Wmm=Wmm, Wml=Wml, Wo=Wo)[nm]
                if q_ % 2 == 0:
                    S.dma("act", w()[:].rearrange("p a b -> p (a b)"), wst[j], nm, writes=[w.k()])
                else:
                    S.dma("sp", w()[:].rearrange("p a b -> p (a b)"), wst[j], nm, reads=[hT.k(ls_next)],
                          writes=[w.k()])
            load_rest3(order[0][0], order[0][1], ls_next, [Wbm.k()])
            for j, (b, i) in enumerate(order):
                ls_cur = ls_next
                if j + 1 < len(order):
                    ls_next = load_tile3(*order[j + 1])
                sweep3_tile(b, i, ls_cur)
            S.flush()
    return nc


def _consts():
    t = np.arange(128)
    same = (t[:, None] // 64) == (t[None, :] // 64)
    cumf = (same & (t[:, None] <= t[None, :])).astype(np.float32)
    cumb = (same & (t[:, None] >= t[None, :])).astype(np.float32)
    maskb = (same & (t[:, None] > t[None, :])).astype(np.float32)
    ind = np.stack([(t < 64), (t >= 64)], axis=1).astype(np.float32)
    onerows = np.zeros((128, 128), np.float32)
    onerows[16] = 1.0
    onerows[48] = 1.0
    onerows[80] = 1.0
    onerows[112] = 1.0
    sel = np.zeros((2, 256), np.float32)
    sel[0, 0:128] = 1.0
    sel[1, 128:256] = 1.0
    return dict(c_ident=np.eye(128, dtype=np.float32), c_cumf=cumf, c_cumb=cumb, c_maskb=maskb,
                c_ind=ind, c_onerows=onerows, c_sel=sel)


def make_in_maps(inp):
    f = lambda a: np.ascontiguousarray(np.asarray(a, dtype=np.float32))
    shared = dict(
        norm_g=f(inp["norm_g"]).reshape(1, D),
        w_ada=f(inp["w_ada"]).reshape(D, 3 * D),
        b_ada=f(inp["b_ada"]).reshape(1, 3 * D),
        w_in=f(inp["w_in"]).reshape(D, IN_W),
        alpha_fw_w=f(inp["alpha_fw_w"]).reshape(16, 512),
        alpha_fw_b=f(inp["alpha_fw_b"]).reshape(1, 512),
        alpha_bw_w=f(inp["alpha_bw_w"]).reshape(16, 512),
        alpha_bw_b=f(inp["alpha_bw_b"]).reshape(1, 512),
        gla_norm_g=f(inp["gla_norm_g"]).reshape(1, D),
        gmlp_ln_g=f(inp["gmlp_ln_g"]).reshape(1, D),
        gmlp_ln_b=f(inp["gmlp_ln_b"]).reshape(1, D),
        wsT=f(np.transpose(np.asarray(inp["gmlp_ws"])[0], (2, 0, 1))),
        bsT=f(np.transpose(np.asarray(inp["gmlp_bs"])[0], (1, 0))),
        w_br_gla=f(inp["w_br_gla"]).reshape(D, D),
        w_br_gmlp=f(inp["w_br_gmlp"]).reshape(D, D),
        w_out=f(inp["w_out"]).reshape(D, D),
        final_g=f(inp["final_g"]).reshape(1, D),
    )
    shared.update(_consts())
    xs = f(inp["x"])
    cs = f(inp["c"])
    maps = []
    for k in range(NCORES):
        m = dict(shared)
        m["x"] = np.ascontiguousarray(xs[NSEQ * k: NSEQ * (k + 1)].reshape(NSEQ * SEQ, D))
        cc = cs[NSEQ * k: NSEQ * (k + 1)]
        m["cT"] = np.ascontiguousarray(cc.reshape(NSEQ, 8, 128).transpose(2, 1, 0))
        maps.append(m)
    return maps


def kernel(**inputs):
    nc = build_program()
    in_maps = make_in_maps(inputs)
    res = run_bass_kernel_spmd(nc, in_maps, core_ids=list(range(NCORES)))
    outs = [np.asarray(r["out"]).reshape(NSEQ, SEQ, D) for r in res.results]
    return np.concatenate(outs, axis=0).astype(np.float32)
```

```python
from contextlib import ExitStack
import math
import numpy as np
import concourse.bass as bass
import concourse.mybir as mybir
from concourse.bass_utils import run_bass_kernel_spmd

F32 = mybir.dt.float32
BF16 = mybir.dt.bfloat16
AF = mybir.ActivationFunctionType
ALU = mybir.AluOpType

NCORES = 8
D = 1024
SEQ = 2048
NSEQ = 2
NT = SEQ // 128
NCH = SEQ // 64
EPS = 1e-6
IN_W = 8224
C_Q, C_K, C_V, C_ZG, C_RAF, C_RAB, C_U, C_VS, C_ZM, C_MGLA, C_MGMLP = (
    0, 512, 1024, 2048, 3072, 3088, 3104, 4128, 5152, 6176, 7200)


class _Dummy:
    def then_inc(self, *a, **k):
        return self


class _Rec:
    def __init__(self, eng):
        self.eng = eng
        self.ns = 0.0
        self.tset = None

    @staticmethod
    def _n(ap):
        n = 1
        for d in list(ap.shape)[1:]:
            n *= int(d)
        return n

    def __getattr__(self, name):
        def f(*a, **kw):
            eng = self.eng
            if name == "matmul":
                n = self._n(kw["rhs"])
                k = 4.0 if kw["lhsT"].dtype == F32 else 1.0
                self.ns += k * max(64, n) / 2.05 + 6
            elif name == "transpose":
                self.ns += 64 / 2.05 + 20
            elif name == "activation":
                fn_ = kw.get("func")
                if fn_ in (AF.Exp, AF.Ln):
                    self.tset = "explog"
                elif fn_ in (AF.Silu, AF.Sigmoid, AF.Gelu):
                    self.tset = str(fn_)
                self.ns += (self._n(kw["in_"]) + 190) / 1.2 + (90 if not isinstance(kw.get("scale", 1.0), float) else 0) \
                    + (90 if kw.get("accum_out") is not None else 0)
            elif name == "dma_start":
                self.ns += 60
            elif eng == "pool":
                ap = kw.get("out", a[0] if a else None)
                self.ns += self._n(ap) * (3.8 if name == "tensor_copy" else 2.3) + 100
            else:
                ap = kw.get("out", a[0] if a else None)
                self.ns += (self._n(ap) + 70) / 0.96
            return _Dummy()
        return f


class _Op:
    __slots__ = ("i", "eng", "fn", "deps", "dur", "busy", "sem", "inc", "ticket", "tset")


class Sched:
    ENGS = ("pe", "act", "dve", "pool", "sp")
    EDGE_NS = 350.0
    SLACK_NS = 120.0

    def __init__(self, nc, stack):
        self.nc = nc
        self.stack = stack
        self.sem = {}
        self.total = {}
        self.isdma = {}
        self.waited = {e: {} for e in self.ENGS}
        for e in self.ENGS:
            self._newsem("E_" + e, False)
        self.ops = []
        self.lastw = {}
        self.readers = {}
        self.lastdma = {}
        self.nblocks = 0

    def _newsem(self, name, isdma):
        self.sem[name] = self.stack.enter_context(self.nc.semaphore(name))
        self.total[name] = 0
        self.isdma[name] = isdma

    def _add(self, eng, fn, reads, writes, sem, inc, dur, busy, tset=None):
        o = _Op()
        o.i = len(self.ops)
        o.eng, o.fn, o.sem, o.inc, o.dur, o.busy = eng, fn, sem, inc, dur, busy
        o.tset = tset
        deps = set()
        for k in reads:
            if k is None:
                continue
            w = self.lastw.get(k)
            if w is not None:
                deps.add(w.i)
        for k in writes:
            if k is None:
                continue
            w = self.lastw.get(k)
            if w is not None:
                deps.add(w.i)
            for r in self.readers.get(k, ()):
                deps.add(r.i)
        if self.isdma[sem]:
            p = self.lastdma.get(sem)
            if p is not None:
                deps.add(p.i)
            self.lastdma[sem] = o
        deps.discard(o.i)
        o.deps = deps
        self.ops.append(o)
        for k in reads:
            if k is not None:
                self.readers.setdefault(k, []).append(o)
        for k in writes:
            if k is not None:
                self.lastw[k] = o
                self.readers[k] = []
        return o

    def op(self, eng, fn, reads=(), writes=()):
        rec = _Rec(eng)
        fn(rec)
        self._add(eng, fn, reads, writes, "E_" + eng, 1, rec.ns, rec.ns, rec.tset)

    def dma(self, queue, out, in_, semkey, reads=(), writes=()):
        s = "D_" + str(semkey)
        if s not in self.sem:
            self._newsem(s, True)
        nbytes = 1
        for d in out.shape:
            nbytes *= int(d)
        nbytes *= 2 if out.dtype == BF16 else 4
        dur = 4000.0 + nbytes / 250.0
        busy = 1000.0 if queue == "pool" else 80.0 + nbytes / 480.0
        self._add(queue, lambda e, o=out, i=in_: e.dma_start(out=o, in_=i), reads, writes, s, 16, dur, busy)

    def _schedule(self):
        ops = self.ops
        n = len(ops)
        succ = [[] for _ in range(n)]
        npred = [0] * n
        for o in ops:
            for d in o.deps:
                succ[d].append(o.i)
                npred[o.i] += 1
        bl = [0.0] * n
        for o in reversed(ops):
            m = 0.0
            for s_ in succ[o.i]:
                if bl[s_] > m:
                    m = bl[s_]
            bl[o.i] = o.dur + m
        ready = {e: [] for e in self.ENGS}
        rt = [0.0] * n
        free = {e: 0.0 for e in self.ENGS}
        order = {e: [] for e in self.ENGS}
        for o in ops:
            if npred[o.i] == 0:
                ready[o.eng].append(o.i)
        done = 0
        end = 0.0
        cur_set = None
        TL = 1350.0
        while done < n:
            best = None
            for e in self.ENGS:
                rl = ready[e]
                if not rl:
                    continue
                t = free[e]
                if e == "act":
                    pen = [TL if (ops[i].tset is not None and ops[i].tset != cur_set) else 0.0 for i in rl]
                else:
                    pen = [0.0] * len(rl)
                ests = [max(t, rt[i]) + p_ for i, p_ in zip(rl, pen)]
                lim = min(ests) + self.SLACK_NS
                c = None
                cst = 0.0
                for i, st_ in zip(rl, ests):
                    if st_ <= lim and (c is None or bl[i] > bl[c] or (bl[i] == bl[c] and i < c)):
                        c = i
                        cst = st_
                if best is None or cst < best[0]:
                    best = (cst, e, c)
            st, e, c = best
            ready[e].remove(c)
            order[e].append(c)
            o = ops[c]
            if e == "act" and o.tset is not None:
                cur_set = o.tset
            free[e] = st + o.busy
            fin = st + o.dur
            end = max(end, fin)
            done += 1
            for s_ in succ[c]:
                lat = 0.0 if (e == "pe" and ops[s_].eng == "pe") else self.EDGE_NS
                if fin + lat > rt[s_]:
                    rt[s_] = fin + lat
                npred[s_] -= 1
                if npred[s_] == 0:
                    ready[ops[s_].eng].append(s_)
        return order, end

    def flush(self):
        nc = self.nc
        ops = self.ops
        fin = _Op()
        fin.i = len(ops)
        fin.eng, fin.fn, fin.sem, fin.inc, fin.dur, fin.busy, fin.tset = "sp", None, None, 0, 0.0, 0.0, None
        fin.deps = set(o.i for o in ops if self.isdma[o.sem])
        ops.append(fin)
        order, est = self._schedule()
        self.last_estimate_ns = est
        for e in self.ENGS:
            for i in order[e]:
                o = ops[i]
                if o.sem is None:
                    continue
                self.total[o.sem] += o.inc
                o.ticket = (o.sem, self.total[o.sem])
        progs = {}
        for e in self.ENGS:
            prog = []
            for i in order[e]:
                o = ops[i]
                need = {}
                for d in o.deps:
                    s_, v = ops[d].ticket
                    if s_ == "E_pe" and e == "pe":
                        continue
                    if need.get(s_, 0) < v:
                        need[s_] = v
                for s_, v in need.items():
                    if self.waited[e].get(s_, 0) >= v:
                        continue
                    self.waited[e][s_] = v
                    prog.append(("wait", s_, v))
                if o.fn is not None:
                    prog.append(("op", o.fn, o.sem, o.inc))
            progs[e] = prog
        self.nblocks += 1
        with nc.Block("blk%d" % self.nblocks) as block:
            for eng, deco in (("pe", block.tensor), ("act", block.scalar), ("dve", block.vector),
                              ("pool", block.gpsimd), ("sp", block.sync)):
                prog = progs[eng]

                def body(e, prog=prog):
                    for it in prog:
                        if it[0] == "wait":
                            e.wait_ge(self.sem[it[1]], it[2])
                        else:
                            it[1](e).then_inc(self.sem[it[2]], it[3])
                deco(body)
        self.ops = []
        self.lastw = {}
        self.readers = {}
        self.lastdma = {}


class Buf:
    def __init__(self, nc, stack, name, shape, dtype, n=1):
        self.name = name
        self.n = n
        self.t = [stack.enter_context(nc.sbuf_tensor("%s_%d" % (name, i), list(shape), dtype))
                  for i in range(n)]

    def __call__(self, s=0):
        return self.t[s % self.n]

    def k(self, s=0):
        return (self.name, s % self.n)


class PsRing:
    def __init__(self, nc, stack):
        self.pairs = [stack.enter_context(nc.psum_tensor("psp%d" % i, [128, 1024], F32))
                      for i in range(4)]
        self.freeb = list(range(8))

    def alloc(self, nb=1):
        fb = self.freeb
        if nb == 1:
            pick = None
            for b in fb:
                if (b ^ 1) not in fb:
                    pick = b
                    break
            if pick is None:
                if not fb:
                    raise RuntimeError("PSUM exhausted (1 bank)")
                pick = fb[0]
            fb.remove(pick)
            p, h = divmod(pick, 2)
            return self.pairs[p][:, h * 512:(h + 1) * 512], [("ps", pick)]
        for b in fb:
            if b % 2 == 0 and (b + 1) in fb:
                fb.remove(b)
                fb.remove(b + 1)
                return self.pairs[b // 2][:, :], [("ps", b), ("ps", b + 1)]
        raise RuntimeError("PSUM exhausted (2 banks), free=%s" % fb)

    def free(self, keys):
        for k in keys:
            assert k[1] not in self.freeb
            self.freeb.append(k[1])


def interleave(gens):
    gens = list(gens)
    while gens:
        for g in list(gens):
            try:
                next(g)
            except StopIteration:
                gens.remove(g)


def build_program(debug=False):
    nc = bass.Bass("TRN2", target_bir_lowering=False)

    def din(name, shape, dt=F32):
        return nc.dram_tensor(name, list(shape), dt, kind="ExternalInput").ap()

    x = din("x", [NSEQ * SEQ, D])
    cT = din("cT", [128, 8, NSEQ])
    norm_g = din("norm_g", [1, D])
    w_ada = din("w_ada", [D, 3 * D])
    b_ada = din("b_ada", [1, 3 * D])
    w_in = din("w_in", [D, IN_W])
    a_fw_w = din("alpha_fw_w", [16, 512])
    a_fw_b = din("alpha_fw_b", [1, 512])
    a_bw_w = din("alpha_bw_w", [16, 512])
    a_bw_b = din("alpha_bw_b", [1, 512])
    gla_g = din("gla_norm_g", [1, D])
    ln_g = din("gmlp_ln_g", [1, D])
    ln_b = din("gmlp_ln_b", [1, D])
    wsT_d = din("wsT", [128, 8, 128])
    bsT_d = din("bsT", [128, 8])
    w_brg = din("w_br_gla", [D, D])
    w_brm = din("w_br_gmlp", [D, D])
    w_out = din("w_out", [D, D])
    fin_g = din("final_g", [1, D])
    ident_d = din("c_ident", [128, 128])
    cumf_d = din("c_cumf", [128, 128])
    cumb_d = din("c_cumb", [128, 128])
    maskb_d = din("c_maskb", [128, 128])
    ind_d = din("c_ind", [128, 2])
    onerows_d = din("c_onerows", [128, 128])
    sel_d = din("c_sel", [2, 256])

    out = nc.dram_tensor("out", [NSEQ * SEQ, D], F32, kind="ExternalOutput").ap()
    skind = "ExternalOutput" if debug else "Internal"
    hts = nc.dram_tensor("s_hts", [NSEQ * NT, 128, D], BF16, kind=skind).ap()
    sbw = nc.dram_tensor("s_sbw", [NSEQ * NCH, 128, D], BF16, kind=skind).ap()
    ygla = nc.dram_tensor("s_ygla", [NSEQ * SEQ, D], F32, kind=skind).ap()
    vsc = nc.dram_tensor("s_v", [NSEQ * NT, 128, D], BF16, kind=skind).ap()
    wst = nc.dram_tensor("s_wst", [7, 128, 8 * D], BF16, kind="Internal").ap()
    labs = nc.dram_tensor("s_lab", [NSEQ * NT, 128, D], BF16, kind=skind).ap()

    w_in_v = w_in.rearrange("(kc p) n -> p kc n", p=128)

    with ExitStack() as top:
        S = Sched(nc, top)
        R = PsRing(nc, top)

        def B(stack, name, shape, dtype, n=1):
            return Buf(nc, stack, name, shape, dtype, n)

        dbg_n = [0]

        def dump(name, buf, slot, shape, dtype):
            if not debug:
                return
            dt_ = nc.dram_tensor("dbg_" + name, list(shape), dtype, kind="ExternalOutput").ap()
            dbg_n[0] += 1
            S.dma("sp", dt_, buf(slot)[:], "dbg%d" % dbg_n[0], reads=[buf.k(slot)])

        ident = B(top, "ident", [128, 128], BF16)
        GATE = B(top, "GATE", [128, D], F32, NSEQ)
        junk = B(top, "junk", [128, D], BF16)
        S.dma("pool", ident()[:], ident_d[:, :], "ident", writes=[ident.k()])

        def load_w(stack, name, src_cols, queue="pool"):
            w = B(stack, name, [128, 8, src_cols.shape[2]], BF16)
            S.dma(queue, w()[:], src_cols, name, writes=[w.k()])
            return w

        def bcast_row(stack, name, row_ap):
            t = B(stack, name, [128, D], F32)
            S.dma("sp", t()[:], row_ap[0:1, :].broadcast_to([128, D]), name, writes=[t.k()])
            return t

        def act(out_ap, in_ap, func, reads, writes, **kw):
            S.op("act", lambda e: e.activation(out=out_ap, in_=in_ap, func=func, **kw), reads, writes)

        def rstd_from_ss(ss_ap, n, key):
            act(ss_ap, ss_ap, AF.Ln, [key], [key], scale=1.0 / n, bias=EPS)
            act(ss_ap, ss_ap, AF.Exp, [key], [key], scale=-0.5)

        def transpose8(src, src_key, dst, dst_key, evac="act"):
            ps, pk = R.alloc(1)
            psb = ps.bitcast(BF16)

            def f(e):
                for j in range(8):
                    i = e.transpose(out=psb[:, j * 128:(j + 1) * 128], in_=src[:, j * 128:(j + 1) * 128],
                                    identity=ident()[:])
                return i
            S.op("pe", f, [src_key, ident.k()], pk)
            dflat = dst.rearrange("p a b -> p (a b)")
            if evac == "act":
                act(dflat, psb, AF.Copy, pk, [dst_key])
            else:
                S.op("dve", lambda e: e.tensor_copy(out=dflat, in_=psb), pk, [dst_key])
            R.free(pk)

        def proj_tok(hT, hT_key, W, col0, ncols):
            nb = ncols // 512
            ps, pk = R.alloc(nb)

            def f(e):
                for n in range(nb):
                    for kc in range(8):
                        i = e.matmul(ps[:, n * 512:(n + 1) * 512], lhsT=hT[:, kc, :],
                                     rhs=W()[:, kc, col0 + n * 512: col0 + (n + 1) * 512],
                                     start=(kc == 0), stop=(kc == 7))
                return i
            S.op("pe", f, [hT_key, W.k()], pk)
            return ps, pk

        def proj_feat(hT, hT_key, W, col0):
            ps, pk = R.alloc(1)

            def f(e):
                for h in range(4):
                    for kc in range(8):
                        i = e.matmul(ps[:, h * 128:(h + 1) * 128],
                                     lhsT=W()[:, kc, col0 + h * 128: col0 + (h + 1) * 128],
                                     rhs=hT[:, kc, :], start=(kc == 0), stop=(kc == 7))
                return i
            S.op("pe", f, [hT_key, W.k()], pk)
            return ps, pk

        ph12 = ExitStack()
        cumb = B(ph12, "cumb", [128, 128], BF16)
        S.dma("pool", cumb()[:], cumb_d[:, :], "cumb", writes=[cumb.k()])
        onerows = B(ph12, "onerows", [128, 128], F32)
        S.dma("sp", onerows()[:], onerows_d[:, :], "onerows", writes=[onerows.k()])
        alpha3 = {}
        Wra = {}
        for dname, aw, ab, c_ra in (("f", a_fw_w, a_fw_b, C_RAF), ("b", a_bw_w, a_bw_b, C_RAB)):
            A3 = B(ph12, "A3" + dname, [128, 512], F32)
            S.op("pool", lambda e, A3=A3: e.memset(A3()[:], 0.0), [], [A3.k()])
            for blk in range(4):
                S.dma("sp", A3()[32 * blk:32 * blk + 16, :], aw[:, :], "A3" + dname, writes=[A3.k()])
                S.dma("sp", A3()[32 * blk + 16:32 * blk + 17, :], ab[:, :], "A3" + dname, writes=[A3.k()])
            a3 = B(ph12, "alpha3" + dname, [128, 512], BF16)
            alo = B(ph12, "alo" + dname, [128, 512], BF16)
            S.op("dve", lambda e, a3=a3, A3=A3: e.tensor_copy(out=a3()[:], in_=A3()[:]), [A3.k()], [a3.k()])
            S.op("dve", lambda e, a3=a3, A3=A3, alo=alo: e.tensor_tensor(out=alo()[:], in0=A3()[:], in1=a3()[:],
                                                                        op=ALU.subtract),
                 [A3.k(), a3.k()], [alo.k()])
            for r0 in (32, 96):
                S.dma("sp", a3()[r0:r0 + 32, :], alo()[r0:r0 + 32, :], "alo" + dname, reads=[alo.k()],
                      writes=[a3.k()])
            alpha3[dname] = a3
            w = B(ph12, "Wra" + dname, [128, 8, 128], BF16)
            S.op("pool", lambda e, w=w: e.memset(w()[:], 0.0), [], [w.k()])
            for blk in range(4):
                S.dma("pool", w()[:, :, 32 * blk:32 * blk + 16], w_in_v[:, :, c_ra:c_ra + 16], "Wra" + dname,
                      writes=[w.k()])
            Wra[dname] = w

        def decay_logs(hTt, hTk, dname, RAt, RAtk, RA3, RA3k, la_, lak, lahi, lahik, lalo, lalok):
            ps_r, pk_r = R.alloc(1)

            def fr(e):
                for kc in range(8):
                    i_ = e.matmul(ps_r[:, 0:128], lhsT=Wra[dname]()[:, kc, :], rhs=hTt[:, kc, :],
                                  start=(kc == 0), stop=(kc == 7))
                return i_
            S.op("pe", fr, [hTk, Wra[dname].k()], pk_r)
            S.op("dve", lambda e: e.tensor_tensor(out=RAt[:], in0=ps_r[:, 0:128], in1=onerows()[:], op=ALU.add),
                 pk_r + [onerows.k()], [RAtk])
            R.free(pk_r)
            S.op("dve", lambda e: e.tensor_copy(out=RA3[:], in_=RAt[:]), [RAtk], [RA3k])
            S.op("dve", lambda e: e.tensor_tensor(out=RA3[64:128, :], in0=RAt[64:128, :], in1=RA3[64:128, :],
                                                  op=ALU.subtract), [RAtk, RA3k], [RA3k])
            ps_z, pk_z = R.alloc(1)
            S.op("pe", lambda e: e.matmul(ps_z, lhsT=RA3[:], rhs=alpha3[dname]()[:], start=True, stop=True),
                 [RA3k, alpha3[dname].k()], pk_z)
            act(la_[:], ps_z, AF.Exp, pk_z, [lak], scale=-1.0)
            R.free(pk_z)
            act(la_[:], la_[:], AF.Ln, [lak], [lak], bias=1.0)
            S.op("dve", lambda e: e.tensor_scalar(out=la_[:], in0=la_[:], scalar1=-1.0 / 16.0, scalar2=-1.25,
                                                  op0=ALU.mult, op1=ALU.max), [lak], [lak])
            S.op("dve", lambda e: e.tensor_copy(out=lahi[:], in_=la_[:]), [lak], [lahik])
            S.op("dve", lambda e: e.tensor_tensor(out=lalo[:], in0=la_[:], in1=lahi[:], op=ALU.subtract),
                 [lak, lahik], [lalok])

        Wqk = load_w(ph12, "Wqk", w_in_v[:, :, C_Q:C_Q + 1024])
        cumf = B(ph12, "cumf", [128, 128], BF16)
        maskf = B(ph12, "maskf", [128, 4, 128], F32)
        maskb = B(ph12, "maskb", [128, 4, 128], F32)
        gng = B(ph12, "gng", [128, D], F32)
        Wz = B(ph12, "Wz2", [128, 8, 1024], BF16)
        Wbr = B(ph12, "Wbr2", [128, 8, 1024], BF16)

        def issue_sweep2_loads(gate, part):
            if part == 1:
                S.dma("pool", Wz()[:], w_in_v[:, :, C_ZG:C_ZG + 1024], "Wz2", reads=[gate], writes=[Wz.k()])
                return
            if part == 2:
                S.dma("pool", Wbr()[:], w_brg.rearrange("(kc p) n -> p kc n", p=128), "Wbr2", reads=[gate],
                      writes=[Wbr.k()])
                return
            S.dma("pool", cumf()[:], cumf_d[:, :], "cumf", reads=[gate], writes=[cumf.k()])
            for h in range(4):
                S.dma("sp", maskf()[:, h, :], cumf_d[:, :], "maskf", writes=[maskf.k()])
                S.dma("sp", maskb()[:, h, :], maskb_d[:, :], "maskb", writes=[maskb.k()])
            S.dma("sp", gng()[:], gla_g[0:1, :].broadcast_to([128, D]), "gng", writes=[gng.k()])

        with ExitStack() as ph:
            ind = B(ph, "ind", [128, 2], BF16)
            S.dma("pool", ind()[:], ind_d[:, :], "ind", writes=[ind.k()])
            sel = B(ph, "sel", [2, 256], F32)
            S.dma("sp", sel()[:], sel_d[:, :], "sel", writes=[sel.k()])
            Wv = load_w(ph, "Wv1", w_in_v[:, :, C_V:C_V + 1024])

            G1 = B(ph, "G1", [128, D], F32, NSEQ)
            SH = B(ph, "SH", [128, D], F32, NSEQ)
            with ExitStack() as ad:
                scT = B(ad, "scT", [128, 8, NSEQ], F32)
                S.dma("sp", scT()[:], cT[:, :, :], "scT", writes=[scT.k()])
                act(scT()[:], scT()[:], AF.Silu, [scT.k()], [scT.k()])
                bada2 = B(ad, "bada2", [2, 3 * D], F32)
                S.dma("sp", bada2()[:], b_ada[0:1, :].broadcast_to([2, 3 * D]), "bada2", writes=[bada2.k()])
                ng2 = B(ad, "ng2", [2, D], F32)
                S.dma("sp", ng2()[:], norm_g[0:1, :].broadcast_to([2, D]), "ng2", writes=[ng2.k()])
                mod = B(ad, "mod", [2, 3 * D], F32)
                g1rows = B(ad, "g1rows", [2, D], F32)
                wa = B(ad, "wa", [128, 3 * D], F32, 4)
                psm = [R.alloc(1) for _ in range(6)]
                for kc in range(8):
                    S.dma("sp", wa(kc)[:], w_ada[kc * 128:(kc + 1) * 128, :], "wa%d" % (kc % 4), writes=[wa.k(kc)])

                    def f(e, kc=kc):
                        for n in range(6):
                            i = e.matmul(psm[n][0][0:2, :], lhsT=scT()[:, kc, :],
                                         rhs=wa(kc)[:, n * 512:(n + 1) * 512],
                                         start=(kc == 0), stop=(kc == 7), skip_group_check=True)
                        return i
                    S.op("pe", f, [scT.k(), wa.k(kc)], [k_ for (_, pk) in psm for k_ in pk])
                for n in range(6):
                    ps, pk = psm[n]
                    S.op("dve", lambda e, n=n, ps=ps: e.tensor_tensor(
                        out=mod()[:, n * 512:(n + 1) * 512], in0=ps[0:2, :],
                        in1=bada2()[:, n * 512:(n + 1) * 512], op=ALU.add), pk + [bada2.k()], [mod.k()])
                    R.free(pk)
                S.op("dve", lambda e: e.scalar_tensor_tensor(
                    out=g1rows()[:], in0=mod()[:, D:2 * D], scalar=1.0, in1=ng2()[:],
                    op0=ALU.add, op1=ALU.mult), [mod.k(), ng2.k()], [g1rows.k()])
                for b in range(NSEQ):
                    for (rows, rk, c0, dst) in ((g1rows, g1rows.k(), 0, G1), (mod, mod.k(), 0, SH),
                                                (mod, mod.k(), 2 * D, GATE)):
                        for hf in range(2):
                            ps, pk = R.alloc(1)
                            S.op("pe", lambda e, ps=ps, rows=rows, c0=c0, hf=hf, b=b: e.matmul(
                                ps, lhsT=sel()[0:2, b * 128:(b + 1) * 128],
                                rhs=rows()[0:2, c0 + hf * 512: c0 + (hf + 1) * 512], start=True, stop=True),
                                [sel.k(), rk], pk)
                            act(dst(b)[:, hf * 512:(hf + 1) * 512], ps, AF.Copy, pk, [dst.k(b)])
                            R.free(pk)
                S.flush()
            xt = B(ph, "xt1", [128, D], F32, 4)
            ss = B(ph, "ss1", [128, 1], F32, 3)
            xn = B(ph, "xn1", [128, D], F32, 3)
            hb = B(ph, "hb1", [128, D], BF16, 3)
            hT = B(ph, "hT1", [128, 8, 128], BF16, 3)
            RAt = B(ph, "RAt1", [128, 128], F32, 3)
            RA3 = B(ph, "RA31", [128, 128], BF16, 3)
            lahl = B(ph, "lahl1", [128, D], BF16, 3)
            la = B(ph, "la1", [128, 512], F32, 3)
            En = B(ph, "En1", [128, 512], F32, 3)
            ktb = B(ph, "ktb1", [128, 512], BF16, 3)
            vb = B(ph, "vb1", [128, D], BF16, 3)
            dec = [B(ph, "dec1_%d" % b, [128, 8], F32, 2) for b in range(NSEQ)]
            T = B(ph, "T1", [128, D], F32, NSEQ)
            Sb = B(ph, "Sb1", [128, D], BF16, 3)
            for b in range(NSEQ):
                S.op("pool", lambda e, b=b: e.memset(T(b)[:], 0.0), [], [("T1h", b, 0), ("T1h", b, 1)])
                for s in range(2):
                    S.op("pool", lambda e, b=b, s=s: e.memset(dec[b](s)[:], 0.0), [], [dec[b].k(s)])

            cnt = {"x": 0, "t": 0, "s": 0}

            def load_x(b, i, xbuf):
                s = cnt["x"]
                cnt["x"] += 1
                r0 = b * SEQ + i * 128
                S.dma("sp", xbuf(s)[:], x[r0:r0 + 128, :], "%s%d" % (xbuf.name, s % xbuf.n), writes=[xbuf.k(s)])
                return s

            def sweep1_tile(b, i, xs):
                s = cnt["t"]
                cnt["t"] += 1
                act(junk()[:], xt(xs)[:], AF.Square, [xt.k(xs)], [ss.k(s), junk.k()], accum_out=ss(s)[:, 0:1])
                rstd_from_ss(ss(s)[:, 0:1], D, ss.k(s))
                S.op("dve", lambda e: e.scalar_tensor_tensor(
                    out=xn(s)[:], in0=xt(xs)[:], scalar=ss(s)[:, 0:1], in1=G1(b)[:],
                    op0=ALU.mult, op1=ALU.mult), [xt.k(xs), ss.k(s), G1.k(b)], [xn.k(s)])
                S.op("dve", lambda e: e.tensor_tensor(out=hb(s)[:], in0=xn(s)[:], in1=SH(b)[:], op=ALU.add),
                     [xn.k(s), SH.k(b)], [hb.k(s)])
                yield
                transpose8(hb(s), hb.k(s), hT(s), hT.k(s))
                S.dma("sp", hts[b * NT + i], hT(s).rearrange("p a b -> p (a b)"), "hT1_%d" % (s % 3),
                      reads=[hT.k(s)])
                yield
                lahi_ap, lalo_ap = lahl(s)[:, 0:512], lahl(s)[:, 512:1024]
                decay_logs(hT(s), hT.k(s), "b", RAt(s), RAt.k(s), RA3(s), RA3.k(s), la(s), la.k(s),
                           lahi_ap, lahl.k(s), lalo_ap, lahl.k(s))
                S.dma("sp", labs[b * NT + i], lahl(s)[:], "lahl1_%d" % (s % 3), reads=[lahl.k(s)])
                yield
                ps_b, pk_b = R.alloc(1)

                def fbt(e):
                    e.matmul(ps_b, lhsT=cumb()[:], rhs=lahi_ap, start=True, stop=False)
                    return e.matmul(ps_b, lhsT=cumb()[:], rhs=lalo_ap, start=False, stop=True)
                S.op("pe", fbt, [cumb.k(), lahl.k(s)], pk_b)
                ps_d, pk_d = R.alloc(1)

                def fd(e):
                    for h in range(4):
                        e.matmul(ps_d[:, 2 * h:2 * h + 2], lhsT=lahi_ap[:, h * 128:(h + 1) * 128],
                                 rhs=ind()[:], start=True, stop=False)
                        i_ = e.matmul(ps_d[:, 2 * h:2 * h + 2], lhsT=lalo_ap[:, h * 128:(h + 1) * 128],
                                      rhs=ind()[:], start=False, stop=True)
                    return i_
                S.op("pe", fd, [lahl.k(s), ind.k()], pk_d)
                act(En(s)[:], ps_b, AF.Exp, pk_b, [En.k(s)], scale=-1.0)
                act(dec[b](i)[:], ps_d[:, 0:8], AF.Exp, pk_d, [dec[b].k(i)])
                R.free(pk_b)
                R.free(pk_d)
                yield
                ps_k, pk_k = proj_tok(hT(s), hT.k(s), Wqk, 512, 512)
                S.op("dve", lambda e: e.tensor_tensor(out=ktb(s)[:], in0=ps_k, in1=En(s)[:], op=ALU.mult),
                     pk_k + [En.k(s)], [ktb.k(s)])
                R.free(pk_k)
                for n_ in range(2):
                    ps_v, pk_v = R.alloc(1)

                    def fv(e, n_=n_, ps_v=ps_v):
                        for kc in range(8):
                            i_ = e.matmul(ps_v, lhsT=hT(s)[:, kc, :], rhs=Wv()[:, kc, n_ * 512:(n_ + 1) * 512],
                                          start=(kc == 0), stop=(kc == 7))
                        return i_
                    S.op("pe", fv, [hT.k(s), Wv.k()], pk_v)
                    act(vb(s)[:, n_ * 512:(n_ + 1) * 512], ps_v, AF.Copy, pk_v, [("vb1h", s % vb.n, n_)])
                    R.free(pk_v)
                vbk = [("vb1h", s % vb.n, 0), ("vb1h", s % vb.n, 1)]
                S.dma("sp", vsc[b * NT + i], vb(s)[:], "vb1_%d" % (s % 3), reads=vbk)
                if i == NT - 1:
                    dump("la_%d" % b, la, s, [128, 512], F32)
                    dump("En_%d" % b, En, s, [128, 512], F32)
                    dump("ktb_%d" % b, ktb, s, [128, 512], BF16)
                    dump("dec_%d" % b, dec[b], i, [128, 8], F32)
                yield
                for c in (1, 0):
                    n = 2 * i + c
                    dprev, dpk = (dec[b](i + 1), dec[b].k(i + 1)) if c == 1 else (dec[b](i), dec[b].k(i))
                    pcol = 0 if c == 1 else 1
                    for hp in range(2):
                        ps_kv, pk_kv = R.alloc(1)

                        def fkv(e, c=c, hp=hp, ps_kv=ps_kv):
                            for hh in range(2):
                                h = 2 * hp + hh
                                i_ = e.matmul(ps_kv[:, hh * 256:(hh + 1) * 256],
                                              lhsT=ktb(s)[64 * c:64 * c + 64, h * 128:(h + 1) * 128],
                                              rhs=vb(s)[64 * c:64 * c + 64, h * 256:(h + 1) * 256],
                                              start=True, stop=True)
                            return i_
                        S.op("pe", fkv, [ktb.k(s), vbk[hp]], pk_kv)

                        def fT(e, hp=hp, ps_kv=ps_kv, dprev=dprev, pcol=pcol):
                            for hh in range(2):
                                h = 2 * hp + hh
                                i_ = e.scalar_tensor_tensor(
                                    out=T(b)[:, h * 256:(h + 1) * 256], in0=T(b)[:, h * 256:(h + 1) * 256],
                                    scalar=dprev[:, 2 * h + pcol:2 * h + pcol + 1],
                                    in1=ps_kv[:, hh * 256:(hh + 1) * 256], op0=ALU.mult, op1=ALU.add)
                            return i_
                        S.op("dve", fT, [("T1h", b, hp), dpk] + pk_kv, [("T1h", b, hp)])
                        R.free(pk_kv)
                    if n >= 1:
                        q = cnt["s"]
                        cnt["s"] += 1

                        def fS(e, c=c, q=q):
                            for h in range(4):
                                i_ = e.activation(out=Sb(q)[:, h * 256:(h + 1) * 256],
                                                  in_=T(b)[:, h * 256:(h + 1) * 256], func=AF.Identity,
                                                  scale=dec[b](i)[:, 2 * h + c:2 * h + c + 1])
                            return i_
                        S.op("act", fS, [("T1h", b, 0), ("T1h", b, 1), dec[b].k(i)], [Sb.k(q)])
                        S.dma("sp", sbw[b * NCH + n - 1], Sb(q)[:], "Sb1_%d" % (q % 3), reads=[Sb.k(q)])
                    yield

            xs_next = [load_x(b, NT - 1, xt) for b in range(NSEQ)]
            for i in range(NT - 1, -1, -1):
                xs_cur = xs_next
                if i > 0:
                    xs_next = [load_x(b, i - 1, xt) for b in range(NSEQ)]
                interleave([sweep1_tile(b, i, xs_cur[b]) for b in range(NSEQ)])
                if i in (NT - 4, NT - 8, NT - 12):
                    issue_sweep2_loads(xt.k(xs_cur[0]), {NT - 4: 0, NT - 8: 1, NT - 12: 2}[i])
            S.flush()

        with ExitStack() as ph:
            hT = B(ph, "hT2", [128, 8, 128], BF16, 4)
            SbL = B(ph, "SbL", [128, 2, D], BF16, 4)
            RAt = B(ph, "RAt2", [128, 128], F32, 2)
            RA3 = B(ph, "RA32", [128, 128], BF16, 2)
            lafh = B(ph, "lafh", [128, 512], BF16, 2)
            lafl = B(ph, "lafl", [128, 512], BF16, 2)
            labhl = B(ph, "labhl", [128, D], BF16, 4)
            laf = B(ph, "laf2", [128, 512], F32, 2)
            Ef = B(ph, "Ef", [128, 512], F32, 2)
            Enf = B(ph, "Enf", [128, 512], F32, 2)
            Eb = B(ph, "Eb", [128, 512], F32, 2)
            Enb = B(ph, "Enb", [128, 512], F32, 2)
            qtf = B(ph, "qtf", [128, 512], BF16, 2)
            ktf = B(ph, "ktf", [128, 512], BF16, 2)
            qtb = B(ph, "qtb", [128, 512], BF16, 2)
            ktbb = B(ph, "ktbb", [128, 512], BF16, 2)
            Af = B(ph, "Af", [128, 512], BF16, 2)
            Ab = B(ph, "Ab", [128, 512], BF16, 2)
            ktok = B(ph, "ktok", [128, 512], BF16, 2)
            vb = B(ph, "vb2", [128, D], BF16, 4)
            decf = [B(ph, "decf_%d" % b, [128, 8], F32, 2) for b in range(NSEQ)]
            T = B(ph, "T2", [128, D], F32, NSEQ)
            Sf = [B(ph, "Sf_%d" % b, [128, D], BF16, 3) for b in range(NSEQ)]
            sso = B(ph, "sso", [128, 4], F32, 2)
            sz = B(ph, "sz2", [128, D], F32, 2)
            og = B(ph, "og", [128, D], BF16, 2)
            ogT = B(ph, "ogT", [128, 8, 128], BF16, 2)
            ysb = B(ph, "ysb", [128, D], F32, 2)
            for b in range(NSEQ):
                S.op("pool", lambda e, b=b: e.memset(T(b)[:], 0.0), [], [("T2h", b, 0), ("T2h", b, 1)])
                for s in range(2):
                    S.op("pool", lambda e, b=b, s=s: e.memset(decf[b](s)[:], 0.0), [], [decf[b].k(s)])
            LNQ = math.log(128.0 ** -0.5)
            cnt = {"l": 0, "t": 0}
            sfc = [0, 0]

            def load_tile2(b, i):
                s = cnt["l"]
                cnt["l"] += 1
                S.dma("sp", hT(s).rearrange("p a b -> p (a b)"), hts[b * NT + i], "hT2_%d" % (s % 4),
                      writes=[hT.k(s)])
                if 2 * i + 1 < NCH - 1:
                    S.dma("sp", SbL(s)[:], sbw[b * NCH + 2 * i: b * NCH + 2 * i + 2].rearrange("c p n -> p c n"),
                          "SbL_%d" % (s % 4), writes=[SbL.k(s)])
                else:
                    S.dma("sp", SbL(s)[:, 0, :], sbw[b * NCH + 2 * i], "SbL_%d" % (s % 4), writes=[SbL.k(s)])
                S.dma("sp", vb(s)[:], vsc[b * NT + i], "vb2_%d" % (s % 4), writes=[vb.k(s)])
                S.dma("sp", labhl(s)[:], labs[b * NT + i], "labhl_%d" % (s % 4), writes=[labhl.k(s)])
                return s

            def sweep2_tile(b, i, ls):
                s = cnt["t"]
                cnt["t"] += 1
                hTt, hTk = hT(ls), hT.k(ls)
                decay_logs(hTt, hTk, "f", RAt(s), RAt.k(s), RA3(s), RA3.k(s), laf(s), laf.k(s),
                           lafh(s), lafh.k(s), lafl(s), lafl.k(s))
                yield
                for (lh, ll, lks, cm, Epos, Eneg, dcy) in (
                        (lafh(s)[:], lafl(s)[:], [lafh.k(s), lafl.k(s)], cumf, Ef, Enf, True),
                        (labhl(ls)[:, 0:512], labhl(ls)[:, 512:1024], [labhl.k(ls)], cumb, Eb, Enb, False)):
                    ps_b, pk_b = R.alloc(1)

                    def fb(e, ps_b=ps_b, lh=lh, ll=ll, cm=cm):
                        for h in range(4):
                            e.matmul(ps_b[:, h * 128:(h + 1) * 128], lhsT=lh[:, h * 128:(h + 1) * 128],
                                     rhs=cm()[:], start=True, stop=False)
                            i_ = e.matmul(ps_b[:, h * 128:(h + 1) * 128], lhsT=ll[:, h * 128:(h + 1) * 128],
                                          rhs=cm()[:], start=False, stop=True)
                        return i_
                    S.op("pe", fb, lks + [cm.k()], pk_b)
                    act(Epos(s)[:], ps_b, AF.Exp, pk_b, [Epos.k(s)], bias=LNQ)
                    act(Eneg(s)[:], ps_b, AF.Exp, pk_b, [Eneg.k(s)], scale=-1.0)
                    if dcy:
                        act(decf[b](i)[:], ps_b[:, 63::64], AF.Exp, pk_b, [decf[b].k(i)])
                    R.free(pk_b)
                yield
                ps_q, pk_q = proj_feat(hTt, hTk, Wqk, 0)
                for (dst, E_) in ((qtf, Ef), (qtb, Eb)):
                    S.op("dve", lambda e, dst=dst, E_=E_: e.tensor_tensor(
                        out=dst(s)[:], in0=ps_q, in1=E_(s)[:], op=ALU.mult), pk_q + [E_.k(s)], [dst.k(s)])
                R.free(pk_q)
                ps_k, pk_k = proj_feat(hTt, hTk, Wqk, 512)
                for (dst, E_) in ((ktf, Enf), (ktbb, Enb)):
                    S.op("dve", lambda e, dst=dst, E_=E_: e.tensor_tensor(
                        out=dst(s)[:], in0=ps_k, in1=E_(s)[:], op=ALU.mult), pk_k + [E_.k(s)], [dst.k(s)])
                R.free(pk_k)
                yield
                for (kt_, qt_, msk, A_) in ((ktf, qtf, maskf, Af), (ktbb, qtb, maskb, Ab)):
                    ps_a, pk_a = R.alloc(1)

                    def fa(e, ps_a=ps_a, kt_=kt_, qt_=qt_):
                        for h in range(4):
                            i_ = e.matmul(ps_a[:, h * 128:(h + 1) * 128], lhsT=kt_(s)[:, h * 128:(h + 1) * 128],
                                          rhs=qt_(s)[:, h * 128:(h + 1) * 128], start=True, stop=True)
                        return i_
                    S.op("pe", fa, [kt_.k(s), qt_.k(s)], pk_a)
                    S.op("dve", lambda e, ps_a=ps_a, msk=msk, A_=A_: e.tensor_tensor(
                        out=A_(s)[:], in0=ps_a, in1=msk().rearrange("p a b -> p (a b)"), op=ALU.mult),
                        pk_a + [msk.k()], [A_.k(s)])
                    R.free(pk_a)
                ps_t, pk_t = R.alloc(1)
                pst = ps_t.bitcast(BF16)

                def ft(e):
                    for h in range(4):
                        i_ = e.transpose(out=pst[:, h * 128:(h + 1) * 128], in_=ktf(s)[:, h * 128:(h + 1) * 128],
                                         identity=ident()[:])
                    return i_
                S.op("pe", ft, [ktf.k(s), ident.k()], pk_t)
                act(ktok(s)[:], pst[:, 0:512], AF.Copy, pk_t, [ktok.k(s)])
                R.free(pk_t)
                yield
                n0 = 2 * i
                q0 = sfc[b]
                pso = []
                for hp in range(2):
                    ps_o, pk_o = R.alloc(1)
                    pso.append((ps_o, pk_o))

                    def fo1(e, hp=hp, ps_o=ps_o):
                        for hh in range(2):
                            h = 2 * hp + hh
                            oc = slice(hh * 256, (hh + 1) * 256)
                            vc = slice(h * 256, (h + 1) * 256)
                            hc = slice(h * 128, (h + 1) * 128)
                            e.matmul(ps_o[:, oc], lhsT=Af(s)[:, hc], rhs=vb(ls)[:, vc], start=(hh == 0), stop=False,
                                     skip_group_check=True)
                            i_ = e.matmul(ps_o[:, oc], lhsT=Ab(s)[:, hc], rhs=vb(ls)[:, vc], start=False, stop=False,
                                          skip_group_check=True)
                            for c in (0, 1):
                                if n0 + c < NCH - 1:
                                    i_ = e.matmul(ps_o[64 * c:64 * c + 64, oc],
                                                  lhsT=qtb(s)[:, h * 128 + 64 * c: h * 128 + 64 * c + 64],
                                                  rhs=SbL(ls)[:, c, vc], start=False, stop=False,
                                                  skip_group_check=True)
                            if n0 >= 1:
                                i_ = e.matmul(ps_o[0:64, oc], lhsT=qtf(s)[:, h * 128: h * 128 + 64],
                                              rhs=Sf[b](q0)[:, vc], start=False, stop=False, skip_group_check=True)
                        return i_
                    rd = [Af.k(s), Ab.k(s), vb.k(ls), qtb.k(s), qtf.k(s), SbL.k(ls)]
                    if n0 >= 1:
                        rd.append(Sf[b].k(q0))
                    S.op("pe", fo1, rd, pk_o)
                for c in (0, 1):
                    dprev, dpk = (decf[b](i - 1), decf[b].k(i - 1)) if c == 0 else (decf[b](i), decf[b].k(i))
                    pcol = 1 if c == 0 else 0
                    for hp in range(2):
                        ps_kv, pk_kv = R.alloc(1)

                        def fkv(e, c=c, hp=hp, ps_kv=ps_kv):
                            for hh in range(2):
                                h = 2 * hp + hh
                                i_ = e.matmul(ps_kv[:, hh * 256:(hh + 1) * 256],
                                              lhsT=ktok(s)[64 * c:64 * c + 64, h * 128:(h + 1) * 128],
                                              rhs=vb(ls)[64 * c:64 * c + 64, h * 256:(h + 1) * 256],
                                              start=True, stop=True)
                            return i_
                        S.op("pe", fkv, [ktok.k(s), vb.k(ls)], pk_kv)

                        def fT(e, hp=hp, ps_kv=ps_kv, dprev=dprev, pcol=pcol):
                            for hh in range(2):
                                h = 2 * hp + hh
                                i_ = e.scalar_tensor_tensor(
                                    out=T(b)[:, h * 256:(h + 1) * 256], in0=T(b)[:, h * 256:(h + 1) * 256],
                                    scalar=dprev[:, 2 * h + pcol:2 * h + pcol + 1],
                                    in1=ps_kv[:, hh * 256:(hh + 1) * 256], op0=ALU.mult, op1=ALU.add)
                            return i_
                        S.op("dve", fT, [("T2h", b, hp), dpk] + pk_kv, [("T2h", b, hp)])
                        R.free(pk_kv)
                    if n0 + c < NCH - 1:
                        sfc[b] += 1
                        q = sfc[b]

                        def fS(e, c=c, q=q):
                            for h in range(4):
                                i_ = e.activation(out=Sf[b](q)[:, h * 256:(h + 1) * 256],
                                                  in_=T(b)[:, h * 256:(h + 1) * 256], func=AF.Identity,
                                                  scale=decf[b](i)[:, 2 * h + c:2 * h + c + 1])
                            return i_
                        S.op("act", fS, [("T2h", b, 0), ("T2h", b, 1), decf[b].k(i)], [Sf[b].k(q)])
                    if c == 0:
                        q1 = sfc[b]
                        for hp in range(2):
                            ps_o, pk_o = pso[hp]

                            def fo2(e, q1=q1, hp=hp, ps_o=ps_o):
                                for hh in range(2):
                                    h = 2 * hp + hh
                                    i_ = e.matmul(ps_o[64:128, hh * 256:(hh + 1) * 256],
                                                  lhsT=qtf(s)[:, h * 128 + 64: h * 128 + 128],
                                                  rhs=Sf[b](q1)[:, h * 256:(h + 1) * 256], start=False, stop=True,
                                                  skip_group_check=True)
                                return i_
                            S.op("pe", fo2, [qtf.k(s), Sf[b].k(q1)] + pk_o, pk_o)
                yield
                for n_ in range(2):
                    ps_z, pk_z = R.alloc(1)

                    def fz(e, n_=n_, ps_z=ps_z):
                        for kc in range(8):
                            i_ = e.matmul(ps_z, lhsT=hTt[:, kc, :], rhs=Wz()[:, kc, n_ * 512:(n_ + 1) * 512],
                                          start=(kc == 0), stop=(kc == 7))
                        return i_
                    S.op("pe", fz, [hTk, Wz.k()], pk_z)
                    act(sz(s)[:, n_ * 512:(n_ + 1) * 512], ps_z, AF.Silu, pk_z, [sz.k(s)])
                    R.free(pk_z)
                S.op("dve", lambda e: e.tensor_tensor(out=sz(s)[:], in0=sz(s)[:], in1=gng()[:], op=ALU.mult),
                     [sz.k(s), gng.k()], [sz.k(s)])
                for hp in range(2):
                    ps_o, pk_o = pso[hp]

                    def fsq(e, hp=hp, ps_o=ps_o):
                        for hh in range(2):
                            h = 2 * hp + hh
                            i_ = e.activation(out=junk()[:, h * 256:(h + 1) * 256], in_=ps_o[:, hh * 256:(hh + 1) * 256],
                                              func=AF.Square, accum_out=sso(s)[:, h:h + 1])
                        return i_
                    S.op("act", fsq, pk_o, [sso.k(s), ("junkh", hp)])
                rstd_from_ss(sso(s)[:], 256, sso.k(s))
                for hp in range(2):
                    ps_o, pk_o = pso[hp]

                    def fog(e, hp=hp, ps_o=ps_o):
                        for hh in range(2):
                            h = 2 * hp + hh
                            i_ = e.scalar_tensor_tensor(out=og(s)[:, h * 256:(h + 1) * 256],
                                                        in0=ps_o[:, hh * 256:(hh + 1) * 256],
                                                        scalar=sso(s)[:, h:h + 1],
                                                        in1=sz(s)[:, h * 256:(h + 1) * 256],
                                                        op0=ALU.mult, op1=ALU.mult)
                        return i_
                    S.op("dve", fog, pk_o + [sso.k(s), sz.k(s)], [og.k(s)])
                    R.free(pk_o)
                yield
                transpose8(og(s), og.k(s), ogT(s), ogT.k(s))
                for n_ in range(2):
                    ps_y, pk_y = R.alloc(1)

                    def fy(e, n_=n_, ps_y=ps_y):
                        for kc in range(8):
                            i_ = e.matmul(ps_y, lhsT=ogT(s)[:, kc, :], rhs=Wbr()[:, kc, n_ * 512:(n_ + 1) * 512],
                                          start=(kc == 0), stop=(kc == 7))
                        return i_
                    S.op("pe", fy, [ogT.k(s), Wbr.k()], pk_y)
                    S.op("dve", lambda e, n_=n_, ps_y=ps_y: e.tensor_copy(out=ysb(s)[:, n_ * 512:(n_ + 1) * 512],
                                                                        in_=ps_y), pk_y, [ysb.k(s)])
                    R.free(pk_y)
                r0 = b * SEQ + i * 128
                S.dma("sp", ygla[r0:r0 + 128, :], ysb(s)[:], "ysb_%d" % (s % 2), reads=[ysb.k(s)])
                yield

            w3src = (w_in_v[:, :, C_U:C_U + 1024], w_in_v[:, :, C_VS:C_VS + 1024], w_in_v[:, :, C_ZM:C_ZM + 1024],
                     w_brm.rearrange("(kc p) n -> p kc n", p=128), w_in_v[:, :, C_MGMLP:C_MGMLP + 1024],
                     w_in_v[:, :, C_MGLA:C_MGLA + 1024], w_out.rearrange("(kc p) n -> p kc n", p=128))
            ls_next = [load_tile2(b, 0) for b in range(NSEQ)]
            for i in range(NT):
                ls_cur = ls_next
                if i + 1 < NT:
                    ls_next = [load_tile2(b, i + 1) for b in range(NSEQ)]
                interleave([sweep2_tile(b, i, ls_cur[b]) for b in range(NSEQ)])
                if 1 <= i <= 7:
                    S.dma("pool", wst[i - 1].rearrange("p (a b) -> p a b", a=8), w3src[i - 1], "wst%d" % (i - 1),
                          reads=[hT.k(ls_cur[0])])
            S.flush()

        ph12.close()

        with ExitStack() as ph:
            lng = B(ph, "lng", [128, D], F32)
            lnb = B(ph, "lnb", [128, D], F32)
            fg = B(ph, "fg", [128, D], F32)
            wsT = B(ph, "wsT", [128, 8, 128], BF16)
            S.dma("pool", wsT()[:], wsT_d[:, :, :], "wsT", writes=[wsT.k()])
            bsT = B(ph, "bsT", [128, 8], F32)
            S.dma("sp", bsT()[:], bsT_d[:, :], "bsT", writes=[bsT.k()])
            W3 = {}
            prev = None
            for j, nm in ((0, "Wu"), (1, "Wvs"), (2, "Wzm"), (4, "Wmm"), (5, "Wml"), (3, "Wbm"), (6, "Wo")):
                w = B(ph, nm, [128, 8, 1024], BF16)
                W3[nm] = w
            W3 = [W3[nm] for nm in ("Wu", "Wvs", "Wzm", "Wbm", "Wmm", "Wml", "Wo")]
            Wu, Wvs, Wzm, Wbm, Wmm, Wml, Wo = W3

            hT = B(ph, "hT3", [128, 8, 128], BF16, 2)
            xt = B(ph, "xt3", [128, D], F32, 2)
            ygl = B(ph, "ygl", [128, D], F32, 2)
            gu = B(ph, "gu", [128, D], F32)
            gv = B(ph, "gv", [128, D], F32)
            st = B(ph, "st3", [128, 16], F32, 2)
            vsn = B(ph, "vsn", [128, D], BF16)
            sz = B(ph, "sz3", [128, D], F32)
            prod = B(ph, "prod", [128, D], BF16)
            prodT = B(ph, "prodT", [128, 8, 128], BF16)
            sgm = B(ph, "sgm", [128, D], F32)
            sgl = B(ph, "sgl", [128, D], F32)
            mrgb = B(ph, "mrgb", [128, D], BF16)
            mrgT = B(ph, "mrgT", [128, 8, 128], BF16)
            ot = B(ph, "ot", [128, D], F32, 2)
            cnt = {"l": 0, "t": 0}

            def load_tile3(b, i, rest=True):
                s = cnt["l"]
                cnt["l"] += 1
                r0 = b * SEQ + i * 128
                S.dma("sp", hT(s).rearrange("p a b -> p (a b)"), hts[b * NT + i], "hT3_%d" % (s % 2),
                      writes=[hT.k(s)])
                if rest:
                    load_rest3(b, i, s, [])
                return s

            def load_rest3(b, i, s, gate):
                r0 = b * SEQ + i * 128
                S.dma("sp", xt(s)[:], x[r0:r0 + 128, :], "xt3_%d" % (s % 2), reads=gate, writes=[xt.k(s)])
                S.dma("sp", ygl(s)[:], ygla[r0:r0 + 128, :], "ygl_%d" % (s % 2), reads=gate, writes=[ygl.k(s)])

            def proj_half(hTt, hTk, W, n):
                ps, pk = R.alloc(1)

                def f(e):
                    for kc in range(8):
                        i_ = e.matmul(ps, lhsT=hTt[:, kc, :], rhs=W()[:, kc, n * 512:(n + 1) * 512],
                                      start=(kc == 0), stop=(kc == 7))
                    return i_
                S.op("pe", f, [hTk, W.k()], pk)
                return ps, pk

            def mm_half(xT, xTk, W, n):
                ps, pk = R.alloc(1)

                def f(e):
                    for kc in range(8):
                        i_ = e.matmul(ps, lhsT=xT()[:, kc, :], rhs=W()[:, kc, n * 512:(n + 1) * 512],
                                      start=(kc == 0), stop=(kc == 7))
                    return i_
                S.op("pe", f, [xTk, W.k()], pk)
                return ps, pk

            def sweep3_tile(b, i, ls):
                s = cnt["t"]
                cnt["t"] += 1
                hTt, hTk = hT(ls), hT.k(ls)
                H = (slice(0, 512), slice(512, 1024))
                for n in range(2):
                    ps, pk = proj_half(hTt, hTk, Wu, n)
                    act(gu()[:, H[n]], ps, AF.Gelu, pk, [gu.k()])
                    R.free(pk)
                for n in range(2):
                    ps, pk = proj_half(hTt, hTk, Wvs, n)
                    act(gv()[:, H[n]], ps, AF.Gelu, pk, [gv.k()])
                    R.free(pk)
                S.op("dve", lambda e: e.bn_stats(out=st(s)[:, 0:6], in_=gv()[:, 0:512]), [gv.k()], [st.k(s)])
                S.op("dve", lambda e: e.bn_stats(out=st(s)[:, 6:12], in_=gv()[:, 512:1024]), [gv.k()], [st.k(s)])
                S.op("dve", lambda e: e.bn_aggr(out=st(s)[:, 12:14], in_=st(s)[:, 0:12]), [st.k(s)], [st.k(s)])
                act(st(s)[:, 13:14], st(s)[:, 13:14], AF.Ln, [st.k(s)], [st.k(s)], bias=EPS)
                act(st(s)[:, 13:14], st(s)[:, 13:14], AF.Exp, [st.k(s)], [st.k(s)], scale=-0.5)
                S.op("dve", lambda e: e.scalar_tensor_tensor(out=gv()[:], in0=gv()[:], scalar=st(s)[:, 12:13],
                                                             in1=lng()[:], op0=ALU.subtract, op1=ALU.mult),
                     [gv.k(), st.k(s), lng.k()], [gv.k()])
                S.op("dve", lambda e: e.scalar_tensor_tensor(out=vsn()[:], in0=gv()[:], scalar=st(s)[:, 13:14],
                                                             in1=lnb()[:], op0=ALU.mult, op1=ALU.add),
                     [gv.k(), st.k(s), lnb.k()], [vsn.k()])
                for n in range(2):
                    ps, pk = proj_half(hTt, hTk, Wzm, n)
                    act(sz()[:, H[n]], ps, AF.Silu, pk, [sz.k()])
                    R.free(pk)
                S.op("dve", lambda e: e.tensor_tensor(out=gu()[:], in0=gu()[:], in1=sz()[:], op=ALU.mult),
                     [gu.k(), sz.k()], [gu.k()])
                for n in range(2):
                    ps, pk = proj_half(hTt, hTk, Wmm, n)
                    act(sgm()[:, H[n]], ps, AF.Sigmoid, pk, [sgm.k()])
                    R.free(pk)
                for n in range(2):
                    ps, pk = proj_half(hTt, hTk, Wml, n)
                    act(sgl()[:, H[n]], ps, AF.Sigmoid, pk, [sgl.k()])
                    R.free(pk)
                S.op("dve", lambda e: e.tensor_tensor(out=sgl()[:], in0=sgl()[:], in1=ygl(ls)[:], op=ALU.mult),
                     [sgl.k(), ygl.k(ls)], [sgl.k()])
                for n in range(2):
                    ps_g, pk_g = R.alloc(1)

                    def fg_(e, n=n, ps_g=ps_g):
                        for g4 in range(4):
                            g = 4 * n + g4
                            i_ = e.matmul(ps_g[:, g4 * 128:(g4 + 1) * 128], lhsT=wsT()[:, g, :],
                                          rhs=vsn()[:, g * 128:(g + 1) * 128], start=True, stop=True)
                        return i_
                    S.op("pe", fg_, [wsT.k(), vsn.k()], pk_g)

                    def fp(e, n=n, ps_g=ps_g):
                        for g4 in range(4):
                            g = 4 * n + g4
                            i_ = e.scalar_tensor_tensor(out=prod()[:, g * 128:(g + 1) * 128],
                                                        in0=ps_g[:, g4 * 128:(g4 + 1) * 128], scalar=bsT()[:, g:g + 1],
                                                        in1=gu()[:, g * 128:(g + 1) * 128], op0=ALU.add, op1=ALU.mult)
                        return i_
                    S.op("dve", fp, pk_g + [bsT.k(), gu.k()], [prod.k()])
                    R.free(pk_g)
                transpose8(prod(), prod.k(), prodT(), prodT.k())
                for n in range(2):
                    ps_y, pk_y = mm_half(prodT, prodT.k(), Wbm, n)
                    S.op("dve", lambda e, n=n, ps_y=ps_y: e.tensor_tensor(
                        out=sgm()[:, H[n]], in0=ps_y, in1=sgm()[:, H[n]], op=ALU.mult), pk_y + [sgm.k()], [sgm.k()])
                    R.free(pk_y)
                S.op("dve", lambda e: e.tensor_tensor(out=mrgb()[:], in0=sgm()[:], in1=sgl()[:], op=ALU.add),
                     [sgm.k(), sgl.k()], [mrgb.k()])
                transpose8(mrgb(), mrgb.k(), mrgT(), mrgT.k())
                for n in range(2):
                    ps_o, pk_o = mm_half(mrgT, mrgT.k(), Wo, n)
                    S.op("dve", lambda e, n=n, ps_o=ps_o: e.tensor_tensor(
                        out=sgm()[:, H[n]], in0=ps_o, in1=GATE(b)[:, H[n]], op=ALU.mult),
                        pk_o + [GATE.k(b)], [sgm.k()])
                    R.free(pk_o)
                S.op("dve", lambda e: e.tensor_tensor(out=sgm()[:], in0=sgm()[:], in1=xt(ls)[:], op=ALU.add),
                     [sgm.k(), xt.k(ls)], [sgm.k()])
                act(junk()[:], sgm()[:], AF.Square, [sgm.k()], [st.k(s), junk.k()], accum_out=st(s)[:, 14:15])
                rstd_from_ss(st(s)[:, 14:15], D, st.k(s))
                S.op("dve", lambda e: e.scalar_tensor_tensor(out=ot(s)[:], in0=sgm()[:], scalar=st(s)[:, 14:15],
                                                             in1=fg()[:], op0=ALU.mult, op1=ALU.mult),
                     [sgm.k(), st.k(s), fg.k()], [ot.k(s)])
                r0 = b * SEQ + i * 128
                S.dma("sp", out[r0:r0 + 128, :], ot(s)[:], "ot_%d" % (s % 2), reads=[ot.k(s)])

            order = [(b, i) for b in range(NSEQ) for i in range(NT)]
            ls_next = load_tile3(*order[0], rest=False)
            for q_, (j, nm) in enumerate(((0, "Wu"), (1, "Wvs"), (2, "Wzm"), (4, "Wmm"), (5, "Wml"), (3, "Wbm"), (6, "Wo"))):
                w = dict(Wu=Wu, Wvs=Wvs, Wzm=Wzm, Wbm=Wbm, Wmm=Wmm, Wml=Wml, Wo=Wo)[nm]
                if q_ % 2 == 0:
                    S.dma("act", w()[:].rearrange("p a b -> p (a b)"), wst[j], nm, writes=[w.k()])
                else:
                    S.dma("sp", w()[:].rearrange("p a b -> p (a b)"), wst[j], nm, reads=[hT.k(ls_next)],
                          writes=[w.k()])
            for t_, row_, gate_ in ((lng, ln_g, Wvs.k()), (lnb, ln_b, Wvs.k()), (fg, fin_g, Wbm.k())):
                S.dma("sp", t_()[:], row_[0:1, :].broadcast_to([128, D]), t_.name, reads=[gate_], writes=[t_.k()])
            load_rest3(order[0][0], order[0][1], ls_next, [Wbm.k()])
            for j, (b, i) in enumerate(order):
                ls_cur = ls_next
                if j + 1 < len(order):
                    ls_next = load_tile3(*order[j + 1])
                sweep3_tile(b, i, ls_cur)
            S.flush()
    return nc


def _consts():
    t = np.arange(128)
    same = (t[:, None] // 64) == (t[None, :] // 64)
    cumf = (same & (t[:, None] <= t[None, :])).astype(np.float32)
    cumb = (same & (t[:, None] >= t[None, :])).astype(np.float32)
    maskb = (same & (t[:, None] > t[None, :])).astype(np.float32)
    ind = np.stack([(t < 64), (t >= 64)], axis=1).astype(np.float32)
    onerows = np.zeros((128, 128), np.float32)
    onerows[16] = 1.0
    onerows[48] = 1.0
    onerows[80] = 1.0
    onerows[112] = 1.0
    sel = np.zeros((2, 256), np.float32)
    sel[0, 0:128] = 1.0
    sel[1, 128:256] = 1.0
    return dict(c_ident=np.eye(128, dtype=np.float32), c_cumf=cumf, c_cumb=cumb, c_maskb=maskb,
                c_ind=ind, c_onerows=onerows, c_sel=sel)


def make_in_maps(inp):
    f = lambda a: np.ascontiguousarray(np.asarray(a, dtype=np.float32))
    shared = dict(
        norm_g=f(inp["norm_g"]).reshape(1, D),
        w_ada=f(inp["w_ada"]).reshape(D, 3 * D),
        b_ada=f(inp["b_ada"]).reshape(1, 3 * D),
        w_in=f(inp["w_in"]).reshape(D, IN_W),
        alpha_fw_w=f(inp["alpha_fw_w"]).reshape(16, 512),
        alpha_fw_b=f(inp["alpha_fw_b"]).reshape(1, 512),
        alpha_bw_w=f(inp["alpha_bw_w"]).reshape(16, 512),
        alpha_bw_b=f(inp["alpha_bw_b"]).reshape(1, 512),
        gla_norm_g=f(inp["gla_norm_g"]).reshape(1, D),
        gmlp_ln_g=f(inp["gmlp_ln_g"]).reshape(1, D),
        gmlp_ln_b=f(inp["gmlp_ln_b"]).reshape(1, D),
        wsT=f(np.transpose(np.asarray(inp["gmlp_ws"])[0], (2, 0, 1))),
        bsT=f(np.transpose(np.asarray(inp["gmlp_bs"])[0], (1, 0))),
        w_br_gla=f(inp["w_br_gla"]).reshape(D, D),
        w_br_gmlp=f(inp["w_br_gmlp"]).reshape(D, D),
        w_out=f(inp["w_out"]).reshape(D, D),
        final_g=f(inp["final_g"]).reshape(1, D),
    )
    shared.update(_consts())
    xs = f(inp["x"])
    cs = f(inp["c"])
    maps = []
    for k in range(NCORES):
        m = dict(shared)
        m["x"] = np.ascontiguousarray(xs[NSEQ * k: NSEQ * (k + 1)].reshape(NSEQ * SEQ, D))
        cc = cs[NSEQ * k: NSEQ * (k + 1)]
        m["cT"] = np.ascontiguousarray(cc.reshape(NSEQ, 8, 128).transpose(2, 1, 0))
        maps.append(m)
    return maps


def kernel(**inputs):
    nc = build_program()
    in_maps = make_in_maps(inputs)
    res = run_bass_kernel_spmd(nc, in_maps, core_ids=list(range(NCORES)))
    outs = [np.asarray(r["out"]).reshape(NSEQ, SEQ, D) for r in res.results]
    return np.concatenate(outs, axis=0).astype(np.float32)
```

```python
from contextlib import ExitStack
import math
import numpy as np
import concourse.bass as bass
import concourse.mybir as mybir
from concourse.bass_utils import run_bass_kernel_spmd

F32 = mybir.dt.float32
BF16 = mybir.dt.bfloat16
AF = mybir.ActivationFunctionType
ALU = mybir.AluOpType

NCORES = 8
D = 1024
SEQ = 2048
NSEQ = 2
NT = SEQ // 128
NCH = SEQ // 64
EPS = 1e-6
IN_W = 8224
C_Q, C_K, C_V, C_ZG, C_RAF, C_RAB, C_U, C_VS, C_ZM, C_MGLA, C_MGMLP = (
    0, 512, 1024, 2048, 3072, 3088, 3104, 4128, 5152, 6176, 7200)


class _Dummy:
    def then_inc(self, *a, **k):
        return self


class _Rec:
    def __init__(self, eng):
        self.eng = eng
        self.ns = 0.0
        self.tset = None

    @staticmethod
    def _n(ap):
        n = 1
        for d in list(ap.shape)[1:]:
            n *= int(d)
        return n

    def __getattr__(self, name):
        def f(*a, **kw):
            eng = self.eng
            if name == "matmul":
                n = self._n(kw["rhs"])
                k = 4.0 if kw["lhsT"].dtype == F32 else 1.0
                self.ns += k * max(64, n) / 2.05 + 6
            elif name == "transpose":
                self.ns += 64 / 2.05 + 20
            elif name == "activation":
                fn_ = kw.get("func")
                if fn_ in (AF.Exp, AF.Ln):
                    self.tset = "explog"
                elif fn_ in (AF.Silu, AF.Sigmoid, AF.Gelu):
                    self.tset = str(fn_)
                self.ns += (self._n(kw["in_"]) + 190) / 1.2 + (90 if not isinstance(kw.get("scale", 1.0), float) else 0) \
                    + (90 if kw.get("accum_out") is not None else 0)
            elif name == "dma_start":
                self.ns += 60
            elif eng == "pool":
                ap = kw.get("out", a[0] if a else None)
                self.ns += self._n(ap) * (3.8 if name == "tensor_copy" else 2.3) + 100
            else:
                ap = kw.get("out", a[0] if a else None)
                self.ns += (self._n(ap) + 70) / 0.96
            return _Dummy()
        return f


class _Op:
    __slots__ = ("i", "eng", "fn", "deps", "dur", "busy", "sem", "inc", "ticket", "tset")


class Sched:
    ENGS = ("pe", "act", "dve", "pool", "sp")
    EDGE_NS = 350.0
    SLACK_NS = 120.0

    def __init__(self, nc, stack):
        self.nc = nc
        self.stack = stack
        self.sem = {}
        self.total = {}
        self.isdma = {}
        self.waited = {e: {} for e in self.ENGS}
        for e in self.ENGS:
            self._newsem("E_" + e, False)
        self.ops = []
        self.lastw = {}
        self.readers = {}
        self.lastdma = {}
        self.nblocks = 0

    def _newsem(self, name, isdma):
        self.sem[name] = self.stack.enter_context(self.nc.semaphore(name))
        self.total[name] = 0
        self.isdma[name] = isdma

    def _add(self, eng, fn, reads, writes, sem, inc, dur, busy, tset=None):
        o = _Op()
        o.i = len(self.ops)
        o.eng, o.fn, o.sem, o.inc, o.dur, o.busy = eng, fn, sem, inc, dur, busy
        o.tset = tset
        deps = set()
        for k in reads:
            if k is None:
                continue
            w = self.lastw.get(k)
            if w is not None:
                deps.add(w.i)
        for k in writes:
            if k is None:
                continue
            w = self.lastw.get(k)
            if w is not None:
                deps.add(w.i)
            for r in self.readers.get(k, ()):
                deps.add(r.i)
        if self.isdma[sem]:
            p = self.lastdma.get(sem)
            if p is not None:
                deps.add(p.i)
            self.lastdma[sem] = o
        deps.discard(o.i)
        o.deps = deps
        self.ops.append(o)
        for k in reads:
            if k is not None:
                self.readers.setdefault(k, []).append(o)
        for k in writes:
            if k is not None:
                self.lastw[k] = o
                self.readers[k] = []
        return o

    def op(self, eng, fn, reads=(), writes=()):
        rec = _Rec(eng)
        fn(rec)
        self._add(eng, fn, reads, writes, "E_" + eng, 1, rec.ns, rec.ns, rec.tset)

    def dma(self, queue, out, in_, semkey, reads=(), writes=()):
        s = "D_" + str(semkey)
        if s not in self.sem:
            self._newsem(s, True)
        nbytes = 1
        for d in out.shape:
            nbytes *= int(d)
        nbytes *= 2 if out.dtype == BF16 else 4
        dur = 4000.0 + nbytes / 250.0
        busy = 1000.0 if queue == "pool" else 80.0 + nbytes / 480.0
        self._add(queue, lambda e, o=out, i=in_: e.dma_start(out=o, in_=i), reads, writes, s, 16, dur, busy)

    def _schedule(self):
        ops = self.ops
        n = len(ops)
        succ = [[] for _ in range(n)]
        npred = [0] * n
        for o in ops:
            for d in o.deps:
                succ[d].append(o.i)
                npred[o.i] += 1
        bl = [0.0] * n
        for o in reversed(ops):
            m = 0.0
            for s_ in succ[o.i]:
                if bl[s_] > m:
                    m = bl[s_]
            bl[o.i] = o.dur + m
        ready = {e: [] for e in self.ENGS}
        rt = [0.0] * n
        free = {e: 0.0 for e in self.ENGS}
        order = {e: [] for e in self.ENGS}
        for o in ops:
            if npred[o.i] == 0:
                ready[o.eng].append(o.i)
        done = 0
        end = 0.0
        cur_set = None
        TL = 1350.0
        while done < n:
            best = None
            for e in self.ENGS:
                rl = ready[e]
                if not rl:
                    continue
                t = free[e]
                if e == "act":
                    pen = [TL if (ops[i].tset is not None and ops[i].tset != cur_set) else 0.0 for i in rl]
                else:
                    pen = [0.0] * len(rl)
                ests = [max(t, rt[i]) + p_ for i, p_ in zip(rl, pen)]
                lim = min(ests) + self.SLACK_NS
                c = None
                cst = 0.0
                for i, st_ in zip(rl, ests):
                    if st_ <= lim and (c is None or bl[i] > bl[c] or (bl[i] == bl[c] and i < c)):
                        c = i
                        cst = st_
                if best is None or cst < best[0]:
                    best = (cst, e, c)
            st, e, c = best
            ready[e].remove(c)
            order[e].append(c)
            o = ops[c]
            if e == "act" and o.tset is not None:
                cur_set = o.tset
            free[e] = st + o.busy
            fin = st + o.dur
            end = max(end, fin)
            done += 1
            for s_ in succ[c]:
                lat = 0.0 if (e == "pe" and ops[s_].eng == "pe") else self.EDGE_NS
                if fin + lat > rt[s_]:
                    rt[s_] = fin + lat
                npred[s_] -= 1
                if npred[s_] == 0:
                    ready[ops[s_].eng].append(s_)
        return order, end

    def flush(self):
        nc = self.nc
        ops = self.ops
        fin = _Op()
        fin.i = len(ops)
        fin.eng, fin.fn, fin.sem, fin.inc, fin.dur, fin.busy, fin.tset = "sp", None, None, 0, 0.0, 0.0, None
        fin.deps = set(o.i for o in ops if self.isdma[o.sem])
        ops.append(fin)
        order, est = self._schedule()
        self.last_estimate_ns = est
        for e in self.ENGS:
            for i in order[e]:
                o = ops[i]
                if o.sem is None:
                    continue
                self.total[o.sem] += o.inc
                o.ticket = (o.sem, self.total[o.sem])
        progs = {}
        for e in self.ENGS:
            prog = []
            for i in order[e]:
                o = ops[i]
                need = {}
                for d in o.deps:
                    s_, v = ops[d].ticket
                    if s_ == "E_pe" and e == "pe":
                        continue
                    if need.get(s_, 0) < v:
                        need[s_] = v
                for s_, v in need.items():
                    if self.waited[e].get(s_, 0) >= v:
                        continue
                    self.waited[e][s_] = v
                    prog.append(("wait", s_, v))
                if o.fn is not None:
                    prog.append(("op", o.fn, o.sem, o.inc))
            progs[e] = prog
        self.nblocks += 1
        with nc.Block("blk%d" % self.nblocks) as block:
            for eng, deco in (("pe", block.tensor), ("act", block.scalar), ("dve", block.vector),
                              ("pool", block.gpsimd), ("sp", block.sync)):
                prog = progs[eng]

                def body(e, prog=prog):
                    for it in prog:
                        if it[0] == "wait":
                            e.wait_ge(self.sem[it[1]], it[2])
                        else:
                            it[1](e).then_inc(self.sem[it[2]], it[3])
                deco(body)
        self.ops = []
        self.lastw = {}
        self.readers = {}
        self.lastdma = {}


class Buf:
    def __init__(self, nc, stack, name, shape, dtype, n=1):
        self.name = name
        self.n = n
        self.t = [stack.enter_context(nc.sbuf_tensor("%s_%d" % (name, i), list(shape), dtype))
                  for i in range(n)]

    def __call__(self, s=0):
        return self.t[s % self.n]

    def k(self, s=0):
        return (self.name, s % self.n)


class PsRing:
    def __init__(self, nc, stack):
        self.pairs = [stack.enter_context(nc.psum_tensor("psp%d" % i, [128, 1024], F32))
                      for i in range(4)]
        self.freeb = list(range(8))

    def alloc(self, nb=1):
        fb = self.freeb
        if nb == 1:
            if not fb:
                raise RuntimeError("PSUM exhausted (1 bank)")
            pick = fb[0]
            fb.remove(pick)
            p, h = divmod(pick, 2)
            return self.pairs[p][:, h * 512:(h + 1) * 512], [("ps", pick)]
        for b in fb:
            if b % 2 == 0 and (b + 1) in fb:
                fb.remove(b)
                fb.remove(b + 1)
                return self.pairs[b // 2][:, :], [("ps", b), ("ps", b + 1)]
        raise RuntimeError("PSUM exhausted (2 banks), free=%s" % fb)

    def free(self, keys):
        for k in keys:
            assert k[1] not in self.freeb
            self.freeb.append(k[1])


def interleave(gens):
    gens = list(gens)
    while gens:
        for g in list(gens):
            try:
                next(g)
            except StopIteration:
                gens.remove(g)


def build_program(debug=False):
    nc = bass.Bass("TRN2", target_bir_lowering=False)

    def din(name, shape, dt=F32):
        return nc.dram_tensor(name, list(shape), dt, kind="ExternalInput").ap()

    x = din("x", [NSEQ * SEQ, D])
    cT = din("cT", [128, 8, NSEQ])
    norm_g = din("norm_g", [1, D])
    w_ada = din("w_ada", [D, 3 * D])
    b_ada = din("b_ada", [1, 3 * D])
    w_in = din("w_in", [D, IN_W])
    a_fw_w = din("alpha_fw_w", [16, 512])
    a_fw_b = din("alpha_fw_b", [1, 512])
    a_bw_w = din("alpha_bw_w", [16, 512])
    a_bw_b = din("alpha_bw_b", [1, 512])
    gla_g = din("gla_norm_g", [1, D])
    ln_g = din("gmlp_ln_g", [1, D])
    ln_b = din("gmlp_ln_b", [1, D])
    wsT_d = din("wsT", [128, 8, 128])
    bsT_d = din("bsT", [128, 8])
    w_brg = din("w_br_gla", [D, D])
    w_brm = din("w_br_gmlp", [D, D])
    w_out = din("w_out", [D, D])
    fin_g = din("final_g", [1, D])
    ident_d = din("c_ident", [128, 128])
    cumf_d = din("c_cumf", [128, 128])
    cumb_d = din("c_cumb", [128, 128])
    maskb_d = din("c_maskb", [128, 128])
    ind_d = din("c_ind", [128, 2])
    onerows_d = din("c_onerows", [128, 128])
    sel_d = din("c_sel", [2, 256])

    out = nc.dram_tensor("out", [NSEQ * SEQ, D], F32, kind="ExternalOutput").ap()
    skind = "ExternalOutput" if debug else "Internal"
    hts = nc.dram_tensor("s_hts", [NSEQ * NT, 128, D], BF16, kind=skind).ap()
    sbw = nc.dram_tensor("s_sbw", [NSEQ * NCH, 128, D], BF16, kind=skind).ap()
    ygla = nc.dram_tensor("s_ygla", [NSEQ * SEQ, D], F32, kind=skind).ap()
    vsc = nc.dram_tensor("s_v", [NSEQ * NT, 128, D], BF16, kind=skind).ap()
    wst = nc.dram_tensor("s_wst", [7, 128, 8 * D], BF16, kind="Internal").ap()
    labs = nc.dram_tensor("s_lab", [NSEQ * NT, 128, D], BF16, kind=skind).ap()

    w_in_v = w_in.rearrange("(kc p) n -> p kc n", p=128)

    with ExitStack() as top:
        S = Sched(nc, top)
        R = PsRing(nc, top)

        def B(stack, name, shape, dtype, n=1):
            return Buf(nc, stack, name, shape, dtype, n)

        dbg_n = [0]

        def dump(name, buf, slot, shape, dtype):
            if not debug:
                return
            dt_ = nc.dram_tensor("dbg_" + name, list(shape), dtype, kind="ExternalOutput").ap()
            dbg_n[0] += 1
            S.dma("sp", dt_, buf(slot)[:], "dbg%d" % dbg_n[0], reads=[buf.k(slot)])

        ident = B(top, "ident", [128, 128], BF16)
        GATE = B(top, "GATE", [128, D], F32, NSEQ)
        junk = B(top, "junk", [128, D], BF16)
        S.dma("pool", ident()[:], ident_d[:, :], "ident", writes=[ident.k()])

        def load_w(stack, name, src_cols, queue="pool"):
            w = B(stack, name, [128, 8, src_cols.shape[2]], BF16)
            S.dma(queue, w()[:], src_cols, name, writes=[w.k()])
            return w

        def bcast_row(stack, name, row_ap):
            t = B(stack, name, [128, D], F32)
            S.dma("sp", t()[:], row_ap[0:1, :].broadcast_to([128, D]), name, writes=[t.k()])
            return t

        def act(out_ap, in_ap, func, reads, writes, **kw):
            S.op("act", lambda e: e.activation(out=out_ap, in_=in_ap, func=func, **kw), reads, writes)

        def rstd_from_ss(ss_ap, n, key):
            act(ss_ap, ss_ap, AF.Ln, [key], [key], scale=1.0 / n, bias=EPS)
            act(ss_ap, ss_ap, AF.Exp, [key], [key], scale=-0.5)

        def transpose8(src, src_key, dst, dst_key, evac="act"):
            ps, pk = R.alloc(1)
            psb = ps.bitcast(BF16)

            def f(e):
                for j in range(8):
                    i = e.transpose(out=psb[:, j * 128:(j + 1) * 128], in_=src[:, j * 128:(j + 1) * 128],
                                    identity=ident()[:])
                return i
            S.op("pe", f, [src_key, ident.k()], pk)
            dflat = dst.rearrange("p a b -> p (a b)")
            if evac == "act":
                act(dflat, psb, AF.Copy, pk, [dst_key])
            else:
                S.op("dve", lambda e: e.tensor_copy(out=dflat, in_=psb), pk, [dst_key])
            R.free(pk)

        def proj_tok(hT, hT_key, W, col0, ncols):
            nb = ncols // 512
            ps, pk = R.alloc(nb)

            def f(e):
                for n in range(nb):
                    for kc in range(8):
                        i = e.matmul(ps[:, n * 512:(n + 1) * 512], lhsT=hT[:, kc, :],
                                     rhs=W()[:, kc, col0 + n * 512: col0 + (n + 1) * 512],
                                     start=(kc == 0), stop=(kc == 7))
                return i
            S.op("pe", f, [hT_key, W.k()], pk)
            return ps, pk

        def proj_feat(hT, hT_key, W, col0):
            ps, pk = R.alloc(1)

            def f(e):
                for h in range(4):
                    for kc in range(8):
                        i = e.matmul(ps[:, h * 128:(h + 1) * 128],
                                     lhsT=W()[:, kc, col0 + h * 128: col0 + (h + 1) * 128],
                                     rhs=hT[:, kc, :], start=(kc == 0), stop=(kc == 7))
                return i
            S.op("pe", f, [hT_key, W.k()], pk)
            return ps, pk

        ph12 = ExitStack()
        cumb = B(ph12, "cumb", [128, 128], BF16)
        S.dma("pool", cumb()[:], cumb_d[:, :], "cumb", writes=[cumb.k()])
        onerows = B(ph12, "onerows", [128, 128], F32)
        S.dma("sp", onerows()[:], onerows_d[:, :], "onerows", writes=[onerows.k()])
        alpha3 = {}
        Wra = {}
        for dname, aw, ab, c_ra in (("f", a_fw_w, a_fw_b, C_RAF), ("b", a_bw_w, a_bw_b, C_RAB)):
            A3 = B(ph12, "A3" + dname, [128, 512], F32)
            S.op("pool", lambda e, A3=A3: e.memset(A3()[:], 0.0), [], [A3.k()])
            for blk in range(4):
                S.dma("sp", A3()[32 * blk:32 * blk + 16, :], aw[:, :], "A3" + dname, writes=[A3.k()])
                S.dma("sp", A3()[32 * blk + 16:32 * blk + 17, :], ab[:, :], "A3" + dname, writes=[A3.k()])
            a3 = B(ph12, "alpha3" + dname, [128, 512], BF16)
            alo = B(ph12, "alo" + dname, [128, 512], BF16)
            S.op("dve", lambda e, a3=a3, A3=A3: e.tensor_copy(out=a3()[:], in_=A3()[:]), [A3.k()], [a3.k()])
            S.op("dve", lambda e, a3=a3, A3=A3, alo=alo: e.tensor_tensor(out=alo()[:], in0=A3()[:], in1=a3()[:],
                                                                        op=ALU.subtract),
                 [A3.k(), a3.k()], [alo.k()])
            for r0 in (32, 96):
                S.dma("sp", a3()[r0:r0 + 32, :], alo()[r0:r0 + 32, :], "alo" + dname, reads=[alo.k()],
                      writes=[a3.k()])
            alpha3[dname] = a3
            w = B(ph12, "Wra" + dname, [128, 8, 128], BF16)
            S.op("pool", lambda e, w=w: e.memset(w()[:], 0.0), [], [w.k()])
            for blk in range(4):
                S.dma("pool", w()[:, :, 32 * blk:32 * blk + 16], w_in_v[:, :, c_ra:c_ra + 16], "Wra" + dname,
                      writes=[w.k()])
            Wra[dname] = w

        def decay_logs(hTt, hTk, dname, RAt, RAtk, RA3, RA3k, la_, lak, lahi, lahik, lalo, lalok):
            ps_r, pk_r = R.alloc(1)

            def fr(e):
                for kc in range(8):
                    i_ = e.matmul(ps_r[:, 0:128], lhsT=Wra[dname]()[:, kc, :], rhs=hTt[:, kc, :],
                                  start=(kc == 0), stop=(kc == 7))
                return i_
            S.op("pe", fr, [hTk, Wra[dname].k()], pk_r)
            S.op("dve", lambda e: e.tensor_tensor(out=RAt[:], in0=ps_r[:, 0:128], in1=onerows()[:], op=ALU.add),
                 pk_r + [onerows.k()], [RAtk])
            R.free(pk_r)
            S.op("dve", lambda e: e.tensor_copy(out=RA3[:], in_=RAt[:]), [RAtk], [RA3k])
            S.op("dve", lambda e: e.tensor_tensor(out=RA3[64:128, :], in0=RAt[64:128, :], in1=RA3[64:128, :],
                                                  op=ALU.subtract), [RAtk, RA3k], [RA3k])
            ps_z, pk_z = R.alloc(1)
            S.op("pe", lambda e: e.matmul(ps_z, lhsT=RA3[:], rhs=alpha3[dname]()[:], start=True, stop=True),
                 [RA3k, alpha3[dname].k()], pk_z)
            act(la_[:], ps_z, AF.Exp, pk_z, [lak], scale=-1.0)
            R.free(pk_z)
            act(la_[:], la_[:], AF.Ln, [lak], [lak], bias=1.0)
            S.op("dve", lambda e: e.tensor_scalar(out=la_[:], in0=la_[:], scalar1=-1.0 / 16.0, scalar2=-1.25,
                                                  op0=ALU.mult, op1=ALU.max), [lak], [lak])
            S.op("dve", lambda e: e.tensor_copy(out=lahi[:], in_=la_[:]), [lak], [lahik])
            S.op("dve", lambda e: e.tensor_tensor(out=lalo[:], in0=la_[:], in1=lahi[:], op=ALU.subtract),
                 [lak, lahik], [lalok])

        Wqk = load_w(ph12, "Wqk", w_in_v[:, :, C_Q:C_Q + 1024])
        cumf = B(ph12, "cumf", [128, 128], BF16)
        maskf = B(ph12, "maskf", [128, 4, 128], F32)
        maskb = B(ph12, "maskb", [128, 4, 128], F32)
        gng = B(ph12, "gng", [128, D], F32)
        Wz = B(ph12, "Wz2", [128, 8, 1024], BF16)
        Wbr = B(ph12, "Wbr2", [128, 8, 1024], BF16)

        def issue_sweep2_loads(gate, part):
            if part == 1:
                S.dma("pool", Wz()[:], w_in_v[:, :, C_ZG:C_ZG + 1024], "Wz2", reads=[gate], writes=[Wz.k()])
                return
            if part == 2:
                S.dma("pool", Wbr()[:], w_brg.rearrange("(kc p) n -> p kc n", p=128), "Wbr2", reads=[gate],
                      writes=[Wbr.k()])
                return
            S.dma("pool", cumf()[:], cumf_d[:, :], "cumf", reads=[gate], writes=[cumf.k()])
            for h in range(4):
                S.dma("sp", maskf()[:, h, :], cumf_d[:, :], "maskf", writes=[maskf.k()])
                S.dma("sp", maskb()[:, h, :], maskb_d[:, :], "maskb", writes=[maskb.k()])
            S.dma("sp", gng()[:], gla_g[0:1, :].broadcast_to([128, D]), "gng", writes=[gng.k()])

        with ExitStack() as ph:
            ind = B(ph, "ind", [128, 2], BF16)
            S.dma("pool", ind()[:], ind_d[:, :], "ind", writes=[ind.k()])
            sel = B(ph, "sel", [2, 256], F32)
            S.dma("sp", sel()[:], sel_d[:, :], "sel", writes=[sel.k()])
            Wv = load_w(ph, "Wv1", w_in_v[:, :, C_V:C_V + 1024])

            G1 = B(ph, "G1", [128, D], F32, NSEQ)
            SH = B(ph, "SH", [128, D], F32, NSEQ)
            xt = B(ph, "xt1", [128, D], F32, 6)
            for b in range(NSEQ):
                r0_ = b * SEQ + (NT - 1) * 128
                S.dma("sp", xt(b)[:], x[r0_:r0_ + 128, :], "xt1%d" % b, writes=[xt.k(b)])
            with ExitStack() as ad:
                scT = B(ad, "scT", [128, 8, NSEQ], F32)
                S.dma("sp", scT()[:], cT[:, :, :], "scT", writes=[scT.k()])
                act(scT()[:], scT()[:], AF.Silu, [scT.k()], [scT.k()])
                bada2 = B(ad, "bada2", [2, 3 * D], F32)
                S.dma("sp", bada2()[:], b_ada[0:1, :].broadcast_to([2, 3 * D]), "bada2", writes=[bada2.k()])
                ng2 = B(ad, "ng2", [2, D], F32)
                S.dma("sp", ng2()[:], norm_g[0:1, :].broadcast_to([2, D]), "ng2", writes=[ng2.k()])
                mod = B(ad, "mod", [2, 3 * D], F32)
                g1rows = B(ad, "g1rows", [2, D], F32)
                wa = B(ad, "wa", [128, 3 * D], F32, 3)
                psm = [R.alloc(1) for _ in range(6)]
                for kc in range(8):
                    S.dma("sp", wa(kc)[:], w_ada[kc * 128:(kc + 1) * 128, :], "wa%d" % (kc % 3), writes=[wa.k(kc)])

                    def f(e, kc=kc):
                        for n in range(6):
                            i = e.matmul(psm[n][0][0:2, :], lhsT=scT()[:, kc, :],
                                         rhs=wa(kc)[:, n * 512:(n + 1) * 512],
                                         start=(kc == 0), stop=(kc == 7), skip_group_check=True)
                        return i
                    S.op("pe", f, [scT.k(), wa.k(kc)], [k_ for (_, pk) in psm for k_ in pk])
                for n in range(6):
                    ps, pk = psm[n]
                    S.op("dve", lambda e, n=n, ps=ps: e.tensor_tensor(
                        out=mod()[:, n * 512:(n + 1) * 512], in0=ps[0:2, :],
                        in1=bada2()[:, n * 512:(n + 1) * 512], op=ALU.add), pk + [bada2.k()], [mod.k()])
                    R.free(pk)
                S.op("dve", lambda e: e.scalar_tensor_tensor(
                    out=g1rows()[:], in0=mod()[:, D:2 * D], scalar=1.0, in1=ng2()[:],
                    op0=ALU.add, op1=ALU.mult), [mod.k(), ng2.k()], [g1rows.k()])
                for b in range(NSEQ):
                    for (rows, rk, c0, dst) in ((g1rows, g1rows.k(), 0, G1), (mod, mod.k(), 0, SH),
                                                (mod, mod.k(), 2 * D, GATE)):
                        for hf in range(2):
                            ps, pk = R.alloc(1)
                            S.op("pe", lambda e, ps=ps, rows=rows, c0=c0, hf=hf, b=b: e.matmul(
                                ps, lhsT=sel()[0:2, b * 128:(b + 1) * 128],
                                rhs=rows()[0:2, c0 + hf * 512: c0 + (hf + 1) * 512], start=True, stop=True),
                                [sel.k(), rk], pk)
                            act(dst(b)[:, hf * 512:(hf + 1) * 512], ps, AF.Copy, pk, [dst.k(b)])
                            R.free(pk)
                S.flush()
            ss = B(ph, "ss1", [128, 1], F32, 3)
            xn = B(ph, "xn1", [128, D], F32, 3)
            hb = B(ph, "hb1", [128, D], BF16, 3)
            hT = B(ph, "hT1", [128, 8, 128], BF16, 3)
            RAt = B(ph, "RAt1", [128, 128], F32, 3)
            RA3 = B(ph, "RA31", [128, 128], BF16, 3)
            lahl = B(ph, "lahl1", [128, D], BF16, 3)
            la = B(ph, "la1", [128, 512], F32, 3)
            En = B(ph, "En1", [128, 512], F32, 3)
            ktb = B(ph, "ktb1", [128, 512], BF16, 3)
            vb = B(ph, "vb1", [128, D], BF16, 3)
            dec = [B(ph, "dec1_%d" % b, [128, 8], F32, 2) for b in range(NSEQ)]
            T = B(ph, "T1", [128, D], F32, NSEQ)
            Sb = B(ph, "Sb1", [128, D], BF16, 3)
            for b in range(NSEQ):
                S.op("pool", lambda e, b=b: e.memset(T(b)[:], 0.0), [], [("T1h", b, 0), ("T1h", b, 1)])
                for s in range(2):
                    S.op("pool", lambda e, b=b, s=s: e.memset(dec[b](s)[:], 0.0), [], [dec[b].k(s)])

            cnt = {"x": NSEQ, "t": 0, "s": 0}

            def load_x(b, i, xbuf):
                s = cnt["x"]
                cnt["x"] += 1
                r0 = b * SEQ + i * 128
                S.dma("sp", xbuf(s)[:], x[r0:r0 + 128, :], "%s%d" % (xbuf.name, s % xbuf.n), writes=[xbuf.k(s)])
                return s

            def sweep1_tile(b, i, xs):
                s = cnt["t"]
                cnt["t"] += 1
                act(junk()[:], xt(xs)[:], AF.Square, [xt.k(xs)], [ss.k(s), junk.k()], accum_out=ss(s)[:, 0:1])
                rstd_from_ss(ss(s)[:, 0:1], D, ss.k(s))
                S.op("dve", lambda e: e.scalar_tensor_tensor(
                    out=xn(s)[:], in0=xt(xs)[:], scalar=ss(s)[:, 0:1], in1=G1(b)[:],
                    op0=ALU.mult, op1=ALU.mult), [xt.k(xs), ss.k(s), G1.k(b)], [xn.k(s)])
                S.op("dve", lambda e: e.tensor_tensor(out=hb(s)[:], in0=xn(s)[:], in1=SH(b)[:], op=ALU.add),
                     [xn.k(s), SH.k(b)], [hb.k(s)])
                yield
                transpose8(hb(s), hb.k(s), hT(s), hT.k(s))
                S.dma("sp", hts[b * NT + i], hT(s).rearrange("p a b -> p (a b)"), "hT1_%d" % (s % 3),
                      reads=[hT.k(s)])
                yield
                lahi_ap, lalo_ap = lahl(s)[:, 0:512], lahl(s)[:, 512:1024]
                decay_logs(hT(s), hT.k(s), "b", RAt(s), RAt.k(s), RA3(s), RA3.k(s), la(s), la.k(s),
                           lahi_ap, lahl.k(s), lalo_ap, lahl.k(s))
                S.dma("sp", labs[b * NT + i], lahl(s)[:], "lahl1_%d" % (s % 3), reads=[lahl.k(s)])
                yield
                ps_b, pk_b = R.alloc(1)

                def fbt(e):
                    e.matmul(ps_b, lhsT=cumb()[:], rhs=lahi_ap, start=True, stop=False)
                    return e.matmul(ps_b, lhsT=cumb()[:], rhs=lalo_ap, start=False, stop=True)
                S.op("pe", fbt, [cumb.k(), lahl.k(s)], pk_b)
                ps_d, pk_d = R.alloc(1)

                def fd(e):
                    for h in range(4):
                        e.matmul(ps_d[:, 2 * h:2 * h + 2], lhsT=lahi_ap[:, h * 128:(h + 1) * 128],
                                 rhs=ind()[:], start=True, stop=False)
                        i_ = e.matmul(ps_d[:, 2 * h:2 * h + 2], lhsT=lalo_ap[:, h * 128:(h + 1) * 128],
                                      rhs=ind()[:], start=False, stop=True)
                    return i_
                S.op("pe", fd, [lahl.k(s), ind.k()], pk_d)
                act(En(s)[:], ps_b, AF.Exp, pk_b, [En.k(s)], scale=-1.0)
                act(dec[b](i)[:], ps_d[:, 0:8], AF.Exp, pk_d, [dec[b].k(i)])
                R.free(pk_b)
                R.free(pk_d)
                yield
                ps_k, pk_k = proj_tok(hT(s), hT.k(s), Wqk, 512, 512)
                S.op("dve", lambda e: e.tensor_tensor(out=ktb(s)[:], in0=ps_k, in1=En(s)[:], op=ALU.mult),
                     pk_k + [En.k(s)], [ktb.k(s)])
                R.free(pk_k)
                for n_ in range(2):
                    ps_v, pk_v = R.alloc(1)

                    def fv(e, n_=n_, ps_v=ps_v):
                        for kc in range(8):
                            i_ = e.matmul(ps_v, lhsT=hT(s)[:, kc, :], rhs=Wv()[:, kc, n_ * 512:(n_ + 1) * 512],
                                          start=(kc == 0), stop=(kc == 7))
                        return i_
                    S.op("pe", fv, [hT.k(s), Wv.k()], pk_v)
                    act(vb(s)[:, n_ * 512:(n_ + 1) * 512], ps_v, AF.Copy, pk_v, [("vb1h", s % vb.n, n_)])
                    R.free(pk_v)
                vbk = [("vb1h", s % vb.n, 0), ("vb1h", s % vb.n, 1)]
                S.dma("sp", vsc[b * NT + i], vb(s)[:], "vb1_%d" % (s % 3), reads=vbk)
                if i == NT - 1:
                    dump("la_%d" % b, la, s, [128, 512], F32)
                    dump("En_%d" % b, En, s, [128, 512], F32)
                    dump("ktb_%d" % b, ktb, s, [128, 512], BF16)
                    dump("dec_%d" % b, dec[b], i, [128, 8], F32)
                yield
                for c in (1, 0):
                    n = 2 * i + c
                    dprev, dpk = (dec[b](i + 1), dec[b].k(i + 1)) if c == 1 else (dec[b](i), dec[b].k(i))
                    pcol = 0 if c == 1 else 1
                    for hp in range(2):
                        ps_kv, pk_kv = R.alloc(1)

                        def fkv(e, c=c, hp=hp, ps_kv=ps_kv):
                            for hh in range(2):
                                h = 2 * hp + hh
                                i_ = e.matmul(ps_kv[:, hh * 256:(hh + 1) * 256],
                                              lhsT=ktb(s)[64 * c:64 * c + 64, h * 128:(h + 1) * 128],
                                              rhs=vb(s)[64 * c:64 * c + 64, h * 256:(h + 1) * 256],
                                              start=True, stop=True)
                            return i_
                        S.op("pe", fkv, [ktb.k(s), vbk[hp]], pk_kv)

                        def fT(e, hp=hp, ps_kv=ps_kv, dprev=dprev, pcol=pcol):
                            for hh in range(2):
                                h = 2 * hp + hh
                                i_ = e.scalar_tensor_tensor(
                                    out=T(b)[:, h * 256:(h + 1) * 256], in0=T(b)[:, h * 256:(h + 1) * 256],
                                    scalar=dprev[:, 2 * h + pcol:2 * h + pcol + 1],
                                    in1=ps_kv[:, hh * 256:(hh + 1) * 256], op0=ALU.mult, op1=ALU.add)
                            return i_
                        S.op("dve", fT, [("T1h", b, hp), dpk] + pk_kv, [("T1h", b, hp)])
                        R.free(pk_kv)
                    if n >= 1:
                        q = cnt["s"]
                        cnt["s"] += 1

                        def fS(e, c=c, q=q):
                            for h in range(4):
                                i_ = e.activation(out=Sb(q)[:, h * 256:(h + 1) * 256],
                                                  in_=T(b)[:, h * 256:(h + 1) * 256], func=AF.Identity,
                                                  scale=dec[b](i)[:, 2 * h + c:2 * h + c + 1])
                            return i_
                        S.op("act", fS, [("T1h", b, 0), ("T1h", b, 1), dec[b].k(i)], [Sb.k(q)])
                        S.dma("sp", sbw[b * NCH + n - 1], Sb(q)[:], "Sb1_%d" % (q % 3), reads=[Sb.k(q)])
                    yield

            xq = {NT - 1: list(range(NSEQ)), NT - 2: [load_x(b, NT - 2, xt) for b in range(NSEQ)]}
            for i in range(NT - 1, -1, -1):
                xs_cur = xq.pop(i)
                if i - 2 >= 0:
                    xq[i - 2] = [load_x(b, i - 2, xt) for b in range(NSEQ)]
                interleave([sweep1_tile(b, i, xs_cur[b]) for b in range(NSEQ)])
                if i in (NT - 4, NT - 8, NT - 12):
                    issue_sweep2_loads(xt.k(xs_cur[0]), {NT - 4: 0, NT - 8: 1, NT - 12: 2}[i])
            S.flush()

        with ExitStack() as ph:
            hT = B(ph, "hT2", [128, 8, 128], BF16, 4)
            SbL = B(ph, "SbL", [128, 2, D], BF16, 4)
            RAt = B(ph, "RAt2", [128, 128], F32, 2)
            RA3 = B(ph, "RA32", [128, 128], BF16, 2)
            lafh = B(ph, "lafh", [128, 512], BF16, 2)
            lafl = B(ph, "lafl", [128, 512], BF16, 2)
            labhl = B(ph, "labhl", [128, D], BF16, 4)
            laf = B(ph, "laf2", [128, 512], F32, 2)
            Ef = B(ph, "Ef", [128, 512], F32, 2)
            Enf = B(ph, "Enf", [128, 512], F32, 2)
            Eb = B(ph, "Eb", [128, 512], F32, 2)
            Enb = B(ph, "Enb", [128, 512], F32, 2)
            qtf = B(ph, "qtf", [128, 512], BF16, 2)
            ktf = B(ph, "ktf", [128, 512], BF16, 2)
            qtb = B(ph, "qtb", [128, 512], BF16, 2)
            ktbb = B(ph, "ktbb", [128, 512], BF16, 2)
            Af = B(ph, "Af", [128, 512], BF16, 2)
            Ab = B(ph, "Ab", [128, 512], BF16, 2)
            ktok = B(ph, "ktok", [128, 512], BF16, 2)
            vb = B(ph, "vb2", [128, D], BF16, 4)
            decf = [B(ph, "decf_%d" % b, [128, 8], F32, 2) for b in range(NSEQ)]
            T = B(ph, "T2", [128, D], F32, NSEQ)
            Sf = [B(ph, "Sf_%d" % b, [128, D], BF16, 3) for b in range(NSEQ)]
            sso = B(ph, "sso", [128, 4], F32, 2)
            sz = B(ph, "sz2", [128, D], F32, 2)
            og = B(ph, "og", [128, D], BF16, 2)
            ogT = B(ph, "ogT", [128, 8, 128], BF16, 2)
            ysb = B(ph, "ysb", [128, D], F32, 2)
            for b in range(NSEQ):
                S.op("pool", lambda e, b=b: e.memset(T(b)[:], 0.0), [], [("T2h", b, 0), ("T2h", b, 1)])
                for s in range(2):
                    S.op("pool", lambda e, b=b, s=s: e.memset(decf[b](s)[:], 0.0), [], [decf[b].k(s)])
            LNQ = math.log(128.0 ** -0.5)
            cnt = {"l": 0, "t": 0}
            sfc = [0, 0]

            def load_tile2(b, i):
                s = cnt["l"]
                cnt["l"] += 1
                S.dma("sp", hT(s).rearrange("p a b -> p (a b)"), hts[b * NT + i], "hT2_%d" % (s % 4),
                      writes=[hT.k(s)])
                if 2 * i + 1 < NCH - 1:
                    S.dma("sp", SbL(s)[:], sbw[b * NCH + 2 * i: b * NCH + 2 * i + 2].rearrange("c p n -> p c n"),
                          "SbL_%d" % (s % 4), writes=[SbL.k(s)])
                else:
                    S.dma("sp", SbL(s)[:, 0, :], sbw[b * NCH + 2 * i], "SbL_%d" % (s % 4), writes=[SbL.k(s)])
                S.dma("sp", vb(s)[:], vsc[b * NT + i], "vb2_%d" % (s % 4), writes=[vb.k(s)])
                S.dma("sp", labhl(s)[:], labs[b * NT + i], "labhl_%d" % (s % 4), writes=[labhl.k(s)])
                return s

            def sweep2_tile(b, i, ls):
                s = cnt["t"]
                cnt["t"] += 1
                hTt, hTk = hT(ls), hT.k(ls)
                decay_logs(hTt, hTk, "f", RAt(s), RAt.k(s), RA3(s), RA3.k(s), laf(s), laf.k(s),
                           lafh(s), lafh.k(s), lafl(s), lafl.k(s))
                yield
                for (lh, ll, lks, cm, Epos, Eneg, dcy) in (
                        (lafh(s)[:], lafl(s)[:], [lafh.k(s), lafl.k(s)], cumf, Ef, Enf, True),
                        (labhl(ls)[:, 0:512], labhl(ls)[:, 512:1024], [labhl.k(ls)], cumb, Eb, Enb, False)):
                    ps_b, pk_b = R.alloc(1)

                    def fb(e, ps_b=ps_b, lh=lh, ll=ll, cm=cm):
                        for h in range(4):
                            e.matmul(ps_b[:, h * 128:(h + 1) * 128], lhsT=lh[:, h * 128:(h + 1) * 128],
                                     rhs=cm()[:], start=True, stop=False)
                            i_ = e.matmul(ps_b[:, h * 128:(h + 1) * 128], lhsT=ll[:, h * 128:(h + 1) * 128],
                                          rhs=cm()[:], start=False, stop=True)
                        return i_
                    S.op("pe", fb, lks + [cm.k()], pk_b)
                    act(Epos(s)[:], ps_b, AF.Exp, pk_b, [Epos.k(s)], bias=LNQ)
                    act(Eneg(s)[:], ps_b, AF.Exp, pk_b, [Eneg.k(s)], scale=-1.0)
                    if dcy:
                        act(decf[b](i)[:], ps_b[:, 63::64], AF.Exp, pk_b, [decf[b].k(i)])
                    R.free(pk_b)
                yield
                ps_q, pk_q = proj_feat(hTt, hTk, Wqk, 0)
                for (dst, E_) in ((qtf, Ef), (qtb, Eb)):
                    S.op("dve", lambda e, dst=dst, E_=E_: e.tensor_tensor(
                        out=dst(s)[:], in0=ps_q, in1=E_(s)[:], op=ALU.mult), pk_q + [E_.k(s)], [dst.k(s)])
                R.free(pk_q)
                ps_k, pk_k = proj_feat(hTt, hTk, Wqk, 512)
                for (dst, E_) in ((ktf, Enf), (ktbb, Enb)):
                    S.op("dve", lambda e, dst=dst, E_=E_: e.tensor_tensor(
                        out=dst(s)[:], in0=ps_k, in1=E_(s)[:], op=ALU.mult), pk_k + [E_.k(s)], [dst.k(s)])
                R.free(pk_k)
                yield
                for (kt_, qt_, msk, A_) in ((ktf, qtf, maskf, Af), (ktbb, qtb, maskb, Ab)):
                    ps_a, pk_a = R.alloc(1)

                    def fa(e, ps_a=ps_a, kt_=kt_, qt_=qt_):
                        for h in range(4):
                            i_ = e.matmul(ps_a[:, h * 128:(h + 1) * 128], lhsT=kt_(s)[:, h * 128:(h + 1) * 128],
                                          rhs=qt_(s)[:, h * 128:(h + 1) * 128], start=True, stop=True)
                        return i_
                    S.op("pe", fa, [kt_.k(s), qt_.k(s)], pk_a)
                    S.op("dve", lambda e, ps_a=ps_a, msk=msk, A_=A_: e.tensor_tensor(
                        out=A_(s)[:], in0=ps_a, in1=msk().rearrange("p a b -> p (a b)"), op=ALU.mult),
                        pk_a + [msk.k()], [A_.k(s)])
                    R.free(pk_a)
                ps_t, pk_t = R.alloc(1)
                pst = ps_t.bitcast(BF16)

                def ft(e):
                    for h in range(4):
                        i_ = e.transpose(out=pst[:, h * 128:(h + 1) * 128], in_=ktf(s)[:, h * 128:(h + 1) * 128],
                                         identity=ident()[:])
                    return i_
                S.op("pe", ft, [ktf.k(s), ident.k()], pk_t)
                act(ktok(s)[:], pst[:, 0:512], AF.Copy, pk_t, [ktok.k(s)])
                R.free(pk_t)
                yield
                n0 = 2 * i
                q0 = sfc[b]
                pso = []
                for hp in range(2):
                    ps_o, pk_o = R.alloc(1)
                    pso.append((ps_o, pk_o))

                    def fo1(e, hp=hp, ps_o=ps_o):
                        for hh in range(2):
                            h = 2 * hp + hh
                            oc = slice(hh * 256, (hh + 1) * 256)
                            vc = slice(h * 256, (h + 1) * 256)
                            hc = slice(h * 128, (h + 1) * 128)
                            e.matmul(ps_o[:, oc], lhsT=Af(s)[:, hc], rhs=vb(ls)[:, vc], start=(hh == 0), stop=False,
                                     skip_group_check=True)
                            i_ = e.matmul(ps_o[:, oc], lhsT=Ab(s)[:, hc], rhs=vb(ls)[:, vc], start=False, stop=False,
                                          skip_group_check=True)
                            for c in (0, 1):
                                if n0 + c < NCH - 1:
                                    i_ = e.matmul(ps_o[64 * c:64 * c + 64, oc],
                                                  lhsT=qtb(s)[:, h * 128 + 64 * c: h * 128 + 64 * c + 64],
                                                  rhs=SbL(ls)[:, c, vc], start=False, stop=False,
                                                  skip_group_check=True)
                            if n0 >= 1:
                                i_ = e.matmul(ps_o[0:64, oc], lhsT=qtf(s)[:, h * 128: h * 128 + 64],
                                              rhs=Sf[b](q0)[:, vc], start=False, stop=False, skip_group_check=True)
                        return i_
                    rd = [Af.k(s), Ab.k(s), vb.k(ls), qtb.k(s), qtf.k(s), SbL.k(ls)]
                    if n0 >= 1:
                        rd.append(Sf[b].k(q0))
                    S.op("pe", fo1, rd, pk_o)
                for c in (0, 1):
                    dprev, dpk = (decf[b](i - 1), decf[b].k(i - 1)) if c == 0 else (decf[b](i), decf[b].k(i))
                    pcol = 1 if c == 0 else 0
                    for hp in range(2):
                        ps_kv, pk_kv = R.alloc(1)

                        def fkv(e, c=c, hp=hp, ps_kv=ps_kv):
                            for hh in range(2):
                                h = 2 * hp + hh
                                i_ = e.matmul(ps_kv[:, hh * 256:(hh + 1) * 256],
                                              lhsT=ktok(s)[64 * c:64 * c + 64, h * 128:(h + 1) * 128],
                                              rhs=vb(ls)[64 * c:64 * c + 64, h * 256:(h + 1) * 256],
                                              start=True, stop=True)
                            return i_
                        S.op("pe", fkv, [ktok.k(s), vb.k(ls)], pk_kv)

                        def fT(e, hp=hp, ps_kv=ps_kv, dprev=dprev, pcol=pcol):
                            for hh in range(2):
                                h = 2 * hp + hh
                                i_ = e.scalar_tensor_tensor(
                                    out=T(b)[:, h * 256:(h + 1) * 256], in0=T(b)[:, h * 256:(h + 1) * 256],
                                    scalar=dprev[:, 2 * h + pcol:2 * h + pcol + 1],
                                    in1=ps_kv[:, hh * 256:(hh + 1) * 256], op0=ALU.mult, op1=ALU.add)
                            return i_
                        S.op("dve", fT, [("T2h", b, hp), dpk] + pk_kv, [("T2h", b, hp)])
                        R.free(pk_kv)
                    if n0 + c < NCH - 1:
                        sfc[b] += 1
                        q = sfc[b]

                        def fS(e, c=c, q=q):
                            for h in range(4):
                                i_ = e.activation(out=Sf[b](q)[:, h * 256:(h + 1) * 256],
                                                  in_=T(b)[:, h * 256:(h + 1) * 256], func=AF.Identity,
                                                  scale=decf[b](i)[:, 2 * h + c:2 * h + c + 1])
                            return i_
                        S.op("act", fS, [("T2h", b, 0), ("T2h", b, 1), decf[b].k(i)], [Sf[b].k(q)])
                    if c == 0:
                        q1 = sfc[b]
                        for hp in range(2):
                            ps_o, pk_o = pso[hp]

                            def fo2(e, q1=q1, hp=hp, ps_o=ps_o):
                                for hh in range(2):
                                    h = 2 * hp + hh
                                    i_ = e.matmul(ps_o[64:128, hh * 256:(hh + 1) * 256],
                                                  lhsT=qtf(s)[:, h * 128 + 64: h * 128 + 128],
                                                  rhs=Sf[b](q1)[:, h * 256:(h + 1) * 256], start=False, stop=True,
                                                  skip_group_check=True)
                                return i_
                            S.op("pe", fo2, [qtf.k(s), Sf[b].k(q1)] + pk_o, pk_o)
                yield
                for n_ in range(2):
                    ps_z, pk_z = R.alloc(1)

                    def fz(e, n_=n_, ps_z=ps_z):
                        for kc in range(8):
                            i_ = e.matmul(ps_z, lhsT=hTt[:, kc, :], rhs=Wz()[:, kc, n_ * 512:(n_ + 1) * 512],
                                          start=(kc == 0), stop=(kc == 7))
                        return i_
                    S.op("pe", fz, [hTk, Wz.k()], pk_z)
                    act(sz(s)[:, n_ * 512:(n_ + 1) * 512], ps_z, AF.Silu, pk_z, [sz.k(s)])
                    R.free(pk_z)
                S.op("dve", lambda e: e.tensor_tensor(out=sz(s)[:], in0=sz(s)[:], in1=gng()[:], op=ALU.mult),
                     [sz.k(s), gng.k()], [sz.k(s)])
                for hp in range(2):
                    ps_o, pk_o = pso[hp]

                    def fsq(e, hp=hp, ps_o=ps_o):
                        for hh in range(2):
                            h = 2 * hp + hh
                            i_ = e.activation(out=junk()[:, h * 256:(h + 1) * 256], in_=ps_o[:, hh * 256:(hh + 1) * 256],
                                              func=AF.Square, accum_out=sso(s)[:, h:h + 1])
                        return i_
                    S.op("act", fsq, pk_o, [sso.k(s), ("junkh", hp)])
                rstd_from_ss(sso(s)[:], 256, sso.k(s))
                for hp in range(2):
                    ps_o, pk_o = pso[hp]

                    def fog(e, hp=hp, ps_o=ps_o):
                        for hh in range(2):
                            h = 2 * hp + hh
                            i_ = e.scalar_tensor_tensor(out=og(s)[:, h * 256:(h + 1) * 256],
                                                        in0=ps_o[:, hh * 256:(hh + 1) * 256],
                                                        scalar=sso(s)[:, h:h + 1],
                                                        in1=sz(s)[:, h * 256:(h + 1) * 256],
                                                        op0=ALU.mult, op1=ALU.mult)
                        return i_
                    S.op("dve", fog, pk_o + [sso.k(s), sz.k(s)], [og.k(s)])
                    R.free(pk_o)
                yield
                transpose8(og(s), og.k(s), ogT(s), ogT.k(s))
                for n_ in range(2):
                    ps_y, pk_y = R.alloc(1)

                    def fy(e, n_=n_, ps_y=ps_y):
                        for kc in range(8):
                            i_ = e.matmul(ps_y, lhsT=ogT(s)[:, kc, :], rhs=Wbr()[:, kc, n_ * 512:(n_ + 1) * 512],
                                          start=(kc == 0), stop=(kc == 7))
                        return i_
                    S.op("pe", fy, [ogT.k(s), Wbr.k()], pk_y)
                    S.op("dve", lambda e, n_=n_, ps_y=ps_y: e.tensor_copy(out=ysb(s)[:, n_ * 512:(n_ + 1) * 512],
                                                                        in_=ps_y), pk_y, [ysb.k(s)])
                    R.free(pk_y)
                r0 = b * SEQ + i * 128
                S.dma("sp", ygla[r0:r0 + 128, :], ysb(s)[:], "ysb_%d" % (s % 2), reads=[ysb.k(s)])
                yield

            w3src = (w_in_v[:, :, C_U:C_U + 1024], w_in_v[:, :, C_VS:C_VS + 1024], w_in_v[:, :, C_ZM:C_ZM + 1024],
                     w_brm.rearrange("(kc p) n -> p kc n", p=128), w_in_v[:, :, C_MGMLP:C_MGMLP + 1024],
                     w_in_v[:, :, C_MGLA:C_MGLA + 1024], w_out.rearrange("(kc p) n -> p kc n", p=128))
            ls_next = [load_tile2(b, 0) for b in range(NSEQ)]
            for i in range(NT):
                ls_cur = ls_next
                if i + 1 < NT:
                    ls_next = [load_tile2(b, i + 1) for b in range(NSEQ)]
                interleave([sweep2_tile(b, i, ls_cur[b]) for b in range(NSEQ)])
                if 1 <= i <= 7:
                    S.dma("pool", wst[i - 1].rearrange("p (a b) -> p a b", a=8), w3src[i - 1], "wst%d" % (i - 1),
                          reads=[hT.k(ls_cur[0])])
            S.flush()

        ph12.close()

        with ExitStack() as ph:
            lng = B(ph, "lng", [128, D], F32)
            lnb = B(ph, "lnb", [128, D], F32)
            fg = B(ph, "fg", [128, D], F32)
            wsT = B(ph, "wsT", [128, 8, 128], BF16)
            S.dma("pool", wsT()[:], wsT_d[:, :, :], "wsT", writes=[wsT.k()])
            bsT = B(ph, "bsT", [128, 8], F32)
            S.dma("sp", bsT()[:], bsT_d[:, :], "bsT", writes=[bsT.k()])
            W3 = {}
            prev = None
            for j, nm in ((0, "Wu"), (1, "Wvs"), (2, "Wzm"), (4, "Wmm"), (5, "Wml"), (3, "Wbm"), (6, "Wo")):
                w = B(ph, nm, [128, 8, 1024], BF16)
                W3[nm] = w
            W3 = [W3[nm] for nm in ("Wu", "Wvs", "Wzm", "Wbm", "Wmm", "Wml", "Wo")]
            Wu, Wvs, Wzm, Wbm, Wmm, Wml, Wo = W3

            hT = B(ph, "hT3", [128, 8, 128], BF16, 2)
            xt = B(ph, "xt3", [128, D], F32, 2)
            ygl = B(ph, "ygl", [128, D], F32, 2)
            gu = B(ph, "gu", [128, D], F32)
            gv = B(ph, "gv", [128, D], F32)
            st = B(ph, "st3", [128, 16], F32, 2)
            vsn = B(ph, "vsn", [128, D], BF16)
            sz = B(ph, "sz3", [128, D], F32)
            prod = B(ph, "prod", [128, D], BF16)
            prodT = B(ph, "prodT", [128, 8, 128], BF16)
            sgm = B(ph, "sgm", [128, D], F32)
            sgl = B(ph, "sgl", [128, D], F32)
            mrgb = B(ph, "mrgb", [128, D], BF16)
            mrgT = B(ph, "mrgT", [128, 8, 128], BF16)
            ot = B(ph, "ot", [128, D], F32, 2)
            cnt = {"l": 0, "t": 0}

            def load_tile3(b, i, rest=True):
                s = cnt["l"]
                cnt["l"] += 1
                r0 = b * SEQ + i * 128
                S.dma("sp", hT(s).rearrange("p a b -> p (a b)"), hts[b * NT + i], "hT3_%d" % (s % 2),
                      writes=[hT.k(s)])
                if rest:
                    load_rest3(b, i, s, [])
                return s

            def load_rest3(b, i, s, gate):
                r0 = b * SEQ + i * 128
                S.dma("sp", xt(s)[:], x[r0:r0 + 128, :], "xt3_%d" % (s % 2), reads=gate, writes=[xt.k(s)])
                S.dma("sp", ygl(s)[:], ygla[r0:r0 + 128, :], "ygl_%d" % (s % 2), reads=gate, writes=[ygl.k(s)])

            def proj_half(hTt, hTk, W, n):
                ps, pk = R.alloc(1)

                def f(e):
                    for kc in range(8):
                        i_ = e.matmul(ps, lhsT=hTt[:, kc, :], rhs=W()[:, kc, n * 512:(n + 1) * 512],
                                      start=(kc == 0), stop=(kc == 7))
                    return i_
                S.op("pe", f, [hTk, W.k()], pk)
                return ps, pk

            def mm_half(xT, xTk, W, n):
                ps, pk = R.alloc(1)

                def f(e):
                    for kc in range(8):
                        i_ = e.matmul(ps, lhsT=xT()[:, kc, :], rhs=W()[:, kc, n * 512:(n + 1) * 512],
                                      start=(kc == 0), stop=(kc == 7))
                    return i_
                S.op("pe", f, [xTk, W.k()], pk)
                return ps, pk

            def sweep3_tile(b, i, ls):
                s = cnt["t"]
                cnt["t"] += 1
                hTt, hTk = hT(ls), hT.k(ls)
                H = (slice(0, 512), slice(512, 1024))
                for n in range(2):
                    ps, pk = proj_half(hTt, hTk, Wu, n)
                    act(gu()[:, H[n]], ps, AF.Gelu, pk, [gu.k()])
                    R.free(pk)
                for n in range(2):
                    ps, pk = proj_half(hTt, hTk, Wvs, n)
                    act(gv()[:, H[n]], ps, AF.Gelu, pk, [gv.k()])
                    R.free(pk)
                S.op("dve", lambda e: e.bn_stats(out=st(s)[:, 0:6], in_=gv()[:, 0:512]), [gv.k()], [st.k(s)])
                S.op("dve", lambda e: e.bn_stats(out=st(s)[:, 6:12], in_=gv()[:, 512:1024]), [gv.k()], [st.k(s)])
                S.op("dve", lambda e: e.bn_aggr(out=st(s)[:, 12:14], in_=st(s)[:, 0:12]), [st.k(s)], [st.k(s)])
                act(st(s)[:, 13:14], st(s)[:, 13:14], AF.Ln, [st.k(s)], [st.k(s)], bias=EPS)
                act(st(s)[:, 13:14], st(s)[:, 13:14], AF.Exp, [st.k(s)], [st.k(s)], scale=-0.5)
                S.op("dve", lambda e: e.scalar_tensor_tensor(out=gv()[:], in0=gv()[:], scalar=st(s)[:, 12:13],
                                                             in1=lng()[:], op0=ALU.subtract, op1=ALU.mult),
                     [gv.k(), st.k(s), lng.k()], [gv.k()])
                S.op("dve", lambda e: e.scalar_tensor_tensor(out=vsn()[:], in0=gv()[:], scalar=st(s)[:, 13:14],
                                                             in1=lnb()[:], op0=ALU.mult, op1=ALU.add),
                     [gv.k(), st.k(s), lnb.k()], [vsn.k()])
                for n in range(2):
                    ps, pk = proj_half(hTt, hTk, Wzm, n)
                    act(sz()[:, H[n]], ps, AF.Silu, pk, [sz.k()])
                    R.free(pk)
                S.op("dve", lambda e: e.tensor_tensor(out=gu()[:], in0=gu()[:], in1=sz()[:], op=ALU.mult),
                     [gu.k(), sz.k()], [gu.k()])
                for n in range(2):
                    ps, pk = proj_half(hTt, hTk, Wmm, n)
                    act(sgm()[:, H[n]], ps, AF.Sigmoid, pk, [sgm.k()])
                    R.free(pk)
                for n in range(2):
                    ps, pk = proj_half(hTt, hTk, Wml, n)
                    act(sgl()[:, H[n]], ps, AF.Sigmoid, pk, [sgl.k()])
                    R.free(pk)
                S.op("dve", lambda e: e.tensor_tensor(out=sgl()[:], in0=sgl()[:], in1=ygl(ls)[:], op=ALU.mult),
                     [sgl.k(), ygl.k(ls)], [sgl.k()])
                for n in range(2):
                    ps_g, pk_g = R.alloc(1)

                    def fg_(e, n=n, ps_g=ps_g):
                        for g4 in range(4):
                            g = 4 * n + g4
                            i_ = e.matmul(ps_g[:, g4 * 128:(g4 + 1) * 128], lhsT=wsT()[:, g, :],
                                          rhs=vsn()[:, g * 128:(g + 1) * 128], start=True, stop=True)
                        return i_
                    S.op("pe", fg_, [wsT.k(), vsn.k()], pk_g)

                    def fp(e, n=n, ps_g=ps_g):
                        for g4 in range(4):
                            g = 4 * n + g4
                            i_ = e.scalar_tensor_tensor(out=prod()[:, g * 128:(g + 1) * 128],
                                                        in0=ps_g[:, g4 * 128:(g4 + 1) * 128], scalar=bsT()[:, g:g + 1],
                                                        in1=gu()[:, g * 128:(g + 1) * 128], op0=ALU.add, op1=ALU.mult)
                        return i_
                    S.op("dve", fp, pk_g + [bsT.k(), gu.k()], [prod.k()])
                    R.free(pk_g)
                transpose8(prod(), prod.k(), prodT(), prodT.k())
                for n in range(2):
                    ps_y, pk_y = mm_half(prodT, prodT.k(), Wbm, n)
                    S.op("dve", lambda e, n=n, ps_y=ps_y: e.tensor_tensor(
                        out=sgm()[:, H[n]], in0=ps_y, in1=sgm()[:, H[n]], op=ALU.mult), pk_y + [sgm.k()], [sgm.k()])
                    R.free(pk_y)
                S.op("dve", lambda e: e.tensor_tensor(out=mrgb()[:], in0=sgm()[:], in1=sgl()[:], op=ALU.add),
                     [sgm.k(), sgl.k()], [mrgb.k()])
                transpose8(mrgb(), mrgb.k(), mrgT(), mrgT.k())
                for n in range(2):
                    ps_o, pk_o = mm_half(mrgT, mrgT.k(), Wo, n)
                    S.op("dve", lambda e, n=n, ps_o=ps_o: e.tensor_tensor(
                        out=sgm()[:, H[n]], in0=ps_o, in1=GATE(b)[:, H[n]], op=ALU.mult),
                        pk_o + [GATE.k(b)], [sgm.k()])
                    R.free(pk_o)
                S.op("dve", lambda e: e.tensor_tensor(out=sgm()[:], in0=sgm()[:], in1=xt(ls)[:], op=ALU.add),
                     [sgm.k(), xt.k(ls)], [sgm.k()])
                act(junk()[:], sgm()[:], AF.Square, [sgm.k()], [st.k(s), junk.k()], accum_out=st(s)[:, 14:15])
                rstd_from_ss(st(s)[:, 14:15], D, st.k(s))
                S.op("dve", lambda e: e.scalar_tensor_tensor(out=ot(s)[:], in0=sgm()[:], scalar=st(s)[:, 14:15],
                                                             in1=fg()[:], op0=ALU.mult, op1=ALU.mult),
                     [sgm.k(), st.k(s), fg.k()], [ot.k(s)])
                r0 = b * SEQ + i * 128
                S.dma("sp", out[r0:r0 + 128, :], ot(s)[:], "ot_%d" % (s % 2), reads=[ot.k(s)])

            order = [(b, i) for b in range(NSEQ) for i in range(NT)]
            ls_next = load_tile3(*order[0], rest=False)
            for q_, (j, nm) in enumerate(((0, "Wu"), (1, "Wvs"), (2, "Wzm"), (4, "Wmm"), (5, "Wml"), (3, "Wbm"), (6, "Wo"))):
                w = dict(Wu=Wu, Wvs=Wvs, Wzm=Wzm, Wbm=Wbm, Wmm=Wmm, Wml=Wml, Wo=Wo)[nm]
                if q_ % 2 == 0:
                    S.dma("act", w()[:].rearrange("p a b -> p (a b)"), wst[j], nm, writes=[w.k()])
                else:
                    S.dma("sp", w()[:].rearrange("p a b -> p (a b)"), wst[j], nm, reads=[hT.k(ls_next)],
                          writes=[w.k()])
            for t_, row_, gate_ in ((lng, ln_g, Wvs.k()), (lnb, ln_b, Wvs.k()), (fg, fin_g, Wbm.k())):
                S.dma("sp", t_()[:], row_[0:1, :].broadcast_to([128, D]), t_.name, reads=[gate_], writes=[t_.k()])
            load_rest3(order[0][0], order[0][1], ls_next, [Wbm.k()])
            for j, (b, i) in enumerate(order):
                ls_cur = ls_next
                if j + 1 < len(order):
                    ls_next = load_tile3(*order[j + 1])
                sweep3_tile(b, i, ls_cur)
            S.flush()
    return nc


def _consts():
    t = np.arange(128)
    same = (t[:, None] // 64) == (t[None, :] // 64)
    cumf = (same & (t[:, None] <= t[None, :])).astype(np.float32)
    cumb = (same & (t[:, None] >= t[None, :])).astype(np.float32)
    maskb = (same & (t[:, None] > t[None, :])).astype(np.float32)
    ind = np.stack([(t < 64), (t >= 64)], axis=1).astype(np.float32)
    onerows = np.zeros((128, 128), np.float32)
    onerows[16] = 1.0
    onerows[48] = 1.0
    onerows[80] = 1.0
    onerows[112] = 1.0
    sel = np.zeros((2, 256), np.float32)
    sel[0, 0:128] = 1.0
    sel[1, 128:256] = 1.0
    return dict(c_ident=np.eye(128, dtype=np.float32), c_cumf=cumf, c_cumb=cumb, c_maskb=maskb,
                c_ind=ind, c_onerows=onerows, c_sel=sel)


def make_in_maps(inp):
    f = lambda a: np.ascontiguousarray(np.asarray(a, dtype=np.float32))
    shared = dict(
        norm_g=f(inp["norm_g"]).reshape(1, D),
        w_ada=f(inp["w_ada"]).reshape(D, 3 * D),
        b_ada=f(inp["b_ada"]).reshape(1, 3 * D),
        w_in=f(inp["w_in"]).reshape(D, IN_W),
        alpha_fw_w=f(inp["alpha_fw_w"]).reshape(16, 512),
        alpha_fw_b=f(inp["alpha_fw_b"]).reshape(1, 512),
        alpha_bw_w=f(inp["alpha_bw_w"]).reshape(16, 512),
        alpha_bw_b=f(inp["alpha_bw_b"]).reshape(1, 512),
        gla_norm_g=f(inp["gla_norm_g"]).reshape(1, D),
        gmlp_ln_g=f(inp["gmlp_ln_g"]).reshape(1, D),
        gmlp_ln_b=f(inp["gmlp_ln_b"]).reshape(1, D),
        wsT=f(np.transpose(np.asarray(inp["gmlp_ws"])[0], (2, 0, 1))),
        bsT=f(np.transpose(np.asarray(inp["gmlp_bs"])[0], (1, 0))),
        w_br_gla=f(inp["w_br_gla"]).reshape(D, D),
        w_br_gmlp=f(inp["w_br_gmlp"]).reshape(D, D),
        w_out=f(inp["w_out"]).reshape(D, D),
        final_g=f(inp["final_g"]).reshape(1, D),
    )
    shared.update(_consts())
    xs = f(inp["x"])
    cs = f(inp["c"])
    maps = []
    for k in range(NCORES):
        m = dict(shared)
        m["x"] = np.ascontiguousarray(xs[NSEQ * k: NSEQ * (k + 1)].reshape(NSEQ * SEQ, D))
        cc = cs[NSEQ * k: NSEQ * (k + 1)]
        m["cT"] = np.ascontiguousarray(cc.reshape(NSEQ, 8, 128).transpose(2, 1, 0))
        maps.append(m)
    return maps


def kernel(**inputs):
    nc = build_program()
    in_maps = make_in_maps(inputs)
    res = run_bass_kernel_spmd(nc, in_maps, core_ids=list(range(NCORES)))
    outs = [np.asarray(r["out"]).reshape(NSEQ, SEQ, D) for r in res.results]
    return np.concatenate(outs, axis=0).astype(np.float32)
```
